# Optimizing a Trainium2 kernel written in Bass

```python
import jax, jax.numpy as jnp
from jax import lax
import numpy as np

D_MODEL = 2048
BATCH = 16
SEQ = 2048
DEPTH = 1

CHUNK = 64
MIX_WIDTH = D_MODEL
POOL_WIDTH = MIX_WIDTH // 2
POOL_WINDOWS = (2, 4, 8, 16)
POOL_GROUP = POOL_WIDTH // len(POOL_WINDOWS)
HGRN_WIDTH = MIX_WIDTH - POOL_WIDTH
HGRN_HEAD_DIM = 128
HGRN_HEADS = HGRN_WIDTH // HGRN_HEAD_DIM
IN_WIDTH = POOL_WIDTH + 4 * HGRN_WIDTH
D_FF = 4 * D_MODEL
EPS = 1e-6

kernel_name = "hybrid_pool_hgrn2_sandwich_block"


def rmsnorm(x, w):
    xf = x.astype(jnp.float32)
    r = xf * lax.rsqrt(jnp.mean(xf * xf, axis=-1, keepdims=True) + EPS)
    return (r * w.astype(jnp.float32)).astype(x.dtype)


def pool_mixer(u, pool_w, pool_b, pool_scale):
    B, S, _ = u.shape
    uf = u.astype(jnp.float32)
    c = jnp.cumsum(uf, axis=1)
    pos = jnp.arange(S)
    outs = []
    for j, w in enumerate(POOL_WINDOWS):
        sl = slice(j * POOL_GROUP, (j + 1) * POOL_GROUP)
        cj = c[..., sl]
        shifted = jnp.pad(cj, ((0, 0), (w, 0), (0, 0)))[:, :S]
        cnt = jnp.minimum(pos + 1, w).astype(jnp.float32)[None, :, None]
        dj = ((cj - shifted) / cnt - uf[..., sl]).astype(u.dtype)
        outs.append(dj @ pool_w[j] + pool_b[j])
    return jnp.concatenate(outs, axis=-1) * pool_scale


def hgrn2_chunk_scan(q, k, v, logf):
    B, S, H, Dk = q.shape
    Dv = v.shape[-1]
    N = S // CHUNK

    def to_chunks(a):
        return a.reshape(B, N, CHUNK, H, a.shape[-1]).transpose(1, 0, 3, 2, 4)

    causal = jnp.tril(jnp.ones((CHUNK, CHUNK), dtype=bool))[None, None, :, :, None]

    def step(state, inp):
        q_c, k_c, v_c, g_c = inp
        b = jnp.cumsum(g_c, axis=2)
        o_inter = jnp.einsum('bhtd,bhde->bhte', q_c * jnp.exp(b), state)
        diff = b[:, :, :, None, :] - b[:, :, None, :, :]
        decay = jnp.exp(jnp.where(causal, diff, -jnp.inf))
        scores = jnp.einsum('bhtd,bhsd,bhtsd->bhts', q_c, k_c, decay)
        o_intra = jnp.einsum('bhts,bhse->bhte', scores, v_c)
        b_last = b[:, :, -1:, :]
        new_state = jnp.exp(b_last[:, :, 0, :])[..., None] * state + jnp.einsum(
            'bhsd,bhse->bhde', k_c * jnp.exp(b_last - b), v_c)
        return new_state, o_inter + o_intra

    init = jnp.zeros((B, H, Dk, Dv), jnp.float32)
    _, o = lax.scan(step, init, (to_chunks(q), to_chunks(k), to_chunks(v), to_chunks(logf)))
    return o.transpose(1, 0, 3, 2, 4).reshape(B, S, H, Dv)


def hgrn2_mixer(q, f_pre, i, g, lb, norm_w):
    B, S, _ = q.shape
    shp = (B, S, HGRN_HEADS, HGRN_HEAD_DIM)
    qf = q.astype(jnp.float32).reshape(shp)
    vf = i.astype(jnp.float32).reshape(shp)
    sig = jax.nn.sigmoid(f_pre.astype(jnp.float32)).reshape(shp)
    lbh = lb.reshape(HGRN_HEADS, HGRN_HEAD_DIM)
    f = lbh + (1.0 - lbh) * sig
    k = (1.0 - lbh) * (1.0 - sig)
    o = hgrn2_chunk_scan(qf, k, vf, jnp.log(f))
    o = o * jax.nn.sigmoid(g.astype(jnp.float32)).reshape(shp)
    o = o * lax.rsqrt(jnp.mean(o * o, axis=-1, keepdims=True) + EPS)
    o = o * norm_w.astype(jnp.float32).reshape(HGRN_HEADS, HGRN_HEAD_DIM)
    return o.reshape(B, S, HGRN_WIDTH).astype(q.dtype)


def setup_inputs(seed: int = 0) -> dict:
    key = jax.random.key(seed)
    ks = jax.random.split(key, 16)
    f32 = jnp.float32
    nrm = lambda k, s, sc: jax.random.normal(k, s, f32) * sc
    gain = lambda k, s: 1.0 + 0.05 * jax.random.normal(k, s, f32)
    return {
        "x": jax.random.normal(ks[0], (BATCH, SEQ, D_MODEL), f32),
        "w_in": nrm(ks[1], (DEPTH, D_MODEL, IN_WIDTH), D_MODEL ** -0.5),
        "pool_w": nrm(ks[2], (DEPTH, len(POOL_WINDOWS), POOL_GROUP, POOL_GROUP), POOL_GROUP ** -0.5),
        "pool_b": nrm(ks[3], (DEPTH, len(POOL_WINDOWS), POOL_GROUP), 0.02),
        "pool_scale": gain(ks[4], (DEPTH, POOL_WIDTH)),
        "lb_logits": nrm(ks[5], (DEPTH + 1, HGRN_WIDTH), 0.5),
        "hgrn_norm_w": gain(ks[6], (DEPTH, HGRN_WIDTH)),
        "w_out": nrm(ks[7], (DEPTH, MIX_WIDTH, D_MODEL), MIX_WIDTH ** -0.5),
        "norm_mix_pre": gain(ks[8], (DEPTH, D_MODEL)),
        "norm_mix_post": gain(ks[9], (DEPTH, D_MODEL)),
        "norm_mlp_pre": gain(ks[10], (DEPTH, D_MODEL)),
        "norm_mlp_post": gain(ks[11], (DEPTH, D_MODEL)),
        "w_mlp_in": nrm(ks[12], (DEPTH, D_MODEL, D_FF), D_MODEL ** -0.5),
        "w_mlp_out": nrm(ks[13], (DEPTH, D_FF, D_MODEL), D_FF ** -0.5),
    }


def reference(x, w_in, pool_w, pool_b, pool_scale, lb_logits, hgrn_norm_w, w_out,
              norm_mix_pre, norm_mix_post, norm_mlp_pre, norm_mlp_post,
              w_mlp_in, w_mlp_out):
    lb_all = jnp.cumsum(jax.nn.softmax(lb_logits.astype(jnp.float32), axis=0), axis=0)
    P, H4 = POOL_WIDTH, HGRN_WIDTH
    for l in range(DEPTH):
        h = rmsnorm(x, norm_mix_pre[l])
        proj = h @ w_in[l]
        u_pool = proj[..., :P]
        q = proj[..., P:P + H4]
        f_pre = proj[..., P + H4:P + 2 * H4]
        i = proj[..., P + 2 * H4:P + 3 * H4]
        g = proj[..., P + 3 * H4:P + 4 * H4]
        y_pool = pool_mixer(u_pool, pool_w[l], pool_b[l], pool_scale[l])
        y_hgrn = hgrn2_mixer(q, f_pre, i, g, lb_all[l], hgrn_norm_w[l])
        mix = jnp.concatenate([y_pool, y_hgrn], axis=-1) @ w_out[l]
        x = x + rmsnorm(mix, norm_mix_post[l])
        h = rmsnorm(x, norm_mlp_pre[l])
        ff = jnp.square(jax.nn.relu(h @ w_mlp_in[l])) @ w_mlp_out[l]
        x = x + rmsnorm(ff, norm_mlp_post[l])
    return x
```

```python
import contextlib
import numpy as np
import ml_dtypes
import concourse.bass as bass
import concourse.mybir as mybir
from concourse.bass_utils import run_bass_kernel_spmd

F32 = mybir.dt.float32
BF16 = mybir.dt.bfloat16
U8 = mybir.dt.uint8
AF = mybir.ActivationFunctionType
ALU = mybir.AluOpType

D = 2048
DFF = 8192
INW = 5120
TT = 512
EPS = 1e-6
N_CORES = 8
ENGS = ("pe", "act", "dve", "pool", "sp")


class Prog:
    def __init__(self):
        self.streams = {e: [] for e in ENGS}
        self.cnt = {e: 0 for e in ENGS}
        self.waited = {e: {} for e in ENGS}
        self.regions = {}
        self.dmacnt = {}

    def _gather(self, deps, key, lo, hi, is_write):
        lst = self.regions.setdefault(key, [])
        for ent in lst:
            if ent[0] < hi and lo < ent[1]:
                w = ent[2]
                if w is not None:
                    if deps.get(w[0], 0) < w[1]:
                        deps[w[0]] = w[1]
                if is_write:
                    for k, v in ent[3].items():
                        if deps.get(k, 0) < v:
                            deps[k] = v

    def _record(self, key, lo, hi, tok, is_write):
        lst = self.regions.setdefault(key, [])
        if is_write:
            lst[:] = [e for e in lst if not (lo <= e[0] and e[1] <= hi)]
            lst.append([lo, hi, tok, {}])
        else:
            for ent in lst:
                if ent[0] == lo and ent[1] == hi:
                    if ent[3].get(tok[0], 0) < tok[1]:
                        ent[3][tok[0]] = tok[1]
                    return
            lst.append([lo, hi, None, {tok[0]: tok[1]}])

    def emit(self, eng, fn, reads=(), writes=(), inc=True, dma_slot=None):
        deps = {}
        for (k, lo, hi) in reads:
            self._gather(deps, k, lo, hi, k.startswith("ps"))
        for (k, lo, hi) in writes:
            self._gather(deps, k, lo, hi, True)
        if eng == "pe":
            deps.pop("pe", None)
        waits = []
        wd = self.waited[eng]
        for k, v in deps.items():
            if wd.get(k, 0) < v:
                wd[k] = v
                waits.append((k, v))
        if dma_slot is not None:
            sk = "dma:" + dma_slot
            self.dmacnt[sk] = self.dmacnt.get(sk, 0) + 16
            tok = (sk, self.dmacnt[sk])
            incinfo = (sk, 16)
        else:
            if inc:
                self.cnt[eng] += 1
                tok = (eng, self.cnt[eng])
                incinfo = (eng, 1)
            else:
                tok = (eng, self.cnt[eng] + 1)
                incinfo = None
        for (k, lo, hi) in reads:
            self._record(k, lo, hi, tok, False)
        for (k, lo, hi) in writes:
            self._record(k, lo, hi, tok, True)
        self.streams[eng].append((waits, fn, incinfo))
        return tok


def build_program(NSEQ, SEQ, STAGE=None):
    NTOK = NSEQ * SEQ
    TPS = SEQ // TT
    NT = NSEQ * TPS
    nc = bass.Bass("TRN2", target_bir_lowering=False)
    P = Prog()

    def din(name, shape, dt=F32):
        return nc.dram_tensor(name, list(shape), dt, kind="ExternalInput").ap()

    x_d = din("x", [NTOK, D])
    w_in_d = din("w_in", [D, INW])
    pool_w_d = din("pool_w", [4, 256, 256])
    pool_b_d = din("pool_b", [8, 128])
    pool_s_d = din("pool_scale", [8, 128])
    lbl_d = din("lb_logits", [16, 128])
    nw_d = din("hgrn_norm_w", [8, 128])
    w_out_d = din("w_out", [D, D])
    g_pre1_d = din("norm_mix_pre", [16, 128])
    g_post1_d = din("norm_mix_post", [1, D])
    g_pre2_d = din("norm_mlp_pre", [16, 128])
    g_post2_d = din("norm_mlp_post", [1, D])
    w_mi_d = din("w_mlp_in", [D, DFF])
    w_mo_d = din("w_mlp_out", [DFF, D])
    identb_d = din("c_identb", [128, 128], BF16)
    identf_d = din("c_identf", [128, 128])
    maskA_d = din("c_maskA", [128, 128])
    invc_d = din("c_invc", [128, 64])
    out_d = nc.dram_tensor("out", [NTOK, D], F32, kind="ExternalOutput").ap()
    dbg_d = nc.dram_tensor("dbg", [128, 8192], BF16, kind="ExternalOutput").ap() if STAGE == 'mixer' else None
    NPIECE = 10 + 4 + 16 + 16
    wscr = nc.dram_tensor("wscr", [NPIECE, 128, 16 * 512], BF16).ap()

    es = contextlib.ExitStack()
    with es:
        def sb(name, shape, dt):
            return es.enter_context(nc.sbuf_tensor(name, list(shape), dt))

        identb = sb("identb", [128, 128], BF16)
        identf = sb("identf", [128, 128], F32)
        maskA = sb("maskA", [128, 128], F32)
        ones = sb("ones", [128, 128], F32)
        smask = sb("smask", [128, 512], F32)
        invc = sb("invc", [128, 64], F32)
        prow = sb("prow", [72, 128], F32)
        par = sb("par", [128, 72], F32)
        par2 = sb("par2", [128, 40], F32)
        gpost = sb("gpost", [128, 2, D], F32)
        poolw = sb("poolw", [128, 8, 256], BF16)
        S = sb("S", [128, 8, 128], F32)
        Sb = sb("Sb", [128, 8, 128], BF16)
        halo = sb("halo", [128, 8, 16], F32)
        xbuf = sb("xbuf", [128, 4, D], F32)
        hy = sb("hy", [128, 2 * 16 * 512], BF16)
        ring = sb("ring", [128, 2, 16 * 512], BF16)
        tiny = sb("tiny", [128, 64], F32)
        dec_t = sb("dec_t", [128, 64], F32)
        ARENA = 78016
        arena = sb("arena", [128, ARENA], U8)
        banks = [es.enter_context(nc.psum_tensor("bank%d" % b, [128, 512], F32)) for b in range(8)]

        sem_names = {}

        def getsem(key):
            if key not in sem_names:
                sem_names[key] = es.enter_context(nc.semaphore("s_" + key.replace(":", "_")))
            return sem_names[key]

        def av(off, n_elem, dt, esz):
            return arena[:, off:off + n_elem * esz].bitcast(dt)

        def AR(off, nbytes):
            return ("arena", off, off + nbytes)

        hT = hy[:, 0:8192].rearrange("p (c t) -> p c t", t=512)
        yT = hy[:, 8192:16384].rearrange("p (c t) -> p c t", t=512)
        ffv = hy[:].bitcast(F32).rearrange("p (s n) -> p s n", n=D)

        def R_hT(c0=0, c1=16):
            return ("hy", c0 * 1024, c1 * 1024)

        def R_yT(c0=0, c1=16):
            return ("hy", 16384 + c0 * 1024, 16384 + c1 * 1024)

        def R_ff(s):
            return ("hy", s * 8192, (s + 1) * 8192)

        def R_x(s):
            return ("xbuf", s * 8192, (s + 1) * 8192)

        def R_bank(b, lo=0, hi=2048):
            return ("ps%d" % b, 0, 2048)

        def R_ring(s):
            return ("ring", s * 16384, (s + 1) * 16384)

        def R_tiny(c0, c1):
            return ("tiny", c0 * 4, c1 * 4)

        HB = 5120
        def qh_v(h): return av(h * HB, 512, BF16, 2), AR(h * HB, 1024)
        def kt_v(h): return av(h * HB + 1024, 512, BF16, 2), AR(h * HB + 1024, 1024)
        def khtok_v(h): return av(h * HB + 2048, 512, BF16, 2).rearrange("p (s d) -> p s d", d=128), AR(h * HB + 2048, 1024)
        def v_v(h): return av(h * HB + 3072, 512, BF16, 2).rearrange("p (s d) -> p s d", d=128), AR(h * HB + 3072, 1024)
        def rg_v(h): return av(h * HB + 4096, 512, BF16, 2), AR(h * HB + 4096, 1024)
        def og_v(i): return av(40960 + i * 2048, 512, F32, 4), AR(40960 + i * 2048, 2048)
        def T_v(i): return av(49152 + i * 2048, 512, F32, 4), AR(49152 + i * 2048, 2048)
        khT_v = (av(61440, 512, BF16, 2), AR(61440, 1024))
        og2_v = T_v(0)
        rstd_v = T_v(1)
        def Am_v(i): return av(62464 + i * 256, 128, BF16, 2), AR(62464 + i * 256, 256)
        ubuf_v = (av(63488, 528, F32, 4), AR(63488, 2112))
        sA_v = (av(65600, 528, F32, 4), AR(65600, 2112))
        sB_v = (av(67712, 528, F32, 4), AR(67712, 2112))
        def dT_v(c): return av(69824 + c * 1024, 512, BF16, 2), AR(69824 + c * 1024, 1024)
        mix_all = av(0, 4 * D, F32, 4).rearrange("p (s n) -> p s n", n=D)
        def R_mix(s): return AR(s * 8192, 8192)
        hid = av(0, 64 * 512, BF16, 2).rearrange("p (c t) -> p c t", t=512)
        def R_hid(c0, c1): return AR(c0 * 1024, (c1 - c0) * 1024)
        junkA = av(57344, 2048, BF16, 2)
        R_junkA = AR(57344, 4096)
        junkB = av(69824, 512, BF16, 2)
        R_junkB = AR(69824, 1024)

        def ACT(out, in_, func, reads, writes, **kw):
            P.emit("act", lambda e: e.activation(out=out, in_=in_, func=func, **kw), reads, writes)

        def DVE_tt(out, in0, in1, op, reads, writes):
            P.emit("dve", lambda e: e.tensor_tensor(out=out, in0=in0, in1=in1, op=op), reads, writes)

        def DVE_ts(out, in0, s1, s2, op0, op1, reads, writes):
            if op1 is None:
                P.emit("dve", lambda e: e.tensor_scalar(out=out, in0=in0, scalar1=s1, scalar2=None, op0=op0), reads, writes)
            else:
                P.emit("dve", lambda e: e.tensor_scalar(out=out, in0=in0, scalar1=s1, scalar2=s2, op0=op0, op1=op1), reads, writes)

        def DVE_stt(out, in0, scalar, in1, op0, op1, reads, writes):
            P.emit("dve", lambda e: e.scalar_tensor_tensor(out=out, in0=in0, scalar=scalar, in1=in1, op0=op0, op1=op1), reads, writes)

        def DVE_copy(out, in_, reads, writes):
            P.emit("dve", lambda e: e.tensor_copy(out=out, in_=in_), reads, writes)

        def DVE_recip(out, in_, reads, writes):
            P.emit("dve", lambda e: e.reciprocal(out=out, in_=in_), reads, writes)

        def DVE_scan(out, d0, d1, reads, writes):
            P.emit("dve", lambda e: e.tensor_tensor_scan(out=out, data0=d0, data1=d1, initial=0.0, op0=ALU.mult, op1=ALU.add), reads, writes)

        def DVE_rsum(out, in_, reads, writes):
            P.emit("dve", lambda e: e.reduce_sum(out=out, in_=in_, axis=mybir.AxisListType.X), reads, writes)

        def ENG_memset(eng, ap, val, writes):
            P.emit(eng, lambda e: e.memset(ap, val), (), writes)

        def POOL_tt(out, in0, in1, op, reads, writes):
            P.emit("pool", lambda e: e.tensor_tensor(out=out, in0=in0, in1=in1, op=op), reads, writes)

        def POOL_copy(out, in_, reads, writes):
            P.emit("pool", lambda e: e.tensor_copy(out=out, in_=in_), reads, writes)

        def MM(out, lhsT, rhs, start, stop, reads, writes, inc):
            P.emit("pe", lambda e: e.matmul(out, lhsT=lhsT, rhs=rhs, start=start, stop=stop), reads, writes, inc=inc)

        def TR(out, in_, ident, reads, writes, inc):
            P.emit("pe", lambda e: e.transpose(out=out, in_=in_, identity=ident), reads, writes, inc=inc)

        def DMA(out, in_, reads, writes, slot):
            P.emit("sp", lambda e: e.dma_start(out=out, in_=in_), reads, writes, dma_slot=slot)

        def FULL(name, nbytes):
            return (name, 0, nbytes)

        DMA(identb[:], identb_d, (), [FULL("identb", 256)], "c0")
        DMA(identf[:], identf_d, (), [FULL("identf", 512)], "c1")
        DMA(maskA[:], maskA_d, (), [FULL("maskA", 512)], "c2")
        DMA(invc[:], invc_d, (), [FULL("invc", 256)], "c3")
        DMA(prow[0:16, :], lbl_d, (), [("prow", 0, 1)], "c4")
        DMA(prow[16:24, :], nw_d, (), [("prow", 1, 2)], "c5")
        DMA(prow[24:32, :], pool_b_d, (), [("prow", 2, 3)], "c6")
        DMA(prow[32:40, :], pool_s_d, (), [("prow", 3, 4)], "c7")
        DMA(prow[40:56, :], g_pre1_d, (), [("prow", 4, 5)], "c8")
        DMA(prow[56:72, :], g_pre2_d, (), [("prow", 5, 6)], "c9")
        DMA(gpost[:, 0, :], g_post1_d.partition_broadcast(128), (), [("gpost", 0, 8192)], "c10")
        DMA(gpost[:, 1, :], g_post2_d.partition_broadcast(128), (), [("gpost", 8192, 16384)], "c11")
        ENG_memset("pool", ones[:], 1.0, [FULL("ones", 512)])
        ENG_memset("pool", smask[:], 1.0, [FULL("smask", 2048)])
        ENG_memset("pool", smask[:].rearrange("p (c t) -> p c t", t=64)[:, :, 0:1], 0.0, [FULL("smask", 2048)])
        TR(banks[4][0:128, 0:72], prow[:, :], identf[0:72, 0:72], [("prow", 0, 6), FULL("identf", 512)], [R_bank(4)], True)
        DVE_copy(par[:], banks[4][:, 0:72], [R_bank(4)], [FULL("par", 288)])
        Rp = FULL("par", 288)
        Rp2 = FULL("par2", 160)
        DVE_tt(par2[:, 32:40], par[:, 8:16], par[:, 0:8], ALU.subtract, [Rp], [Rp2])
        ACT(par2[:, 32:40], par2[:, 32:40], AF.Exp, [Rp2], [Rp2])
        DVE_ts(par2[:, 32:40], par2[:, 32:40], 1.0, None, ALU.add, None, [Rp2], [Rp2])
        DVE_recip(par2[:, 0:8], par2[:, 32:40], [Rp2], [Rp2])
        DVE_ts(par2[:, 8:16], par2[:, 0:8], -1.0, 1.0, ALU.mult, ALU.add, [Rp2], [Rp2])
        DVE_ts(par2[:, 16:24], par2[:, 8:16], -1.0, None, ALU.mult, None, [Rp2], [Rp2])
        DVE_tt(par2[:, 24:32], par[:, 24:32], par[:, 32:40], ALU.mult, [Rp, Rp2], [Rp2])

        def stg_v(i):
            return av(i * 16384, 4096, F32, 4).rearrange("p (c n) -> p c n", n=512), AR(i * 16384, 16384)

        def cvt_v(i):
            return ring[:, i // 2, (i % 2) * 4096:(i % 2) * 4096 + 4096].rearrange("p (c n) -> p c n", n=512), ("ring", i * 8192, (i + 1) * 8192)

        pieces_src = []
        for p in range(10):
            pieces_src.append(w_in_d[:, p * 512:(p + 1) * 512])
        for p in range(4):
            pieces_src.append(w_out_d[:, p * 512:(p + 1) * 512])
        for p in range(16):
            pieces_src.append(w_mi_d[:, p * 512:(p + 1) * 512])
        for cg in range(4):
            for jp in range(4):
                pieces_src.append(w_mo_d[jp * 2048:(jp + 1) * 2048, cg * 512:(cg + 1) * 512])
        PIECE_WOUT = 10
        PIECE_MI = 14
        PIECE_MO = 30
        cv_engs = ["act", "dve", "pool"]
        u = 0
        for pi, src in enumerate(pieces_src):
            for half in range(2):
                sl = u % 4
                sv_ap, sv_r = stg_v(sl)
                cv_ap, cv_r = cvt_v(sl)
                srcv = src[half * 1024:(half + 1) * 1024, :].rearrange("(c p) n -> p c n", p=128)
                DMA(sv_ap, srcv, (), [sv_r], "stg%d" % sl)
                eng = cv_engs[u % 3]
                if eng == "act":
                    ACT(cv_ap, sv_ap, AF.Copy, [sv_r], [cv_r])
                elif eng == "dve":
                    DVE_copy(cv_ap, sv_ap, [sv_r], [cv_r])
                else:
                    POOL_copy(cv_ap, sv_ap, [sv_r], [cv_r])
                dst = wscr[pi, :, half * 4096:(half + 1) * 4096].rearrange("p (c n) -> p c n", n=512)
                DMA(dst, cv_ap, [cv_r], [("wscr", pi * 2 + half, pi * 2 + half + 1)], "cvt%d" % sl)
                u += 1
        pw_stage = av(65536, 2048, F32, 4).rearrange("p (a n) -> p a n", n=256)
        R_pws = AR(65536, 8192)
        DMA(pw_stage, pool_w_d.rearrange("j (kc p) n -> p (j kc) n", p=128), (), [R_pws], "c12")
        DVE_copy(poolw[:], pw_stage, [R_pws], [FULL("poolw", 4096)])

        sched = []
        tile_order = [2, 4, 6, 8, 0, 3, 5, 7, 9, 1] + list(range(PIECE_WOUT, PIECE_WOUT + 4)) + \
            list(range(PIECE_MI, PIECE_MI + 16)) + list(range(PIECE_MO, PIECE_MO + 16))
        for t in range(NT):
            sched.extend(tile_order)
        state = {"loaded": 0, "cur": 0}

        def ensure_loaded(upto):
            while state["loaded"] <= upto and state["loaded"] < len(sched):
                i = state["loaded"]
                pi = sched[i]
                sl = i % 2
                DMA(ring[:, sl, :], wscr[pi, :, :], [("wscr", pi * 2, pi * 2 + 2)], [R_ring(sl)], "ring%d" % sl)
                state["loaded"] += 1

        def next_piece(expect):
            i = state["cur"]
            assert sched[i] == expect, (sched[i], expect)
            ensure_loaded(i + 1)
            state["cur"] += 1
            sl = i % 2
            return ring[:, sl, :].rearrange("p (c n) -> p c n", n=512), R_ring(sl)

        bigbank = {"i": 0}

        def next_bank():
            b = bigbank["i"] % 4
            bigbank["i"] += 1
            return b

        def rstd_from_ms(ms_ap, out_ap, scale, rd, wr):
            ACT(out_ap, ms_ap, AF.Ln, rd, wr, bias=EPS, scale=scale)
            ACT(out_ap, out_ap, AF.Exp, wr, wr, scale=-0.5)

        def prenorm_transpose(gcol0, dstT, R_dst):
            for s in range(4):
                ms = tiny[:, s:s + 1]
                rs = tiny[:, 4 + s:5 + s]
                ACT(junkA, xbuf[:, s, :], AF.Square, [R_x(s)], [R_junkA, R_tiny(s, s + 1)],
                    scale=float(D) ** -0.5, accum_out=ms)
                rstd_from_ms(ms, rs, 1.0, [R_tiny(s, s + 1)], [R_tiny(4 + s, 5 + s)])
                xn, R_xn = av(49152 + (s % 2) * 4096, 2048, BF16, 2), AR(49152 + (s % 2) * 4096, 4096)
                DVE_ts(xn, xbuf[:, s, :], rs, None, ALU.mult, None, [R_x(s), R_tiny(4 + s, 5 + s)], [R_xn])
                for half in range(2):
                    b = next_bank()
                    pb = banks[b][:].bitcast(BF16)
                    for j in range(8):
                        c = half * 8 + j
                        TR(pb[:, j * 128:(j + 1) * 128], xn[:, c * 128:(c + 1) * 128], identb[:],
                           [R_xn, FULL("identb", 256)], [R_bank(b)], inc=(j == 7))
                    gain = par[:, gcol0 + half * 8:gcol0 + half * 8 + 8].unsqueeze(2).broadcast_to([128, 8, 128])
                    DVE_tt(dstT[:, half * 8:half * 8 + 8, s * 128:(s + 1) * 128],
                           pb.rearrange("p (c t) -> p c t", t=128), gain, ALU.mult,
                           [R_bank(b), Rp], [R_dst(half * 8, half * 8 + 8)])

        dec3 = dec_t[:, :].rearrange("p (h c) -> p h c", c=8)

        def R_dec(h):
            return ("dec_t", h * 32, h * 32 + 32)

        def R_S(h):
            return ("S", h * 512, (h + 1) * 512)

        def R_Sb(h):
            return ("Sb", h * 256, (h + 1) * 256)

        R_ident = FULL("identb", 256)
        WIN = [2, 2, 4, 4, 8, 8, 16, 16]

        for ti in range(NT):
            pos = ti % TPS
            first = (pos == 0)
            tok0 = ti * TT
            for s in range(4):
                DMA(xbuf[:, s, :], x_d[tok0 + s * 128:tok0 + (s + 1) * 128, :], (), [R_x(s)], "x%d" % s)
            if first:
                ENG_memset("pool", S[:], 0.0, [FULL("S", 4096)])
                ENG_memset("pool", Sb[:], 0.0, [FULL("Sb", 2048)])
                ENG_memset("pool", halo[:], 0.0, [FULL("halo", 512)])
            def _early_store():
                for s in range(4):
                    DMA(out_d[tok0 + s * 128:tok0 + (s + 1) * 128, :], xbuf[:, s, :], [R_x(s)],
                        [("out", tok0 + s * 128, tok0 + (s + 1) * 128)], "st%d" % s)
            if STAGE == 'pro':
                _early_store()
                break
            prenorm_transpose(40, hT, R_hT)
            if STAGE == 'A':
                _early_store()
                break

            def proj_ws(piece_idx, evac):
                wv, wr = next_piece(piece_idx)
                for sl in range(4):
                    b = next_bank()
                    for c in range(16):
                        MM(banks[b][:, :], wv[:, c, sl * 128:(sl + 1) * 128], hT[:, c, :], c == 0, c == 15,
                           [wr, R_hT(c, c + 1)], [R_bank(b)], inc=(c == 15))
                    evac(sl, b)

            def hgrn_prep(G, upto=9):
                hs = [4 * G + i for i in range(4)]

                def evac_q(sl, b):
                    h = hs[sl]
                    qh, Rq = qh_v(h)
                    ACT(qh, banks[b][:, :], AF.Copy, [R_bank(b)], [Rq])
                proj_ws(2 + G, evac_q)
                if upto <= 1:
                    return

                def evac_f(sl, b):
                    h = hs[sl]
                    T1, R1 = T_v(0); T2, R2 = T_v(1); T3, R3 = T_v(2); T4, R4 = T_v(3); T5, R5 = T_v(4)
                    ACT(T1, banks[b][:, :], AF.Exp, [R_bank(b)], [R1], scale=-1.0)
                    DVE_ts(T1, T1, 1.0, None, ALU.add, None, [R1], [R1])
                    DVE_recip(T2, T1, [R1], [R2])
                    ACT(T3, T2, AF.Ln, [R2, Rp2], [R3], scale=par2[:, 8 + h:9 + h], bias=par2[:, h:h + 1])
                    DVE_ts(T4, T2, par2[:, 16 + h:17 + h], par2[:, 8 + h:9 + h], ALU.mult, ALU.add, [R2, Rp2], [R4])
                    DVE_scan(T1, smask[:], T3, [R3, FULL("smask", 2048)], [R1])
                    ACT(T3, T1, AF.Exp, [R1], [R3])
                    ACT(T5, T1, AF.Exp, [R1], [R5], scale=-1.0)
                    DVE_copy(dec3[:, h, :], T3.rearrange("p (c t) -> p c t", t=64)[:, :, 63], [R3], [R_dec(h)])
                    qh, Rq = qh_v(h)
                    DVE_tt(qh, qh, T3, ALU.mult, [Rq, R3], [Rq])
                    DVE_tt(T2, T4, T5, ALU.mult, [R4, R5], [R2])
                    kt, Rkt = kt_v(h)
                    ACT(kt, T2, AF.Copy, [R2], [Rkt])
                    khT, RkhT = khT_v
                    DVE_tt(khT.rearrange("p (c t) -> p c t", t=64), T2.rearrange("p (c t) -> p c t", t=64),
                           dec3[:, h, :].unsqueeze(2).broadcast_to([128, 8, 64]), ALU.mult, [R2, R_dec(h)], [RkhT])
                    pb = banks[4][:].bitcast(BF16)
                    for p in range(4):
                        TR(pb[:, p * 128:(p + 1) * 128], khT[:, p * 128:(p + 1) * 128], identb[:],
                           [RkhT, R_ident], [R_bank(4, 0, 1024)], inc=(p == 3))
                    kh, Rkh = khtok_v(h)
                    ACT(kh, pb[:, 0:512].rearrange("p (s d) -> p s d", d=128), AF.Copy, [R_bank(4, 0, 1024)], [Rkh])
                proj_ws(4 + G, evac_f)
                if upto <= 2:
                    return

                wv, wr = next_piece(6 + G)
                for s in range(4):
                    b = next_bank()
                    for c in range(16):
                        MM(banks[b][:, :], hT[:, c, s * 128:(s + 1) * 128], wv[:, c, :], c == 0, c == 15,
                           [wr, R_hT(c, c + 1)], [R_bank(b)], inc=(c == 15))
                    for i, h in enumerate(hs):
                        vv, Rv = v_v(h)
                        ACT(vv[:, s, :], banks[b][:, i * 128:(i + 1) * 128], AF.Copy, [R_bank(b)], [Rv])

                if upto <= 3:
                    return

                def evac_g(sl, b):
                    h = hs[sl]
                    T6, R6 = T_v(5)
                    rg, Rrg = rg_v(h)
                    ACT(T6, banks[b][:, :], AF.Exp, [R_bank(b)], [R6], scale=-1.0)
                    DVE_ts(T6, T6, 1.0, None, ALU.add, None, [R6], [R6])
                    DVE_recip(rg, T6, [R6], [Rrg])
                proj_ws(8 + G, evac_g)

            def hgrn_chunks(G):
                hs = [4 * G + i for i in range(4)]
                for p in range(4):
                    for i, h in enumerate(hs):
                        kt, Rkt = kt_v(h)
                        qh, Rq = qh_v(h)
                        Rq5 = R_bank(5, i * 512, (i + 1) * 512)
                        MM(banks[5][:, i * 128:(i + 1) * 128], kt[:, p * 128:(p + 1) * 128], qh[:, p * 128:(p + 1) * 128],
                           True, True, [Rkt, Rq], [Rq5], inc=True)
                        Am, RAm = Am_v(i)
                        DVE_tt(Am, banks[5][:, i * 128:(i + 1) * 128], maskA[:], ALU.mult, [Rq5, FULL("maskA", 512)], [RAm])
                    for half in range(2):
                        c = 2 * p + half
                        for i, h in enumerate(hs):
                            qh, Rq = qh_v(h)
                            vv, Rv = v_v(h)
                            kh, Rkh = khtok_v(h)
                            Am, RAm = Am_v(i)
                            oq = i
                            o_lo = oq * 128 + half * 64
                            Ro = R_bank(7)
                            o_ap = banks[7][:, o_lo:o_lo + 64]
                            MM(o_ap, Sb[:, h, :], qh[:, c * 64:(c + 1) * 64], True, False, [R_Sb(h), Rq], [Ro], inc=False)
                            MM(o_ap, vv[:, p, :], Am[:, half * 64:(half + 1) * 64], False, True, [Rv, RAm], [Ro], inc=True)
                            uq = i
                            Ru = R_bank(6)
                            MM(banks[6][:, uq * 128:(uq + 1) * 128], kh[64 * half:64 * half + 64, p, :],
                               vv[64 * half:64 * half + 64, p, :], True, True, [Rkh, Rv], [Ru], inc=True)
                            DVE_stt(S[:, h, :], S[:, h, :], dec3[:, h, c:c + 1], banks[6][:, uq * 128:(uq + 1) * 128],
                                    ALU.mult, ALU.add, [R_S(h), R_dec(h), Ru], [R_S(h)])
                            ACT(Sb[:, h, :], S[:, h, :], AF.Copy, [R_S(h)], [R_Sb(h)])
                            if half == 1:
                                og, Rog = og_v(i)
                                rg, Rrg = rg_v(h)
                                DVE_tt(og[:, p * 128:(p + 1) * 128], banks[7][:, oq * 128:(oq + 1) * 128],
                                       rg[:, p * 128:(p + 1) * 128], ALU.mult, [Ro, Rrg], [Rog])
                for i, h in enumerate(hs):
                    og, Rog = og_v(i)
                    og2, Rog2 = og2_v
                    rs, Rrs = rstd_v
                    POOL_tt(og2, og, og, ALU.mult, [Rog], [Rog2])
                    MM(banks[7][:, :], ones[:], og2, True, True, [FULL("ones", 512), Rog2], [R_bank(7)], inc=True)
                    rstd_from_ms(banks[7][:, :], rs, 1.0 / 128.0, [R_bank(7)], [Rrs])
                    DVE_stt(yT[:, 8 + h, :], og, par[:, 16 + h:17 + h], rs, ALU.mult, ALU.mult, [Rog, Rrs, Rp], [R_yT(8 + h, 9 + h)])

            def pool_proj(piece):
                def evac_u(sl, b):
                    ch = piece * 4 + sl
                    w = WIN[ch]
                    ub, Rub = ubuf_v
                    sA, RsA = sA_v
                    sB, RsB = sB_v
                    Rh = ("halo", ch * 64, ch * 64 + 64)
                    POOL_copy(ub[:, 0:16], halo[:, ch, :], [Rh], [Rub])
                    ACT(ub[:, 16:528], banks[b][:, :], AF.Copy, [R_bank(b)], [Rub])
                    POOL_tt(sA[:, 1:528], ub[:, 1:528], ub[:, 0:527], ALU.add, [Rub], [RsA])
                    fin, Rfin = sA, RsA
                    if w >= 4:
                        POOL_tt(sB[:, 3:528], sA[:, 3:528], sA[:, 1:526], ALU.add, [RsA], [RsB])
                        fin, Rfin = sB, RsB
                    if w >= 8:
                        POOL_tt(sA[:, 7:528], sB[:, 7:528], sB[:, 3:524], ALU.add, [RsB], [RsA])
                        fin, Rfin = sA, RsA
                    if w >= 16:
                        POOL_tt(sB[:, 15:528], sA[:, 15:528], sA[:, 7:520], ALU.add, [RsA], [RsB])
                        fin, Rfin = sB, RsB
                    POOL_copy(halo[:, ch, :], ub[:, 512:528], [Rub], [Rh])
                    dT, RdT = dT_v(ch)
                    DVE_stt(dT, fin[:, 16:528], 1.0 / w, ub[:, 16:528], ALU.mult, ALU.subtract, [Rfin, Rub], [RdT])
                    if first:
                        widx = ch // 2
                        t16 = tiny[:, 32:48]
                        POOL_tt(t16, fin[:, 16:32], invc[:, widx * 16:(widx + 1) * 16], ALU.mult,
                                [Rfin, FULL("invc", 256)], [R_tiny(32, 48)])
                        POOL_tt(dT[:, 0:16], t16, ub[:, 16:32], ALU.subtract, [R_tiny(32, 48), Rub], [RdT])
                proj_ws(piece, evac_u)

            def pool_mm():
                for j in range(4):
                    for oc in range(2):
                        b = next_bank()
                        for kc in range(2):
                            dT, RdT = dT_v(2 * j + kc)
                            MM(banks[b][:, :], poolw[:, j * 2 + kc, oc * 128:(oc + 1) * 128], dT, kc == 0, kc == 1,
                               [FULL("poolw", 4096), RdT], [R_bank(b)], inc=(kc == 1))
                        ch = 2 * j + oc
                        ACT(yT[:, ch, :], banks[b][:, :], AF.Identity, [R_bank(b), Rp, Rp2], [R_yT(ch, ch + 1)],
                            scale=par[:, 32 + ch:33 + ch], bias=par2[:, 24 + ch:25 + ch])

            if STAGE and STAGE.startswith('prep0_'):
                hgrn_prep(0, int(STAGE[6:]))
                _early_store()
                break
            hgrn_prep(0)
            if STAGE == 'prep0':
                _early_store()
                break
            pool_proj(0)
            if STAGE == 'pool0':
                _early_store()
                break
            hgrn_chunks(0)
            if STAGE == 'chunks0':
                _early_store()
                break
            hgrn_prep(1)
            pool_proj(1)
            hgrn_chunks(1)
            pool_mm()
            if STAGE == 'mixer' and ti == NT - 1:
                DMA(dbg_d, hy[:, 8192:16384], [R_yT(0, 16)], [("dbg", 0, 1)], "dbg")

            def post_norm_residual(src_ap_fn, R_src_fn, gi, store):
                for s in range(4):
                    ms = tiny[:, 24 + s:25 + s]
                    rs = tiny[:, 28 + s:29 + s]
                    DVE_rsum(ms, tiny[:, 8 + s * 4:12 + s * 4], [R_tiny(8 + s * 4, 12 + s * 4)], [R_tiny(24 + s, 25 + s)])
                    rstd_from_ms(ms, rs, 1.0, [R_tiny(24 + s, 25 + s)], [R_tiny(28 + s, 29 + s)])
                    src = src_ap_fn(s)
                    DVE_stt(src, src, rs, gpost[:, gi, :], ALU.mult, ALU.mult,
                            [R_src_fn(s), R_tiny(28 + s, 29 + s), ("gpost", gi * 8192, (gi + 1) * 8192)], [R_src_fn(s)])
                    POOL_tt(xbuf[:, s, :], xbuf[:, s, :], src, ALU.add, [R_x(s), R_src_fn(s)], [R_x(s)])
                    if store:
                        DMA(out_d[tok0 + s * 128:tok0 + (s + 1) * 128, :], xbuf[:, s, :], [R_x(s)],
                            [("out", tok0 + s * 128, tok0 + (s + 1) * 128)], "st%d" % s)

            def evac_tokmajor(b, dst_ap, R_dst, s, cg, junk, R_junk):
                ACT(dst_ap, banks[b][:, :], AF.Copy, [R_bank(b)], [R_dst])
                ACT(junk[:, 0:512], banks[b][:, :], AF.Square, [R_bank(b)], [R_junk, R_tiny(8 + s * 4 + cg, 9 + s * 4 + cg)],
                    scale=float(D) ** -0.5, accum_out=tiny[:, 8 + s * 4 + cg:9 + s * 4 + cg])

            for cg in range(4):
                wv, wr = next_piece(PIECE_WOUT + cg)
                for s in range(4):
                    b = next_bank()
                    for c in range(16):
                        MM(banks[b][:, :], yT[:, c, s * 128:(s + 1) * 128], wv[:, c, :], c == 0, c == 15,
                           [wr, R_yT(c, c + 1)], [R_bank(b)], inc=(c == 15))
                    evac_tokmajor(b, mix_all[:, s, cg * 512:(cg + 1) * 512], R_mix(s), s, cg, junkA, R_junkA)
            post_norm_residual(lambda s: mix_all[:, s, :], R_mix, 0, STAGE == 'mixer')
            if STAGE == 'mixer':
                for _p in range(32):
                    state['cur'] += 1
                continue

            prenorm_transpose(56, hT, R_hT)
            for p in range(16):
                wv, wr = next_piece(PIECE_MI + p)
                for sl in range(4):
                    b = next_bank()
                    for c in range(16):
                        MM(banks[b][:, :], wv[:, c, sl * 128:(sl + 1) * 128], hT[:, c, :], c == 0, c == 15,
                           [wr, R_hT(c, c + 1)], [R_bank(b)], inc=(c == 15))
                    k = (p * 4 + sl) % 2
                    rt, Rrt = av(65600 + k * 2112, 512, F32, 4), AR(65600 + k * 2112, 2048)
                    ACT(rt, banks[b][:, :], AF.Relu, [R_bank(b)], [Rrt])
                    hc = p * 4 + sl
                    POOL_tt(hid[:, hc, :], rt, rt, ALU.mult, [Rrt], [R_hid(hc, hc + 1)])
            for cg in range(4):
                for jp in range(4):
                    wv, wr = next_piece(PIECE_MO + cg * 4 + jp)
                    for s in range(4):
                        for c in range(16):
                            hc = jp * 16 + c
                            MM(banks[s][:, :], hid[:, hc, s * 128:(s + 1) * 128], wv[:, c, :],
                               (jp == 0 and c == 0), (jp == 3 and c == 15),
                               [wr, R_hid(hc, hc + 1)], [R_bank(s)], inc=(c == 15))
                for s in range(4):
                    evac_tokmajor(s, ffv[:, s, cg * 512:(cg + 1) * 512], R_ff(s), s, cg, junkB, R_junkB)
            post_norm_residual(lambda s: ffv[:, s, :], R_ff, 1, True)

        keys = set()
        for e in ENGS:
            for waits, fn, incinfo in P.streams[e]:
                for (k, v) in waits:
                    keys.add(k)
                if incinfo:
                    keys.add(incinfo[0])
        for k in sorted(keys):
            getsem(k)
        final_waits = [(k, v) for k, v in P.dmacnt.items() if k.startswith("dma:st") or k.startswith("dma:dbg")]

        with nc.allow_low_precision(reason="bf16 matmul operands, fp32 accumulation"), nc.Block() as block:
            def runner(name, extra=()):
                stream = P.streams[name]

                def f(e):
                    for waits, fn, incinfo in stream:
                        for (k, v) in waits:
                            e.wait_ge(getsem(k), v)
                        ins = fn(e)
                        if incinfo:
                            ins.then_inc(getsem(incinfo[0]), incinfo[1])
                    for (k, v) in extra:
                        e.wait_ge(getsem(k), v)
                return f
            block.sync(runner("sp", final_waits))
            block.tensor(runner("pe"))
            block.scalar(runner("act"))
            block.vector(runner("dve"))
            block.gpsimd(runner("pool"))
    return nc, P


_CACHE = {}


def _consts():
    identb = np.eye(128, dtype=np.float32).astype(ml_dtypes.bfloat16)
    identf = np.eye(128, dtype=np.float32)
    s = np.arange(128)[:, None]
    t = np.arange(128)[None, :]
    maskA = ((s // 64 == t // 64) & (s <= t)).astype(np.float32)
    invc = np.zeros((128, 64), np.float32)
    for wi, w in enumerate((2, 4, 8, 16)):
        invc[:, wi * 16:(wi + 1) * 16] = 1.0 / np.minimum(np.arange(16) + 1, w).astype(np.float32)
    return {"c_identb": identb, "c_identf": identf, "c_maskA": maskA, "c_invc": invc}


def _run(inputs, n_cores, nseq, seq, stage=None):
    key = (nseq, seq, stage)
    if key not in _CACHE:
        _CACHE[key] = build_program(nseq, seq, stage)[0]
    nc = _CACHE[key]
    f32 = lambda a: np.ascontiguousarray(np.asarray(a, dtype=np.float32))
    x = f32(inputs["x"])
    shared = {
        "w_in": f32(inputs["w_in"]).reshape(D, INW),
        "pool_w": f32(inputs["pool_w"]).reshape(4, 256, 256),
        "pool_b": f32(inputs["pool_b"]).reshape(8, 128),
        "pool_scale": f32(inputs["pool_scale"]).reshape(8, 128),
        "lb_logits": f32(inputs["lb_logits"]).reshape(16, 128),
        "hgrn_norm_w": f32(inputs["hgrn_norm_w"]).reshape(8, 128),
        "w_out": f32(inputs["w_out"]).reshape(D, D),
        "norm_mix_pre": f32(inputs["norm_mix_pre"]).reshape(16, 128),
        "norm_mix_post": f32(inputs["norm_mix_post"]).reshape(1, D),
        "norm_mlp_pre": f32(inputs["norm_mlp_pre"]).reshape(16, 128),
        "norm_mlp_post": f32(inputs["norm_mlp_post"]).reshape(1, D),
        "w_mlp_in": f32(inputs["w_mlp_in"]).reshape(D, DFF),
        "w_mlp_out": f32(inputs["w_mlp_out"]).reshape(DFF, D),
    }
    shared.update(_consts())
    in_maps = []
    for c in range(n_cores):
        m = dict(shared)
        m["x"] = np.ascontiguousarray(x[c * nseq:(c + 1) * nseq].reshape(nseq * seq, D))
        in_maps.append(m)
    res = run_bass_kernel_spmd(nc, in_maps, core_ids=list(range(n_cores)))
    if stage == 'mixer':
        global DBG
        DBG = [np.asarray(r["dbg"]) for r in res.results]
    outs = [np.asarray(r["out"], dtype=np.float32).reshape(nseq, seq, D) for r in res.results]
    return np.concatenate(outs, axis=0)


def kernel(**inputs):
    x = inputs["x"]
    B, S_, _ = x.shape
    return _run(inputs, N_CORES, B // N_CORES, S_)
```

```python
import contextlib
import numpy as np
import ml_dtypes
import concourse.bass as bass
import concourse.mybir as mybir
from concourse.bass_utils import run_bass_kernel_spmd

F32 = mybir.dt.float32
BF16 = mybir.dt.bfloat16
U8 = mybir.dt.uint8
AF = mybir.ActivationFunctionType
ALU = mybir.AluOpType

D = 2048
DFF = 8192
INW = 5120
TT = 512
EPS = 1e-6
N_CORES = 8
ENGS = ("pe", "act", "dve", "pool", "sp")


class Prog:
    def __init__(self):
        self.streams = {e: [] for e in ENGS}
        self.cnt = {e: 0 for e in ENGS}
        self.waited = {e: {} for e in ENGS}
        self.regions = {}
        self.dmacnt = {}

    def _gather(self, deps, key, lo, hi, is_write):
        lst = self.regions.setdefault(key, [])
        for ent in lst:
            if ent[0] < hi and lo < ent[1]:
                w = ent[2]
                if w is not None:
                    if deps.get(w[0], 0) < w[1]:
                        deps[w[0]] = w[1]
                if is_write:
                    for k, v in ent[3].items():
                        if deps.get(k, 0) < v:
                            deps[k] = v

    def _record(self, key, lo, hi, tok, is_write):
        lst = self.regions.setdefault(key, [])
        if is_write:
            lst[:] = [e for e in lst if not (lo <= e[0] and e[1] <= hi)]
            lst.append([lo, hi, tok, {}])
        else:
            for ent in lst:
                if ent[0] == lo and ent[1] == hi:
                    if ent[3].get(tok[0], 0) < tok[1]:
                        ent[3][tok[0]] = tok[1]
                    return
            lst.append([lo, hi, None, {tok[0]: tok[1]}])

    def emit(self, eng, fn, reads=(), writes=(), inc=True, dma_slot=None):
        deps = {}
        for (k, lo, hi) in reads:
            self._gather(deps, k, lo, hi, k.startswith("ps"))
        for (k, lo, hi) in writes:
            self._gather(deps, k, lo, hi, True)
        if eng == "pe":
            deps.pop("pe", None)
        waits = []
        wd = self.waited[eng]
        for k, v in deps.items():
            if wd.get(k, 0) < v:
                wd[k] = v
                waits.append((k, v))
        if dma_slot is not None:
            sk = "dma:" + dma_slot
            self.dmacnt[sk] = self.dmacnt.get(sk, 0) + 16
            tok = (sk, self.dmacnt[sk])
            incinfo = (sk, 16)
        else:
            if inc:
                self.cnt[eng] += 1
                tok = (eng, self.cnt[eng])
                incinfo = (eng, 1)
            else:
                tok = (eng, self.cnt[eng] + 1)
                incinfo = None
        for (k, lo, hi) in reads:
            self._record(k, lo, hi, tok, False)
        for (k, lo, hi) in writes:
            self._record(k, lo, hi, tok, True)
        self.streams[eng].append((waits, fn, incinfo))
        return tok


def build_program(NSEQ, SEQ, STAGE=None):
    NTOK = NSEQ * SEQ
    TPS = SEQ // TT
    NT = NSEQ * TPS
    nc = bass.Bass("TRN2", target_bir_lowering=False)
    P = Prog()

    def din(name, shape, dt=F32):
        return nc.dram_tensor(name, list(shape), dt, kind="ExternalInput").ap()

    x_d = din("x", [NTOK, D])
    w_in_d = din("w_in", [D, INW])
    pool_w_d = din("pool_w", [4, 256, 256])
    pool_b_d = din("pool_b", [8, 128])
    pool_s_d = din("pool_scale", [8, 128])
    lbl_d = din("lb_logits", [16, 128])
    nw_d = din("hgrn_norm_w", [8, 128])
    w_out_d = din("w_out", [D, D])
    g_pre1_d = din("norm_mix_pre", [16, 128])
    g_post1_d = din("norm_mix_post", [1, D])
    g_pre2_d = din("norm_mlp_pre", [16, 128])
    g_post2_d = din("norm_mlp_post", [1, D])
    w_mi_d = din("w_mlp_in", [D, DFF])
    w_mo_d = din("w_mlp_out", [DFF, D])
    identb_d = din("c_identb", [128, 128], BF16)
    identf_d = din("c_identf", [128, 128])
    maskA_d = din("c_maskA", [128, 128])
    invc_d = din("c_invc", [128, 64])
    out_d = nc.dram_tensor("out", [NTOK, D], F32, kind="ExternalOutput").ap()
    dbg_d = nc.dram_tensor("dbg", [128, 8192], BF16, kind="ExternalOutput").ap() if STAGE == 'mixer' else None
    NPIECE = 10 + 4 + 16 + 16
    wscr = nc.dram_tensor("wscr", [NPIECE, 128, 16 * 512], BF16).ap()

    es = contextlib.ExitStack()
    with es:
        def sb(name, shape, dt):
            return es.enter_context(nc.sbuf_tensor(name, list(shape), dt))

        identb = sb("identb", [128, 128], BF16)
        identf = sb("identf", [128, 128], F32)
        maskA = sb("maskA", [128, 128], F32)
        ones = sb("ones", [128, 128], F32)
        smask = sb("smask", [128, 512], F32)
        invc = sb("invc", [128, 64], F32)
        prow = sb("prow", [72, 128], F32)
        par = sb("par", [128, 72], F32)
        par2 = sb("par2", [128, 40], F32)
        gpost = sb("gpost", [128, 2, D], F32)
        poolw = sb("poolw", [128, 8, 256], BF16)
        S = sb("S", [128, 8, 128], F32)
        Sb = sb("Sb", [128, 8, 128], BF16)
        halo = sb("halo", [128, 8, 16], F32)
        xbuf = sb("xbuf", [128, 4, D], F32)
        hy = sb("hy", [128, 2 * 16 * 512], BF16)
        ring = sb("ring", [128, 2, 16 * 512], BF16)
        tiny = sb("tiny", [128, 64], F32)
        dec_t = sb("dec_t", [128, 64], F32)
        ARENA = 78016 + 2112
        arena = sb("arena", [128, ARENA], U8)
        banks = [es.enter_context(nc.psum_tensor("bank%d" % b, [128, 512], F32)) for b in range(8)]

        sem_names = {}

        def getsem(key):
            if key not in sem_names:
                sem_names[key] = es.enter_context(nc.semaphore("s_" + key.replace(":", "_")))
            return sem_names[key]

        def av(off, n_elem, dt, esz):
            return arena[:, off:off + n_elem * esz].bitcast(dt)

        def AR(off, nbytes):
            return ("arena", off, off + nbytes)

        hT = hy[:, 0:8192].rearrange("p (c t) -> p c t", t=512)
        yT = hy[:, 8192:16384].rearrange("p (c t) -> p c t", t=512)
        ffv = hy[:].bitcast(F32).rearrange("p (s n) -> p s n", n=D)

        def R_hT(c0=0, c1=16):
            return ("hy", c0 * 1024, c1 * 1024)

        def R_yT(c0=0, c1=16):
            return ("hy", 16384 + c0 * 1024, 16384 + c1 * 1024)

        def R_ff(s):
            return ("hy", s * 8192, (s + 1) * 8192)

        def R_x(s):
            return ("xbuf", s * 8192, (s + 1) * 8192)

        def R_bank(b, lo=0, hi=2048):
            return ("ps%d" % b, 0, 2048)

        def R_ring(s):
            return ("ring", s * 16384, (s + 1) * 16384)

        def R_tiny(c0, c1):
            return ("tiny", c0 * 4, c1 * 4)

        HB = 5120
        def qh_v(h): return av(h * HB, 512, BF16, 2), AR(h * HB, 1024)
        def kt_v(h): return av(h * HB + 1024, 512, BF16, 2), AR(h * HB + 1024, 1024)
        def khtok_v(h): return av(h * HB + 2048, 512, BF16, 2).rearrange("p (s d) -> p s d", d=128), AR(h * HB + 2048, 1024)
        def v_v(h): return av(h * HB + 3072, 512, BF16, 2).rearrange("p (s d) -> p s d", d=128), AR(h * HB + 3072, 1024)
        def rg_v(h): return av(h * HB + 4096, 512, BF16, 2), AR(h * HB + 4096, 1024)
        def og_v(i): return av(40960 + i * 2048, 512, F32, 4), AR(40960 + i * 2048, 2048)
        def T_v(i): return av(49152 + i * 2048, 512, F32, 4), AR(49152 + i * 2048, 2048)
        khT_v = (av(61440, 512, BF16, 2), AR(61440, 1024))
        og2_v = T_v(0)
        rstd_v = T_v(1)
        def Am_v(i): return av(62464 + i * 256, 128, BF16, 2), AR(62464 + i * 256, 256)
        ubuf_v = (av(63488, 528, F32, 4), AR(63488, 2112))
        sA_v = (av(65600, 528, F32, 4), AR(65600, 2112))
        sB_v = (av(67712, 528, F32, 4), AR(67712, 2112))
        def dT_v(c): return av(69824 + c * 1024, 512, BF16, 2), AR(69824 + c * 1024, 1024)
        mix_all = av(0, 4 * D, F32, 4).rearrange("p (s n) -> p s n", n=D)
        def R_mix(s): return AR(s * 8192, 8192)
        hid = av(0, 64 * 512, BF16, 2).rearrange("p (c t) -> p c t", t=512)
        def R_hid(c0, c1): return AR(c0 * 1024, (c1 - c0) * 1024)
        junkA = av(57344, 2048, BF16, 2)
        R_junkA = AR(57344, 4096)
        junkB = av(69824, 512, BF16, 2)
        R_junkB = AR(69824, 1024)

        def ACT(out, in_, func, reads, writes, **kw):
            P.emit("act", lambda e: e.activation(out=out, in_=in_, func=func, **kw), reads, writes)

        def DVE_tt(out, in0, in1, op, reads, writes):
            P.emit("dve", lambda e: e.tensor_tensor(out=out, in0=in0, in1=in1, op=op), reads, writes)

        def DVE_ts(out, in0, s1, s2, op0, op1, reads, writes):
            if op1 is None:
                P.emit("dve", lambda e: e.tensor_scalar(out=out, in0=in0, scalar1=s1, scalar2=None, op0=op0), reads, writes)
            else:
                P.emit("dve", lambda e: e.tensor_scalar(out=out, in0=in0, scalar1=s1, scalar2=s2, op0=op0, op1=op1), reads, writes)

        def DVE_stt(out, in0, scalar, in1, op0, op1, reads, writes):
            P.emit("dve", lambda e: e.scalar_tensor_tensor(out=out, in0=in0, scalar=scalar, in1=in1, op0=op0, op1=op1), reads, writes)

        def DVE_copy(out, in_, reads, writes):
            P.emit("dve", lambda e: e.tensor_copy(out=out, in_=in_), reads, writes)

        def DVE_recip(out, in_, reads, writes):
            P.emit("dve", lambda e: e.reciprocal(out=out, in_=in_), reads, writes)

        def DVE_scan(out, d0, d1, reads, writes):
            P.emit("dve", lambda e: e.tensor_tensor_scan(out=out, data0=d0, data1=d1, initial=0.0, op0=ALU.mult, op1=ALU.add), reads, writes)

        def DVE_rsum(out, in_, reads, writes):
            P.emit("dve", lambda e: e.reduce_sum(out=out, in_=in_, axis=mybir.AxisListType.X), reads, writes)

        def ENG_memset(eng, ap, val, writes):
            P.emit(eng, lambda e: e.memset(ap, val), (), writes)

        def POOL_tt(out, in0, in1, op, reads, writes):
            P.emit("pool", lambda e: e.tensor_tensor(out=out, in0=in0, in1=in1, op=op), reads, writes)

        def POOL_copy(out, in_, reads, writes):
            P.emit("pool", lambda e: e.tensor_copy(out=out, in_=in_), reads, writes)

        def MM(out, lhsT, rhs, start, stop, reads, writes, inc):
            P.emit("pe", lambda e: e.matmul(out, lhsT=lhsT, rhs=rhs, start=start, stop=stop), reads, writes, inc=inc)

        def TR(out, in_, ident, reads, writes, inc):
            P.emit("pe", lambda e: e.transpose(out=out, in_=in_, identity=ident), reads, writes, inc=inc)

        def DMA(out, in_, reads, writes, slot):
            P.emit("sp", lambda e: e.dma_start(out=out, in_=in_), reads, writes, dma_slot=slot)

        def FULL(name, nbytes):
            return (name, 0, nbytes)

        DMA(identb[:], identb_d, (), [FULL("identb", 256)], "c0")
        DMA(identf[:], identf_d, (), [FULL("identf", 512)], "c1")
        DMA(maskA[:], maskA_d, (), [FULL("maskA", 512)], "c2")
        DMA(invc[:], invc_d, (), [FULL("invc", 256)], "c3")
        DMA(prow[0:16, :], lbl_d, (), [("prow", 0, 1)], "c4")
        DMA(prow[16:24, :], nw_d, (), [("prow", 1, 2)], "c5")
        DMA(prow[24:32, :], pool_b_d, (), [("prow", 2, 3)], "c6")
        DMA(prow[32:40, :], pool_s_d, (), [("prow", 3, 4)], "c7")
        DMA(prow[40:56, :], g_pre1_d, (), [("prow", 4, 5)], "c8")
        DMA(prow[56:72, :], g_pre2_d, (), [("prow", 5, 6)], "c9")
        DMA(gpost[:, 0, :], g_post1_d.partition_broadcast(128), (), [("gpost", 0, 8192)], "c10")
        DMA(gpost[:, 1, :], g_post2_d.partition_broadcast(128), (), [("gpost", 8192, 16384)], "c11")
        ENG_memset("pool", ones[:], 1.0, [FULL("ones", 512)])
        ENG_memset("pool", smask[:], 1.0, [FULL("smask", 2048)])
        ENG_memset("pool", smask[:].rearrange("p (c t) -> p c t", t=64)[:, :, 0:1], 0.0, [FULL("smask", 2048)])
        TR(banks[4][0:128, 0:72], prow[:, :], identf[0:72, 0:72], [("prow", 0, 6), FULL("identf", 512)], [R_bank(4)], True)
        DVE_copy(par[:], banks[4][:, 0:72], [R_bank(4)], [FULL("par", 288)])
        Rp = FULL("par", 288)
        Rp2 = FULL("par2", 160)
        DVE_tt(par2[:, 32:40], par[:, 8:16], par[:, 0:8], ALU.subtract, [Rp], [Rp2])
        ACT(par2[:, 32:40], par2[:, 32:40], AF.Exp, [Rp2], [Rp2])
        DVE_ts(par2[:, 32:40], par2[:, 32:40], 1.0, None, ALU.add, None, [Rp2], [Rp2])
        DVE_recip(par2[:, 0:8], par2[:, 32:40], [Rp2], [Rp2])
        DVE_ts(par2[:, 8:16], par2[:, 0:8], -1.0, 1.0, ALU.mult, ALU.add, [Rp2], [Rp2])
        DVE_ts(par2[:, 16:24], par2[:, 8:16], -1.0, None, ALU.mult, None, [Rp2], [Rp2])
        DVE_tt(par2[:, 24:32], par[:, 24:32], par[:, 32:40], ALU.mult, [Rp, Rp2], [Rp2])

        def stg_v(i):
            return av(i * 16384, 4096, F32, 4).rearrange("p (c n) -> p c n", n=512), AR(i * 16384, 16384)

        def cvt_v(i):
            return ring[:, i // 2, (i % 2) * 4096:(i % 2) * 4096 + 4096].rearrange("p (c n) -> p c n", n=512), ("ring", i * 8192, (i + 1) * 8192)

        pieces_src = []
        for p in range(10):
            pieces_src.append(w_in_d[:, p * 512:(p + 1) * 512])
        for p in range(4):
            pieces_src.append(w_out_d[:, p * 512:(p + 1) * 512])
        for p in range(16):
            pieces_src.append(w_mi_d[:, p * 512:(p + 1) * 512])
        for cg in range(4):
            for jp in range(4):
                pieces_src.append(w_mo_d[jp * 2048:(jp + 1) * 2048, cg * 512:(cg + 1) * 512])
        PIECE_WOUT = 10
        PIECE_MI = 14
        PIECE_MO = 30
        cv_engs = ["act", "dve", "pool"]
        units = [(pi, half) for pi in range(len(pieces_src)) for half in range(2)]

        def unit_load(u):
            pi, half = units[u]
            sl = u % 4
            sv_ap, sv_r = stg_v(sl)
            srcv = pieces_src[pi][half * 1024:(half + 1) * 1024, :].rearrange("(c p) n -> p c n", p=128)
            DMA(sv_ap, srcv, (), [sv_r], "stg%d" % sl)

        LOOKAHEAD = 3
        for u in range(min(LOOKAHEAD, len(units))):
            unit_load(u)
        for u, (pi, half) in enumerate(units):
            if u + LOOKAHEAD < len(units):
                unit_load(u + LOOKAHEAD)
            sl = u % 4
            sv_ap, sv_r = stg_v(sl)
            cv_ap, cv_r = cvt_v(sl)
            eng = cv_engs[u % 3]
            if eng == "act":
                ACT(cv_ap, sv_ap, AF.Copy, [sv_r], [cv_r])
            elif eng == "dve":
                DVE_copy(cv_ap, sv_ap, [sv_r], [cv_r])
            else:
                POOL_copy(cv_ap, sv_ap, [sv_r], [cv_r])
            dst = wscr[pi, :, half * 4096:(half + 1) * 4096].rearrange("p (c n) -> p c n", n=512)
            DMA(dst, cv_ap, [cv_r], [("wscr", pi * 2 + half, pi * 2 + half + 1)], "cvt%d" % sl)
        pw_stage = av(65536, 2048, F32, 4).rearrange("p (a n) -> p a n", n=256)
        R_pws = AR(65536, 8192)
        DMA(pw_stage, pool_w_d.rearrange("j (kc p) n -> p (j kc) n", p=128), (), [R_pws], "c12")
        DVE_copy(poolw[:], pw_stage, [R_pws], [FULL("poolw", 4096)])

        sched = []
        tile_order = [2, 4, 0, 3, 5, 1, 8, 9, 6, 7] + list(range(PIECE_WOUT, PIECE_WOUT + 4)) + \
            list(range(PIECE_MI, PIECE_MI + 16)) + list(range(PIECE_MO, PIECE_MO + 16))
        for t in range(NT):
            sched.extend(tile_order)
        state = {"loaded": 0, "cur": 0}

        def ensure_loaded(upto):
            while state["loaded"] <= upto and state["loaded"] < len(sched):
                i = state["loaded"]
                pi = sched[i]
                sl = i % 2
                DMA(ring[:, sl, :], wscr[pi, :, :], [("wscr", pi * 2, pi * 2 + 2)], [R_ring(sl)], "ring%d" % sl)
                state["loaded"] += 1

        def next_piece(expect):
            i = state["cur"]
            assert sched[i] == expect, (sched[i], expect)
            ensure_loaded(i + 1)
            state["cur"] += 1
            sl = i % 2
            return ring[:, sl, :].rearrange("p (c n) -> p c n", n=512), R_ring(sl)

        bigbank = {"i": 0}

        def next_bank():
            b = bigbank["i"] % 4
            bigbank["i"] += 1
            return b

        def rstd_from_ms(ms_ap, out_ap, scale, rd, wr):
            ACT(out_ap, ms_ap, AF.Ln, rd, wr, bias=EPS, scale=scale)
            ACT(out_ap, out_ap, AF.Exp, wr, wr, scale=-0.5)

        def prenorm_transpose(gcol0, dstT, R_dst):
            for s in range(4):
                ms = tiny[:, s:s + 1]
                rs = tiny[:, 4 + s:5 + s]
                ACT(junkA, xbuf[:, s, :], AF.Square, [R_x(s)], [R_junkA, R_tiny(s, s + 1)],
                    scale=float(D) ** -0.5, accum_out=ms)
                rstd_from_ms(ms, rs, 1.0, [R_tiny(s, s + 1)], [R_tiny(4 + s, 5 + s)])
                xn, R_xn = av(49152 + (s % 2) * 4096, 2048, BF16, 2), AR(49152 + (s % 2) * 4096, 4096)
                DVE_ts(xn, xbuf[:, s, :], rs, None, ALU.mult, None, [R_x(s), R_tiny(4 + s, 5 + s)], [R_xn])
                for half in range(2):
                    b = next_bank()
                    pb = banks[b][:].bitcast(BF16)
                    for j in range(8):
                        c = half * 8 + j
                        TR(pb[:, j * 128:(j + 1) * 128], xn[:, c * 128:(c + 1) * 128], identb[:],
                           [R_xn, FULL("identb", 256)], [R_bank(b)], inc=(j == 7))
                    gain = par[:, gcol0 + half * 8:gcol0 + half * 8 + 8].unsqueeze(2).broadcast_to([128, 8, 128])
                    DVE_tt(dstT[:, half * 8:half * 8 + 8, s * 128:(s + 1) * 128],
                           pb.rearrange("p (c t) -> p c t", t=128), gain, ALU.mult,
                           [R_bank(b), Rp], [R_dst(half * 8, half * 8 + 8)])

        dec3 = dec_t[:, :].rearrange("p (h c) -> p h c", c=8)

        def R_dec(h):
            return ("dec_t", h * 32, h * 32 + 32)

        def R_S(h):
            return ("S", h * 512, (h + 1) * 512)

        def R_Sb(h):
            return ("Sb", h * 256, (h + 1) * 256)

        R_ident = FULL("identb", 256)
        WIN = [2, 2, 4, 4, 8, 8, 16, 16]

        for ti in range(NT):
            pos = ti % TPS
            first = (pos == 0)
            tok0 = ti * TT
            for s in range(4):
                DMA(xbuf[:, s, :], x_d[tok0 + s * 128:tok0 + (s + 1) * 128, :], (), [R_x(s)], "x%d" % s)
            if first:
                ENG_memset("pool", S[:], 0.0, [FULL("S", 4096)])
                ENG_memset("pool", Sb[:], 0.0, [FULL("Sb", 2048)])
                ENG_memset("pool", halo[:], 0.0, [FULL("halo", 512)])
            def _early_store():
                for s in range(4):
                    DMA(out_d[tok0 + s * 128:tok0 + (s + 1) * 128, :], xbuf[:, s, :], [R_x(s)],
                        [("out", tok0 + s * 128, tok0 + (s + 1) * 128)], "st%d" % s)
            if STAGE == 'pro':
                _early_store()
                break
            prenorm_transpose(40, hT, R_hT)
            if STAGE == 'A':
                _early_store()
                break

            from collections import deque
            q_main = deque()
            q_pe = deque()
            slabctr = {"n": 0}

            def drain(n):
                for _ in range(n):
                    if not q_main:
                        break
                    q_main.popleft()()
                while q_pe and q_pe[0][0] <= slabctr["n"]:
                    q_pe.popleft()[1]()

            def drain_all():
                while q_main or q_pe:
                    while q_main:
                        q_main.popleft()()
                    while q_pe:
                        q_pe.popleft()[1]()

            def proj_ws(piece_idx, evac, ndrain=5):
                wv, wr = next_piece(piece_idx)
                for sl in range(4):
                    b = next_bank()
                    for c in range(16):
                        MM(banks[b][:, :], wv[:, c, sl * 128:(sl + 1) * 128], hT[:, c, :], c == 0, c == 15,
                           [wr, R_hT(c, c + 1)], [R_bank(b)], inc=(c == 15))
                    evac(sl, b)
                    slabctr["n"] += 1
                    drain(ndrain)

            def ef_v(h): return av(h * HB + 1024, 512, F32, 4), AR(h * HB + 1024, 2048)
            def eg_v(h): return av(h * HB + 3072, 512, F32, 4), AR(h * HB + 3072, 2048)

            def proj_q(G):
                hs = [4 * G + i for i in range(4)]

                def evac_q(sl, b):
                    qh, Rq = qh_v(hs[sl])
                    ACT(qh, banks[b][:, :], AF.Copy, [R_bank(b)], [Rq])
                proj_ws(2 + G, evac_q)

            def proj_f(G):
                hs = [4 * G + i for i in range(4)]

                def evac_f(sl, b):
                    h = hs[sl]
                    ef, Ref = ef_v(h)
                    ACT(ef, banks[b][:, :], AF.Exp, [R_bank(b)], [Ref], scale=-1.0)
                    T1, R1 = T_v(0); T2, R2 = T_v(1); T3, R3 = T_v(2); T4, R4 = T_v(3); T5, R5 = T_v(4)
                    qh, Rq = qh_v(h)
                    kt, Rkt = kt_v(h)
                    khT, RkhT = av(40960 + (h % 4) * 2048, 512, BF16, 2), AR(40960 + (h % 4) * 2048, 1024)
                    kh, Rkh = khtok_v(h)
                    ch = [
                        lambda: DVE_ts(T1, ef, 1.0, None, ALU.add, None, [Ref], [R1]),
                        lambda: DVE_recip(T2, T1, [R1], [R2]),
                        lambda: ACT(T3, T2, AF.Ln, [R2, Rp2], [R3], scale=par2[:, 8 + h:9 + h], bias=par2[:, h:h + 1]),
                        lambda: DVE_ts(T4, T2, par2[:, 16 + h:17 + h], par2[:, 8 + h:9 + h], ALU.mult, ALU.add, [R2, Rp2], [R4]),
                        lambda: DVE_scan(T1, smask[:], T3, [R3, FULL("smask", 2048)], [R1]),
                        lambda: ACT(T3, T1, AF.Exp, [R1], [R3]),
                        lambda: ACT(T5, T1, AF.Exp, [R1], [R5], scale=-1.0),
                        lambda: DVE_copy(dec3[:, h, :], T3.rearrange("p (c t) -> p c t", t=64)[:, :, 63], [R3], [R_dec(h)]),
                        lambda: DVE_tt(qh, qh, T3, ALU.mult, [Rq, R3], [Rq]),
                        lambda: DVE_tt(T2, T4, T5, ALU.mult, [R4, R5], [R2]),
                        lambda: POOL_copy(kt, T2, [R2], [Rkt]),
                        lambda: DVE_tt(khT.rearrange("p (c t) -> p c t", t=64), T2.rearrange("p (c t) -> p c t", t=64),
                                       dec3[:, h, :].unsqueeze(2).broadcast_to([128, 8, 64]), ALU.mult, [R2, R_dec(h)], [RkhT]),
                    ]

                    def pe_part():
                        pb = banks[4][:].bitcast(BF16)
                        for p in range(4):
                            TR(pb[:, p * 128:(p + 1) * 128], khT[:, p * 128:(p + 1) * 128], identb[:],
                               [RkhT, R_ident], [R_bank(4)], inc=(p == 3))
                        ACT(kh, pb[:, 0:512].rearrange("p (s d) -> p s d", d=128), AF.Copy, [R_bank(4)], [Rkh])

                    def last():
                        q_pe.append((slabctr["n"] + 3, pe_part))
                    q_main.extend(ch)
                    q_main.append(last)
                proj_ws(4 + G, evac_f)

            def proj_g(G):
                hs = [4 * G + i for i in range(4)]

                def evac_g(sl, b):
                    h = hs[sl]
                    eg, Reg = eg_v(h)
                    T6, R6 = T_v(5)
                    rg, Rrg = rg_v(h)
                    ACT(eg, banks[b][:, :], AF.Exp, [R_bank(b)], [Reg], scale=-1.0)
                    q_main.appendleft(lambda: DVE_recip(rg, T6, [R6], [Rrg]))
                    q_main.appendleft(lambda: DVE_ts(T6, eg, 1.0, None, ALU.add, None, [Reg], [R6]))
                proj_ws(8 + G, evac_g)

            def proj_v(G):
                wv, wr = next_piece(6 + G)
                off0 = (4 * G) * HB + 3072
                for s in range(4):
                    b = next_bank()
                    for c in range(16):
                        MM(banks[b][:, :], hT[:, c, s * 128:(s + 1) * 128], wv[:, c, :], c == 0, c == 15,
                           [wr, R_hT(c, c + 1)], [R_bank(b)], inc=(c == 15))
                    off = off0 + s * 256
                    vout = arena[:, off:off + 4 * HB].bitcast(BF16).rearrange("p (h n) -> p h n", n=HB // 2)[:, :, 0:128]
                    ACT(vout, banks[b][:, :].rearrange("p (h d) -> p h d", d=128), AF.Copy, [R_bank(b)],
                        [v_v(h)[1] for h in range(4 * G, 4 * G + 4)])
                    slabctr["n"] += 1
                    drain(5)

            ub2_v = (av(ARENA - 2112, 528, F32, 4), AR(ARENA - 2112, 2112))

            def pool_proj(piece):
                def evac_u(sl, b):
                    ch = piece * 4 + sl
                    w = WIN[ch]
                    ub, Rub = ubuf_v if ch % 2 == 0 else ub2_v
                    sA, RsA = sA_v
                    sB, RsB = sB_v
                    Rh = ("halo", ch * 64, ch * 64 + 64)
                    ACT(ub[:, 16:528], banks[b][:, :], AF.Copy, [R_bank(b)], [Rub])
                    ops = [lambda: POOL_copy(ub[:, 0:16], halo[:, ch, :], [Rh], [Rub]),
                           lambda: POOL_tt(sA[:, 1:528], ub[:, 1:528], ub[:, 0:527], ALU.add, [Rub], [RsA])]
                    fin, Rfin = sA, RsA
                    if w >= 4:
                        ops.append(lambda: POOL_tt(sB[:, 3:528], sA[:, 3:528], sA[:, 1:526], ALU.add, [RsA], [RsB]))
                        fin, Rfin = sB, RsB
                    if w >= 8:
                        ops.append(lambda: POOL_tt(sA[:, 7:528], sB[:, 7:528], sB[:, 3:524], ALU.add, [RsB], [RsA]))
                        fin, Rfin = sA, RsA
                    if w >= 16:
                        ops.append(lambda: POOL_tt(sB[:, 15:528], sA[:, 15:528], sA[:, 7:520], ALU.add, [RsA], [RsB]))
                        fin, Rfin = sB, RsB
                    ops.append(lambda: POOL_copy(halo[:, ch, :], ub[:, 512:528], [Rub], [Rh]))
                    dT, RdT = dT_v(ch)
                    ops.append(lambda: DVE_stt(dT, fin[:, 16:528], 1.0 / w, ub[:, 16:528], ALU.mult, ALU.subtract, [Rfin, Rub], [RdT]))
                    if first:
                        widx = ch // 2
                        t16 = tiny[:, 32:48]
                        ops.append(lambda: POOL_tt(t16, fin[:, 16:32], invc[:, widx * 16:(widx + 1) * 16], ALU.mult,
                                                   [Rfin, FULL("invc", 256)], [R_tiny(32, 48)]))
                        ops.append(lambda: POOL_tt(dT[:, 0:16], t16, ub[:, 16:32], ALU.subtract, [R_tiny(32, 48), Rub], [RdT]))
                    for o in reversed(ops):
                        q_main.appendleft(o)
                proj_ws(piece, evac_u, ndrain=10)

            def pool_mm():
                for j in range(4):
                    for oc in range(2):
                        b = next_bank()
                        for kc in range(2):
                            dT, RdT = dT_v(2 * j + kc)
                            MM(banks[b][:, :], poolw[:, j * 2 + kc, oc * 128:(oc + 1) * 128], dT, kc == 0, kc == 1,
                               [FULL("poolw", 4096), RdT], [R_bank(b)], inc=(kc == 1))
                        ch = 2 * j + oc
                        ACT(yT[:, ch, :], banks[b][:, :], AF.Identity, [R_bank(b), Rp, Rp2], [R_yT(ch, ch + 1)],
                            scale=par[:, 32 + ch:33 + ch], bias=par2[:, 24 + ch:25 + ch])

            def Am8_v(h): return av(61440 + h * 256, 128, BF16, 2), AR(61440 + h * 256, 256)
            A_BANK = [4, 5]
            U_BANK = [6, 0, 2, 3]
            O_BANK = [7, 1]

            def hgrn_chunks_all():
                HS = list(range(8))
                for p in range(4):
                    for h in HS:
                        kt, Rkt = kt_v(h)
                        qh, Rq = qh_v(h)
                        ab = A_BANK[h // 4]
                        MM(banks[ab][:, (h % 4) * 128:(h % 4 + 1) * 128], kt[:, p * 128:(p + 1) * 128], qh[:, p * 128:(p + 1) * 128],
                           True, True, [Rkt, Rq], [R_bank(ab)], inc=(h % 4 == 3))
                    for h in HS:
                        ab = A_BANK[h // 4]
                        Am, RAm = Am8_v(h)
                        DVE_tt(Am, banks[ab][:, (h % 4) * 128:(h % 4 + 1) * 128], maskA[:], ALU.mult,
                               [R_bank(ab), FULL("maskA", 512)], [RAm])
                    for half in range(2):
                        c = 2 * p + half
                        for h in HS:
                            qh, Rq = qh_v(h)
                            vv, Rv = v_v(h)
                            kh, Rkh = khtok_v(h)
                            Am, RAm = Am8_v(h)
                            ob = O_BANK[h // 4]
                            o_lo = (h % 4) * 128 + half * 64
                            o_ap = banks[ob][:, o_lo:o_lo + 64]
                            MM(o_ap, Sb[:, h, :], qh[:, c * 64:(c + 1) * 64], True, False, [R_Sb(h), Rq], [R_bank(ob)], inc=False)
                            MM(o_ap, vv[:, p, :], Am[:, half * 64:(half + 1) * 64], False, True, [Rv, RAm], [R_bank(ob)], inc=True)
                            ub_ = U_BANK[h % 4]
                            uq = h // 4
                            MM(banks[ub_][:, uq * 128:(uq + 1) * 128], kh[64 * half:64 * half + 64, p, :],
                               vv[64 * half:64 * half + 64, p, :], True, True, [Rkh, Rv], [R_bank(ub_)], inc=True)
                        for h in HS:
                            ub_ = U_BANK[h % 4]
                            uq = h // 4
                            DVE_stt(S[:, h, :], S[:, h, :], dec3[:, h, c:c + 1], banks[ub_][:, uq * 128:(uq + 1) * 128],
                                    ALU.mult, ALU.add, [R_S(h), R_dec(h), R_bank(ub_)], [R_S(h)])
                        for h in HS:
                            ACT(Sb[:, h, :], S[:, h, :], AF.Copy, [R_S(h)], [R_Sb(h)])
                        if half == 1:
                            for h in HS:
                                ob = O_BANK[h // 4]
                                og, Rog = og_v(h)
                                rg, Rrg = rg_v(h)
                                DVE_tt(og[:, p * 128:(p + 1) * 128], banks[ob][:, (h % 4) * 128:(h % 4 + 1) * 128],
                                       rg[:, p * 128:(p + 1) * 128], ALU.mult, [R_bank(ob), Rrg], [Rog])
                tmp2 = [(T_v(4), T_v(5), 7), (ubuf_v, sA_v, 6)]
                for h in HS:
                    (og2, Rog2), (rs, Rrs), sbk = tmp2[h % 2]
                    og2 = og2[:, 0:512]
                    rs = rs[:, 0:512]
                    og, Rog = og_v(h)
                    POOL_tt(og2, og, og, ALU.mult, [Rog], [Rog2])
                    MM(banks[sbk][:, :], ones[:], og2, True, True, [FULL("ones", 512), Rog2], [R_bank(sbk)], inc=True)
                    rstd_from_ms(banks[sbk][:, :], rs, 1.0 / 128.0, [R_bank(sbk)], [Rrs])
                    DVE_stt(yT[:, 8 + h, :], og, par[:, 16 + h:17 + h], rs, ALU.mult, ALU.mult, [Rog, Rrs, Rp], [R_yT(8 + h, 9 + h)])

            proj_q(0)
            proj_f(0)
            pool_proj(0)
            proj_q(1)
            proj_f(1)
            pool_proj(1)
            proj_g(0)
            proj_g(1)
            proj_v(0)
            proj_v(1)
            drain_all()
            pool_mm()
            hgrn_chunks_all()
            if STAGE == 'mixer' and ti == NT - 1:
                DMA(dbg_d, hy[:, 8192:16384], [R_yT(0, 16)], [("dbg", 0, 1)], "dbg")

            def post_norm_residual(src_ap_fn, R_src_fn, gi, store):
                for s in range(4):
                    ms = tiny[:, 24 + s:25 + s]
                    rs = tiny[:, 28 + s:29 + s]
                    DVE_rsum(ms, tiny[:, 8 + s * 4:12 + s * 4], [R_tiny(8 + s * 4, 12 + s * 4)], [R_tiny(24 + s, 25 + s)])
                    rstd_from_ms(ms, rs, 1.0, [R_tiny(24 + s, 25 + s)], [R_tiny(28 + s, 29 + s)])
                    src = src_ap_fn(s)
                    DVE_stt(src, src, rs, gpost[:, gi, :], ALU.mult, ALU.mult,
                            [R_src_fn(s), R_tiny(28 + s, 29 + s), ("gpost", gi * 8192, (gi + 1) * 8192)], [R_src_fn(s)])
                    POOL_tt(xbuf[:, s, :], xbuf[:, s, :], src, ALU.add, [R_x(s), R_src_fn(s)], [R_x(s)])
                    if store:
                        DMA(out_d[tok0 + s * 128:tok0 + (s + 1) * 128, :], xbuf[:, s, :], [R_x(s)],
                            [("out", tok0 + s * 128, tok0 + (s + 1) * 128)], "st%d" % s)

            def evac_tokmajor(b, dst_ap, R_dst, s, cg, junk, R_junk):
                ACT(dst_ap, banks[b][:, :], AF.Copy, [R_bank(b)], [R_dst])
                ACT(junk[:, 0:512], banks[b][:, :], AF.Square, [R_bank(b)], [R_junk, R_tiny(8 + s * 4 + cg, 9 + s * 4 + cg)],
                    scale=float(D) ** -0.5, accum_out=tiny[:, 8 + s * 4 + cg:9 + s * 4 + cg])

            for cg in range(4):
                wv, wr = next_piece(PIECE_WOUT + cg)
                for s in range(4):
                    b = next_bank()
                    for c in range(16):
                        MM(banks[b][:, :], yT[:, c, s * 128:(s + 1) * 128], wv[:, c, :], c == 0, c == 15,
                           [wr, R_yT(c, c + 1)], [R_bank(b)], inc=(c == 15))
                    evac_tokmajor(b, mix_all[:, s, cg * 512:(cg + 1) * 512], R_mix(s), s, cg, junkA, R_junkA)
            post_norm_residual(lambda s: mix_all[:, s, :], R_mix, 0, STAGE == 'mixer')
            if STAGE == 'mixer':
                for _p in range(32):
                    state['cur'] += 1
                continue

            prenorm_transpose(56, hT, R_hT)
            for p in range(16):
                wv, wr = next_piece(PIECE_MI + p)
                for sl in range(4):
                    b = next_bank()
                    for c in range(16):
                        MM(banks[b][:, :], wv[:, c, sl * 128:(sl + 1) * 128], hT[:, c, :], c == 0, c == 15,
                           [wr, R_hT(c, c + 1)], [R_bank(b)], inc=(c == 15))
                    k = (p * 4 + sl) % 2
                    rt, Rrt = av(65600 + k * 2112, 512, F32, 4), AR(65600 + k * 2112, 2048)
                    ACT(rt, banks[b][:, :], AF.Relu, [R_bank(b)], [Rrt])
                    hc = p * 4 + sl
                    POOL_tt(hid[:, hc, :], rt, rt, ALU.mult, [Rrt], [R_hid(hc, hc + 1)])
            for cg in range(4):
                for jp in range(4):
                    wv, wr = next_piece(PIECE_MO + cg * 4 + jp)
                    for s in range(4):
                        for c in range(16):
                            hc = jp * 16 + c
                            MM(banks[s][:, :], hid[:, hc, s * 128:(s + 1) * 128], wv[:, c, :],
                               (jp == 0 and c == 0), (jp == 3 and c == 15),
                               [wr, R_hid(hc, hc + 1)], [R_bank(s)], inc=(c == 15))
                for s in range(4):
                    evac_tokmajor(s, ffv[:, s, cg * 512:(cg + 1) * 512], R_ff(s), s, cg, junkB, R_junkB)
            post_norm_residual(lambda s: ffv[:, s, :], R_ff, 1, True)

        keys = set()
        for e in ENGS:
            for waits, fn, incinfo in P.streams[e]:
                for (k, v) in waits:
                    keys.add(k)
                if incinfo:
                    keys.add(incinfo[0])
        for k in sorted(keys):
            getsem(k)
        final_waits = [(k, v) for k, v in P.dmacnt.items() if k.startswith("dma:st") or k.startswith("dma:dbg")]

        with nc.allow_low_precision(reason="bf16 matmul operands, fp32 accumulation"), nc.Block() as block:
            def runner(name, extra=()):
                stream = P.streams[name]

                def f(e):
                    for waits, fn, incinfo in stream:
                        for (k, v) in waits:
                            e.wait_ge(getsem(k), v)
                        ins = fn(e)
                        if incinfo:
                            ins.then_inc(getsem(incinfo[0]), incinfo[1])
                    for (k, v) in extra:
                        e.wait_ge(getsem(k), v)
                return f
            block.sync(runner("sp", final_waits))
            block.tensor(runner("pe"))
            block.scalar(runner("act"))
            block.vector(runner("dve"))
            block.gpsimd(runner("pool"))
    return nc, P


_CACHE = {}


def _consts():
    identb = np.eye(128, dtype=np.float32).astype(ml_dtypes.bfloat16)
    identf = np.eye(128, dtype=np.float32)
    s = np.arange(128)[:, None]
    t = np.arange(128)[None, :]
    maskA = ((s // 64 == t // 64) & (s <= t)).astype(np.float32)
    invc = np.zeros((128, 64), np.float32)
    for wi, w in enumerate((2, 4, 8, 16)):
        invc[:, wi * 16:(wi + 1) * 16] = 1.0 / np.minimum(np.arange(16) + 1, w).astype(np.float32)
    return {"c_identb": identb, "c_identf": identf, "c_maskA": maskA, "c_invc": invc}


def _run(inputs, n_cores, nseq, seq, stage=None):
    key = (nseq, seq, stage)
    if key not in _CACHE:
        _CACHE[key] = build_program(nseq, seq, stage)[0]
    nc = _CACHE[key]
    f32 = lambda a: np.ascontiguousarray(np.asarray(a, dtype=np.float32))
    x = f32(inputs["x"])
    shared = {
        "w_in": f32(inputs["w_in"]).reshape(D, INW),
        "pool_w": f32(inputs["pool_w"]).reshape(4, 256, 256),
        "pool_b": f32(inputs["pool_b"]).reshape(8, 128),
        "pool_scale": f32(inputs["pool_scale"]).reshape(8, 128),
        "lb_logits": f32(inputs["lb_logits"]).reshape(16, 128),
        "hgrn_norm_w": f32(inputs["hgrn_norm_w"]).reshape(8, 128),
        "w_out": f32(inputs["w_out"]).reshape(D, D),
        "norm_mix_pre": f32(inputs["norm_mix_pre"]).reshape(16, 128),
        "norm_mix_post": f32(inputs["norm_mix_post"]).reshape(1, D),
        "norm_mlp_pre": f32(inputs["norm_mlp_pre"]).reshape(16, 128),
        "norm_mlp_post": f32(inputs["norm_mlp_post"]).reshape(1, D),
        "w_mlp_in": f32(inputs["w_mlp_in"]).reshape(D, DFF),
        "w_mlp_out": f32(inputs["w_mlp_out"]).reshape(DFF, D),
    }
    shared.update(_consts())
    in_maps = []
    for c in range(n_cores):
        m = dict(shared)
        m["x"] = np.ascontiguousarray(x[c * nseq:(c + 1) * nseq].reshape(nseq * seq, D))
        in_maps.append(m)
    res = run_bass_kernel_spmd(nc, in_maps, core_ids=list(range(n_cores)))
    if stage == 'mixer':
        global DBG
        DBG = [np.asarray(r["dbg"]) for r in res.results]
    outs = [np.asarray(r["out"], dtype=np.float32).reshape(nseq, seq, D) for r in res.results]
    return np.concatenate(outs, axis=0)


def kernel(**inputs):
    x = inputs["x"]
    B, S_, _ = x.shape
    return _run(inputs, N_CORES, B // N_CORES, S_)
```

```python
import contextlib
import numpy as np
import ml_dtypes
import concourse.bass as bass
import concourse.mybir as mybir
from concourse.bass_utils import run_bass_kernel_spmd

F32 = mybir.dt.float32
BF16 = mybir.dt.bfloat16
U8 = mybir.dt.uint8
AF = mybir.ActivationFunctionType
ALU = mybir.AluOpType

D = 2048
DFF = 8192
INW = 5120
TT = 512
EPS = 1e-6
N_CORES = 8
ENGS = ("pe", "act", "dve", "pool", "sp")


class Prog:
    def __init__(self):
        self.streams = {e: [] for e in ENGS}
        self.cnt = {e: 0 for e in ENGS}
        self.waited = {e: {} for e in ENGS}
        self.regions = {}
        self.dmacnt = {}

    def _gather(self, deps, key, lo, hi, is_write):
        lst = self.regions.setdefault(key, [])
        for ent in lst:
            if ent[0] < hi and lo < ent[1]:
                w = ent[2]
                if w is not None:
                    if deps.get(w[0], 0) < w[1]:
                        deps[w[0]] = w[1]
                if is_write:
                    for k, v in ent[3].items():
                        if deps.get(k, 0) < v:
                            deps[k] = v

    def _record(self, key, lo, hi, tok, is_write):
        lst = self.regions.setdefault(key, [])
        if is_write:
            lst[:] = [e for e in lst if not (lo <= e[0] and e[1] <= hi)]
            lst.append([lo, hi, tok, {}])
        else:
            for ent in lst:
                if ent[0] == lo and ent[1] == hi:
                    if ent[3].get(tok[0], 0) < tok[1]:
                        ent[3][tok[0]] = tok[1]
                    return
            lst.append([lo, hi, None, {tok[0]: tok[1]}])

    def emit(self, eng, fn, reads=(), writes=(), inc=True, dma_slot=None):
        deps = {}
        for (k, lo, hi) in reads:
            self._gather(deps, k, lo, hi, k.startswith("ps"))
        for (k, lo, hi) in writes:
            self._gather(deps, k, lo, hi, True)
        if eng == "pe":
            deps.pop("pe", None)
        waits = []
        wd = self.waited[eng]
        for k, v in deps.items():
            if wd.get(k, 0) < v:
                wd[k] = v
                waits.append((k, v))
        if dma_slot is not None:
            sk = "dma:" + dma_slot
            self.dmacnt[sk] = self.dmacnt.get(sk, 0) + 16
            tok = (sk, self.dmacnt[sk])
            incinfo = (sk, 16)
        else:
            if inc:
                self.cnt[eng] += 1
                tok = (eng, self.cnt[eng])
                incinfo = (eng, 1)
            else:
                tok = (eng, self.cnt[eng] + 1)
                incinfo = None
        for (k, lo, hi) in reads:
            self._record(k, lo, hi, tok, False)
        for (k, lo, hi) in writes:
            self._record(k, lo, hi, tok, True)
        self.streams[eng].append((waits, fn, incinfo))
        return tok


def build_program(NSEQ, SEQ, STAGE=None):
    NTOK = NSEQ * SEQ
    TPS = SEQ // TT
    NT = NSEQ * TPS
    nc = bass.Bass("TRN2", target_bir_lowering=False)
    P = Prog()

    def din(name, shape, dt=F32):
        return nc.dram_tensor(name, list(shape), dt, kind="ExternalInput").ap()

    x_d = din("x", [NTOK, D])
    w_in_d = din("w_in", [D, INW])
    pool_w_d = din("pool_w", [4, 256, 256])
    pool_b_d = din("pool_b", [8, 128])
    pool_s_d = din("pool_scale", [8, 128])
    lbl_d = din("lb_logits", [16, 128])
    nw_d = din("hgrn_norm_w", [8, 128])
    w_out_d = din("w_out", [D, D])
    g_pre1_d = din("norm_mix_pre", [16, 128])
    g_post1_d = din("norm_mix_post", [1, D])
    g_pre2_d = din("norm_mlp_pre", [16, 128])
    g_post2_d = din("norm_mlp_post", [1, D])
    w_mi_d = din("w_mlp_in", [D, DFF])
    w_mo_d = din("w_mlp_out", [DFF, D])
    identb_d = din("c_identb", [128, 128], BF16)
    identf_d = din("c_identf", [128, 128])
    maskA_d = din("c_maskA", [128, 128])
    invc_d = din("c_invc", [128, 64])
    out_d = nc.dram_tensor("out", [NTOK, D], F32, kind="ExternalOutput").ap()
    dbg_d = nc.dram_tensor("dbg", [128, 8192], BF16, kind="ExternalOutput").ap() if STAGE == 'mixer' else None
    NPIECE = 10 + 4 + 16 + 16
    wscr = nc.dram_tensor("wscr", [NPIECE, 128, 16 * 512], BF16).ap()

    es = contextlib.ExitStack()
    with es:
        def sb(name, shape, dt):
            return es.enter_context(nc.sbuf_tensor(name, list(shape), dt))

        identb = sb("identb", [128, 128], BF16)
        identf = sb("identf", [128, 128], F32)
        maskA = sb("maskA", [128, 128], F32)
        ones = sb("ones", [128, 128], F32)
        smask = sb("smask", [128, 512], F32)
        invc = sb("invc", [128, 64], F32)
        prow = sb("prow", [72, 128], F32)
        par = sb("par", [128, 72], F32)
        par2 = sb("par2", [128, 40], F32)
        gpost = sb("gpost", [128, 2, D], F32)
        poolw = sb("poolw", [128, 8, 256], BF16)
        S = sb("S", [128, 8, 128], F32)
        Sb = sb("Sb", [128, 8, 128], BF16)
        halo = sb("halo", [128, 8, 16], F32)
        xbuf = sb("xbuf", [128, 4, D], F32)
        hy = sb("hy", [128, 2 * 16 * 512], BF16)
        ring = sb("ring", [128, 2, 16 * 512], BF16)
        tiny = sb("tiny", [128, 64], F32)
        dec_t = sb("dec_t", [128, 64], F32)
        ARENA = 78016 + 2112
        arena = sb("arena", [128, ARENA], U8)
        banks = [es.enter_context(nc.psum_tensor("bank%d" % b, [128, 512], F32)) for b in range(8)]

        sem_names = {}

        def getsem(key):
            if key not in sem_names:
                sem_names[key] = es.enter_context(nc.semaphore("s_" + key.replace(":", "_")))
            return sem_names[key]

        def av(off, n_elem, dt, esz):
            return arena[:, off:off + n_elem * esz].bitcast(dt)

        def AR(off, nbytes):
            return ("arena", off, off + nbytes)

        hT = hy[:, 0:8192].rearrange("p (c t) -> p c t", t=512)
        yT = hy[:, 8192:16384].rearrange("p (c t) -> p c t", t=512)
        ffv = hy[:].bitcast(F32).rearrange("p (s n) -> p s n", n=D)

        def R_hT(c0=0, c1=16):
            return ("hy", c0 * 1024, c1 * 1024)

        def R_yT(c0=0, c1=16):
            return ("hy", 16384 + c0 * 1024, 16384 + c1 * 1024)

        def R_ff(s):
            return ("hy", s * 8192, (s + 1) * 8192)

        def R_x(s):
            return ("xbuf", s * 8192, (s + 1) * 8192)

        def R_bank(b, lo=0, hi=2048):
            return ("ps%d" % b, 0, 2048)

        def R_ring(s):
            return ("ring", s * 16384, (s + 1) * 16384)

        def R_tiny(c0, c1):
            return ("tiny", c0 * 4, c1 * 4)

        HB = 5120
        def qh_v(h): return av(h * HB, 512, BF16, 2), AR(h * HB, 1024)
        def kt_v(h): return av(h * HB + 1024, 512, BF16, 2), AR(h * HB + 1024, 1024)
        def khtok_v(h): return av(h * HB + 2048, 512, BF16, 2).rearrange("p (s d) -> p s d", d=128), AR(h * HB + 2048, 1024)
        def v_v(h): return av(h * HB + 3072, 512, BF16, 2).rearrange("p (s d) -> p s d", d=128), AR(h * HB + 3072, 1024)
        def rg_v(h): return av(h * HB + 4096, 512, BF16, 2), AR(h * HB + 4096, 1024)
        def og_v(i): return av(40960 + i * 2048, 512, F32, 4), AR(40960 + i * 2048, 2048)
        def T_v(i): return av(49152 + i * 2048, 512, F32, 4), AR(49152 + i * 2048, 2048)
        khT_v = (av(61440, 512, BF16, 2), AR(61440, 1024))
        og2_v = T_v(0)
        rstd_v = T_v(1)
        def Am_v(i): return av(62464 + i * 256, 128, BF16, 2), AR(62464 + i * 256, 256)
        ubuf_v = (av(63488, 528, F32, 4), AR(63488, 2112))
        sA_v = (av(65600, 528, F32, 4), AR(65600, 2112))
        sB_v = (av(67712, 528, F32, 4), AR(67712, 2112))
        def dT_v(c): return av(69824 + c * 1024, 512, BF16, 2), AR(69824 + c * 1024, 1024)
        mix_all = av(0, 4 * D, F32, 4).rearrange("p (s n) -> p s n", n=D)
        def R_mix(s): return AR(s * 8192, 8192)
        hid = av(0, 64 * 512, BF16, 2).rearrange("p (c t) -> p c t", t=512)
        def R_hid(c0, c1): return AR(c0 * 1024, (c1 - c0) * 1024)
        junkA = av(57344, 2048, BF16, 2)
        R_junkA = AR(57344, 4096)
        junkB = av(69824, 512, BF16, 2)
        R_junkB = AR(69824, 1024)

        def ACT(out, in_, func, reads, writes, **kw):
            P.emit("act", lambda e: e.activation(out=out, in_=in_, func=func, **kw), reads, writes)

        def DVE_tt(out, in0, in1, op, reads, writes):
            P.emit("dve", lambda e: e.tensor_tensor(out=out, in0=in0, in1=in1, op=op), reads, writes)

        def DVE_ts(out, in0, s1, s2, op0, op1, reads, writes):
            if op1 is None:
                P.emit("dve", lambda e: e.tensor_scalar(out=out, in0=in0, scalar1=s1, scalar2=None, op0=op0), reads, writes)
            else:
                P.emit("dve", lambda e: e.tensor_scalar(out=out, in0=in0, scalar1=s1, scalar2=s2, op0=op0, op1=op1), reads, writes)

        def DVE_stt(out, in0, scalar, in1, op0, op1, reads, writes):
            P.emit("dve", lambda e: e.scalar_tensor_tensor(out=out, in0=in0, scalar=scalar, in1=in1, op0=op0, op1=op1), reads, writes)

        def DVE_copy(out, in_, reads, writes):
            P.emit("dve", lambda e: e.tensor_copy(out=out, in_=in_), reads, writes)

        def DVE_recip(out, in_, reads, writes):
            P.emit("dve", lambda e: e.reciprocal(out=out, in_=in_), reads, writes)

        def DVE_scan(out, d0, d1, reads, writes):
            P.emit("dve", lambda e: e.tensor_tensor_scan(out=out, data0=d0, data1=d1, initial=0.0, op0=ALU.mult, op1=ALU.add), reads, writes)

        def DVE_rsum(out, in_, reads, writes):
            P.emit("dve", lambda e: e.reduce_sum(out=out, in_=in_, axis=mybir.AxisListType.X), reads, writes)

        def ENG_memset(eng, ap, val, writes):
            P.emit(eng, lambda e: e.memset(ap, val), (), writes)

        def POOL_tt(out, in0, in1, op, reads, writes):
            P.emit("pool", lambda e: e.tensor_tensor(out=out, in0=in0, in1=in1, op=op), reads, writes)

        def POOL_copy(out, in_, reads, writes):
            P.emit("pool", lambda e: e.tensor_copy(out=out, in_=in_), reads, writes)

        def MM(out, lhsT, rhs, start, stop, reads, writes, inc):
            P.emit("pe", lambda e: e.matmul(out, lhsT=lhsT, rhs=rhs, start=start, stop=stop), reads, writes, inc=inc)

        def TR(out, in_, ident, reads, writes, inc):
            P.emit("pe", lambda e: e.transpose(out=out, in_=in_, identity=ident), reads, writes, inc=inc)

        def DMA(out, in_, reads, writes, slot):
            P.emit("sp", lambda e: e.dma_start(out=out, in_=in_), reads, writes, dma_slot=slot)

        def FULL(name, nbytes):
            return (name, 0, nbytes)

        DMA(identb[:], identb_d, (), [FULL("identb", 256)], "c0")
        DMA(identf[:], identf_d, (), [FULL("identf", 512)], "c1")
        DMA(maskA[:], maskA_d, (), [FULL("maskA", 512)], "c2")
        DMA(invc[:], invc_d, (), [FULL("invc", 256)], "c3")
        DMA(prow[0:16, :], lbl_d, (), [("prow", 0, 1)], "c4")
        DMA(prow[16:24, :], nw_d, (), [("prow", 1, 2)], "c5")
        DMA(prow[24:32, :], pool_b_d, (), [("prow", 2, 3)], "c6")
        DMA(prow[32:40, :], pool_s_d, (), [("prow", 3, 4)], "c7")
        DMA(prow[40:56, :], g_pre1_d, (), [("prow", 4, 5)], "c8")
        DMA(prow[56:72, :], g_pre2_d, (), [("prow", 5, 6)], "c9")
        DMA(gpost[:, 0, :], g_post1_d.partition_broadcast(128), (), [("gpost", 0, 8192)], "c10")
        DMA(gpost[:, 1, :], g_post2_d.partition_broadcast(128), (), [("gpost", 8192, 16384)], "c11")
        ENG_memset("pool", ones[:], 1.0, [FULL("ones", 512)])
        ENG_memset("pool", smask[:], 1.0, [FULL("smask", 2048)])
        ENG_memset("pool", smask[:].rearrange("p (c t) -> p c t", t=64)[:, :, 0:1], 0.0, [FULL("smask", 2048)])
        TR(banks[4][0:128, 0:72], prow[:, :], identf[0:72, 0:72], [("prow", 0, 6), FULL("identf", 512)], [R_bank(4)], True)
        DVE_copy(par[:], banks[4][:, 0:72], [R_bank(4)], [FULL("par", 288)])
        Rp = FULL("par", 288)
        Rp2 = FULL("par2", 160)
        DVE_tt(par2[:, 32:40], par[:, 8:16], par[:, 0:8], ALU.subtract, [Rp], [Rp2])
        ACT(par2[:, 32:40], par2[:, 32:40], AF.Exp, [Rp2], [Rp2])
        DVE_ts(par2[:, 32:40], par2[:, 32:40], 1.0, None, ALU.add, None, [Rp2], [Rp2])
        DVE_recip(par2[:, 0:8], par2[:, 32:40], [Rp2], [Rp2])
        DVE_ts(par2[:, 8:16], par2[:, 0:8], -1.0, 1.0, ALU.mult, ALU.add, [Rp2], [Rp2])
        DVE_ts(par2[:, 16:24], par2[:, 8:16], -1.0, None, ALU.mult, None, [Rp2], [Rp2])
        DVE_tt(par2[:, 24:32], par[:, 24:32], par[:, 32:40], ALU.mult, [Rp, Rp2], [Rp2])

        def stg_v(i):
            return av(i * 16384, 4096, F32, 4).rearrange("p (c n) -> p c n", n=512), AR(i * 16384, 16384)

        def cvt_v(i):
            return ring[:, i // 2, (i % 2) * 4096:(i % 2) * 4096 + 4096].rearrange("p (c n) -> p c n", n=512), ("ring", i * 8192, (i + 1) * 8192)

        pieces_src = []
        for p in range(10):
            pieces_src.append(w_in_d[:, p * 512:(p + 1) * 512])
        for p in range(4):
            pieces_src.append(w_out_d[:, p * 512:(p + 1) * 512])
        for p in range(16):
            pieces_src.append(w_mi_d[:, p * 512:(p + 1) * 512])
        for cg in range(4):
            for jp in range(4):
                pieces_src.append(w_mo_d[jp * 2048:(jp + 1) * 2048, cg * 512:(cg + 1) * 512])
        PIECE_WOUT = 10
        PIECE_MI = 14
        PIECE_MO = 30
        cv_engs = ["act", "dve", "pool"]
        units = [(pi, half) for pi in range(len(pieces_src)) for half in range(2)]

        def unit_load(u):
            pi, half = units[u]
            sl = u % 4
            sv_ap, sv_r = stg_v(sl)
            srcv = pieces_src[pi][half * 1024:(half + 1) * 1024, :].rearrange("(c p) n -> p c n", p=128)
            DMA(sv_ap, srcv, (), [sv_r], "stg%d" % sl)

        LOOKAHEAD = 3
        for u in range(min(LOOKAHEAD, len(units))):
            unit_load(u)
        for u, (pi, half) in enumerate(units):
            if u + LOOKAHEAD < len(units):
                unit_load(u + LOOKAHEAD)
            sl = u % 4
            sv_ap, sv_r = stg_v(sl)
            cv_ap, cv_r = cvt_v(sl)
            eng = cv_engs[u % 3]
            if eng == "act":
                ACT(cv_ap, sv_ap, AF.Copy, [sv_r], [cv_r])
            elif eng == "dve":
                DVE_copy(cv_ap, sv_ap, [sv_r], [cv_r])
            else:
                POOL_copy(cv_ap, sv_ap, [sv_r], [cv_r])
            dst = wscr[pi, :, half * 4096:(half + 1) * 4096].rearrange("p (c n) -> p c n", n=512)
            DMA(dst, cv_ap, [cv_r], [("wscr", pi * 2 + half, pi * 2 + half + 1)], "cvt%d" % sl)
        pw_stage = av(65536, 2048, F32, 4).rearrange("p (a n) -> p a n", n=256)
        R_pws = AR(65536, 8192)
        DMA(pw_stage, pool_w_d.rearrange("j (kc p) n -> p (j kc) n", p=128), (), [R_pws], "c12")
        DVE_copy(poolw[:], pw_stage, [R_pws], [FULL("poolw", 4096)])

        sched = []
        tile_order = [2, 4, 0, 3, 5, 1, 8, 9, 6, 7] + list(range(PIECE_WOUT, PIECE_WOUT + 4)) + \
            list(range(PIECE_MI, PIECE_MI + 16)) + list(range(PIECE_MO, PIECE_MO + 16))
        for t in range(NT):
            sched.extend(tile_order)
        state = {"loaded": 0, "cur": 0}

        def ensure_loaded(upto):
            while state["loaded"] <= upto and state["loaded"] < len(sched):
                i = state["loaded"]
                pi = sched[i]
                sl = i % 2
                DMA(ring[:, sl, :], wscr[pi, :, :], [("wscr", pi * 2, pi * 2 + 2)], [R_ring(sl)], "ring%d" % sl)
                state["loaded"] += 1

        def next_piece(expect):
            i = state["cur"]
            assert sched[i] == expect, (sched[i], expect)
            ensure_loaded(i + 1)
            state["cur"] += 1
            sl = i % 2
            return ring[:, sl, :].rearrange("p (c n) -> p c n", n=512), R_ring(sl)

        bigbank = {"i": 0}

        def next_bank():
            b = bigbank["i"] % 4
            bigbank["i"] += 1
            return b

        def rstd_from_ms(ms_ap, out_ap, scale, rd, wr):
            ACT(out_ap, ms_ap, AF.Ln, rd, wr, bias=EPS, scale=scale)
            ACT(out_ap, out_ap, AF.Exp, wr, wr, scale=-0.5)

        def prenorm_transpose(gcol0, dstT, R_dst):
            for s in range(4):
                ms = tiny[:, s:s + 1]
                rs = tiny[:, 4 + s:5 + s]
                ACT(junkA, xbuf[:, s, :], AF.Square, [R_x(s)], [R_junkA, R_tiny(s, s + 1)],
                    scale=float(D) ** -0.5, accum_out=ms)
                rstd_from_ms(ms, rs, 1.0, [R_tiny(s, s + 1)], [R_tiny(4 + s, 5 + s)])
                xn, R_xn = av(49152 + (s % 2) * 4096, 2048, BF16, 2), AR(49152 + (s % 2) * 4096, 4096)
                DVE_ts(xn, xbuf[:, s, :], rs, None, ALU.mult, None, [R_x(s), R_tiny(4 + s, 5 + s)], [R_xn])
                for half in range(2):
                    b = next_bank()
                    pb = banks[b][:].bitcast(BF16)
                    for j in range(8):
                        c = half * 8 + j
                        TR(pb[:, j * 128:(j + 1) * 128], xn[:, c * 128:(c + 1) * 128], identb[:],
                           [R_xn, FULL("identb", 256)], [R_bank(b)], inc=(j == 7))
                    gain = par[:, gcol0 + half * 8:gcol0 + half * 8 + 8].unsqueeze(2).broadcast_to([128, 8, 128])
                    DVE_tt(dstT[:, half * 8:half * 8 + 8, s * 128:(s + 1) * 128],
                           pb.rearrange("p (c t) -> p c t", t=128), gain, ALU.mult,
                           [R_bank(b), Rp], [R_dst(half * 8, half * 8 + 8)])

        dec3 = dec_t[:, :].rearrange("p (h c) -> p h c", c=8)

        def R_dec(h):
            return ("dec_t", h * 32, h * 32 + 32)

        def R_S(h):
            return ("S", h * 512, (h + 1) * 512)

        def R_Sb(h):
            return ("Sb", h * 256, (h + 1) * 256)

        R_ident = FULL("identb", 256)
        WIN = [2, 2, 4, 4, 8, 8, 16, 16]

        for ti in range(NT):
            pos = ti % TPS
            first = (pos == 0)
            tok0 = ti * TT
            for s in range(4):
                DMA(xbuf[:, s, :], x_d[tok0 + s * 128:tok0 + (s + 1) * 128, :], (), [R_x(s)], "x%d" % s)
            if first:
                ENG_memset("pool", S[:], 0.0, [FULL("S", 4096)])
                ENG_memset("pool", Sb[:], 0.0, [FULL("Sb", 2048)])
                ENG_memset("pool", halo[:], 0.0, [FULL("halo", 512)])
            def _early_store():
                for s in range(4):
                    DMA(out_d[tok0 + s * 128:tok0 + (s + 1) * 128, :], xbuf[:, s, :], [R_x(s)],
                        [("out", tok0 + s * 128, tok0 + (s + 1) * 128)], "st%d" % s)
            if STAGE == 'pro':
                _early_store()
                break
            prenorm_transpose(40, hT, R_hT)
            if STAGE == 'A':
                _early_store()
                break

            from collections import deque
            q_main = deque()
            q_pe = deque()
            slabctr = {"n": 0}

            def drain(n):
                for _ in range(n):
                    if not q_main:
                        break
                    q_main.popleft()()
                while q_pe and q_pe[0][0] <= slabctr["n"]:
                    q_pe.popleft()[1]()

            def drain_all():
                while q_main or q_pe:
                    while q_main:
                        q_main.popleft()()
                    while q_pe:
                        q_pe.popleft()[1]()

            def proj_ws(piece_idx, evac, ndrain=5):
                wv, wr = next_piece(piece_idx)
                for sl in range(4):
                    b = next_bank()
                    for c in range(16):
                        MM(banks[b][:, :], wv[:, c, sl * 128:(sl + 1) * 128], hT[:, c, :], c == 0, c == 15,
                           [wr, R_hT(c, c + 1)], [R_bank(b)], inc=(c == 15))
                    evac(sl, b)
                    slabctr["n"] += 1
                    drain(ndrain)

            def ef_v(h): return av(h * HB + 1024, 512, F32, 4), AR(h * HB + 1024, 2048)
            def eg_v(h): return av(h * HB + 3072, 512, F32, 4), AR(h * HB + 3072, 2048)

            def proj_q(G):
                hs = [4 * G + i for i in range(4)]

                def evac_q(sl, b):
                    qh, Rq = qh_v(hs[sl])
                    ACT(qh, banks[b][:, :], AF.Copy, [R_bank(b)], [Rq])
                proj_ws(2 + G, evac_q)

            def proj_f(G):
                hs = [4 * G + i for i in range(4)]

                def evac_f(sl, b):
                    h = hs[sl]
                    ef, Ref = ef_v(h)
                    ACT(ef, banks[b][:, :], AF.Exp, [R_bank(b)], [Ref], scale=-1.0)
                    T1, R1 = T_v(0); T2, R2 = T_v(1); T3, R3 = T_v(2); T4, R4 = T_v(3); T5, R5 = T_v(4)
                    qh, Rq = qh_v(h)
                    kt, Rkt = kt_v(h)
                    khT, RkhT = av(40960 + (h % 4) * 2048, 512, BF16, 2), AR(40960 + (h % 4) * 2048, 1024)
                    kh, Rkh = khtok_v(h)
                    ch = [
                        lambda: ACT(T1, ef, AF.Ln, [Ref], [R1], bias=1.0),
                        lambda: ACT(T2, T1, AF.Exp, [R1], [R2], scale=-1.0),
                        lambda: ACT(T3, T2, AF.Ln, [R2, Rp2], [R3], scale=par2[:, 8 + h:9 + h], bias=par2[:, h:h + 1]),
                        lambda: DVE_ts(T4, T2, par2[:, 16 + h:17 + h], par2[:, 8 + h:9 + h], ALU.mult, ALU.add, [R2, Rp2], [R4]),
                        lambda: DVE_scan(T1, smask[:], T3, [R3, FULL("smask", 2048)], [R1]),
                        lambda: ACT(T3, T1, AF.Exp, [R1], [R3]),
                        lambda: ACT(T5, T1, AF.Exp, [R1], [R5], scale=-1.0),
                        lambda: DVE_copy(dec3[:, h, :], T3.rearrange("p (c t) -> p c t", t=64)[:, :, 63], [R3], [R_dec(h)]),
                        lambda: DVE_tt(qh, qh, T3, ALU.mult, [Rq, R3], [Rq]),
                        lambda: DVE_tt(T2, T4, T5, ALU.mult, [R4, R5], [R2]),
                        lambda: POOL_copy(kt, T2, [R2], [Rkt]),
                        lambda: DVE_tt(khT.rearrange("p (c t) -> p c t", t=64), T2.rearrange("p (c t) -> p c t", t=64),
                                       dec3[:, h, :].unsqueeze(2).broadcast_to([128, 8, 64]), ALU.mult, [R2, R_dec(h)], [RkhT]),
                    ]

                    def pe_part():
                        pb = banks[4][:].bitcast(BF16)
                        for p in range(4):
                            TR(pb[:, p * 128:(p + 1) * 128], khT[:, p * 128:(p + 1) * 128], identb[:],
                               [RkhT, R_ident], [R_bank(4)], inc=(p == 3))
                        ACT(kh, pb[:, 0:512].rearrange("p (s d) -> p s d", d=128), AF.Copy, [R_bank(4)], [Rkh])

                    def last():
                        q_pe.append((slabctr["n"] + 3, pe_part))
                    q_main.extend(ch)
                    q_main.append(last)
                proj_ws(4 + G, evac_f)

            def proj_g(G):
                hs = [4 * G + i for i in range(4)]

                def evac_g(sl, b):
                    h = hs[sl]
                    eg, Reg = eg_v(h)
                    T6, R6 = T_v(5)
                    rg, Rrg = rg_v(h)
                    ACT(eg, banks[b][:, :], AF.Exp, [R_bank(b)], [Reg], scale=-1.0)
                    q_main.appendleft(lambda: ACT(rg, T6, AF.Exp, [R6], [Rrg], scale=-1.0))
                    q_main.appendleft(lambda: ACT(T6, eg, AF.Ln, [Reg], [R6], bias=1.0))
                proj_ws(8 + G, evac_g)

            def proj_v(G):
                wv, wr = next_piece(6 + G)
                off0 = (4 * G) * HB + 3072
                for s in range(4):
                    b = next_bank()
                    for c in range(16):
                        MM(banks[b][:, :], hT[:, c, s * 128:(s + 1) * 128], wv[:, c, :], c == 0, c == 15,
                           [wr, R_hT(c, c + 1)], [R_bank(b)], inc=(c == 15))
                    off = off0 + s * 256
                    vout = arena[:, off:off + 4 * HB].bitcast(BF16).rearrange("p (h n) -> p h n", n=HB // 2)[:, :, 0:128]
                    ACT(vout, banks[b][:, :].rearrange("p (h d) -> p h d", d=128), AF.Copy, [R_bank(b)],
                        [v_v(h)[1] for h in range(4 * G, 4 * G + 4)])
                    slabctr["n"] += 1
                    drain(5)

            ub2_v = (av(ARENA - 2112, 528, F32, 4), AR(ARENA - 2112, 2112))

            def pool_proj(piece):
                def evac_u(sl, b):
                    ch = piece * 4 + sl
                    w = WIN[ch]
                    ub, Rub = ubuf_v if ch % 2 == 0 else ub2_v
                    sA, RsA = sA_v
                    sB, RsB = sB_v
                    Rh = ("halo", ch * 64, ch * 64 + 64)
                    ACT(ub[:, 16:528], banks[b][:, :], AF.Copy, [R_bank(b)], [Rub])
                    ops = [lambda: POOL_copy(ub[:, 0:16], halo[:, ch, :], [Rh], [Rub]),
                           lambda: POOL_tt(sA[:, 1:528], ub[:, 1:528], ub[:, 0:527], ALU.add, [Rub], [RsA])]
                    fin, Rfin = sA, RsA
                    if w >= 4:
                        ops.append(lambda: POOL_tt(sB[:, 3:528], sA[:, 3:528], sA[:, 1:526], ALU.add, [RsA], [RsB]))
                        fin, Rfin = sB, RsB
                    if w >= 8:
                        ops.append(lambda: POOL_tt(sA[:, 7:528], sB[:, 7:528], sB[:, 3:524], ALU.add, [RsB], [RsA]))
                        fin, Rfin = sA, RsA
                    if w >= 16:
                        ops.append(lambda: POOL_tt(sB[:, 15:528], sA[:, 15:528], sA[:, 7:520], ALU.add, [RsA], [RsB]))
                        fin, Rfin = sB, RsB
                    ops.append(lambda: POOL_copy(halo[:, ch, :], ub[:, 512:528], [Rub], [Rh]))
                    dT, RdT = dT_v(ch)
                    ops.append(lambda: DVE_stt(dT, fin[:, 16:528], 1.0 / w, ub[:, 16:528], ALU.mult, ALU.subtract, [Rfin, Rub], [RdT]))
                    if first:
                        widx = ch // 2
                        t16 = tiny[:, 32:48]
                        ops.append(lambda: POOL_tt(t16, fin[:, 16:32], invc[:, widx * 16:(widx + 1) * 16], ALU.mult,
                                                   [Rfin, FULL("invc", 256)], [R_tiny(32, 48)]))
                        ops.append(lambda: POOL_tt(dT[:, 0:16], t16, ub[:, 16:32], ALU.subtract, [R_tiny(32, 48), Rub], [RdT]))
                    for o in reversed(ops):
                        q_main.appendleft(o)
                proj_ws(piece, evac_u, ndrain=10)

            def pool_mm():
                for j in range(4):
                    for oc in range(2):
                        b = next_bank()
                        for kc in range(2):
                            dT, RdT = dT_v(2 * j + kc)
                            MM(banks[b][:, :], poolw[:, j * 2 + kc, oc * 128:(oc + 1) * 128], dT, kc == 0, kc == 1,
                               [FULL("poolw", 4096), RdT], [R_bank(b)], inc=(kc == 1))
                        ch = 2 * j + oc
                        ACT(yT[:, ch, :], banks[b][:, :], AF.Identity, [R_bank(b), Rp, Rp2], [R_yT(ch, ch + 1)],
                            scale=par[:, 32 + ch:33 + ch], bias=par2[:, 24 + ch:25 + ch])

            def Am8_v(h): return av(61440 + h * 256, 128, BF16, 2), AR(61440 + h * 256, 256)
            A_BANK = [4, 5]
            U_BANK = [6, 0, 2, 3]
            O_BANK = [7, 1]

            def hgrn_chunks_all():
                HS = list(range(8))
                for p in range(4):
                    for h in HS:
                        kt, Rkt = kt_v(h)
                        qh, Rq = qh_v(h)
                        ab = A_BANK[h // 4]
                        MM(banks[ab][:, (h % 4) * 128:(h % 4 + 1) * 128], kt[:, p * 128:(p + 1) * 128], qh[:, p * 128:(p + 1) * 128],
                           True, True, [Rkt, Rq], [R_bank(ab)], inc=(h % 4 == 3))
                    for h in HS:
                        ab = A_BANK[h // 4]
                        Am, RAm = Am8_v(h)
                        DVE_tt(Am, banks[ab][:, (h % 4) * 128:(h % 4 + 1) * 128], maskA[:], ALU.mult,
                               [R_bank(ab), FULL("maskA", 512)], [RAm])
                    for half in range(2):
                        c = 2 * p + half
                        for h in HS:
                            qh, Rq = qh_v(h)
                            vv, Rv = v_v(h)
                            kh, Rkh = khtok_v(h)
                            Am, RAm = Am8_v(h)
                            ob = O_BANK[h // 4]
                            o_lo = (h % 4) * 128 + half * 64
                            o_ap = banks[ob][:, o_lo:o_lo + 64]
                            MM(o_ap, Sb[:, h, :], qh[:, c * 64:(c + 1) * 64], True, False, [R_Sb(h), Rq], [R_bank(ob)], inc=False)
                            MM(o_ap, vv[:, p, :], Am[:, half * 64:(half + 1) * 64], False, True, [Rv, RAm], [R_bank(ob)], inc=True)
                            ub_ = U_BANK[h % 4]
                            uq = h // 4
                            MM(banks[ub_][:, uq * 128:(uq + 1) * 128], kh[64 * half:64 * half + 64, p, :],
                               vv[64 * half:64 * half + 64, p, :], True, True, [Rkh, Rv], [R_bank(ub_)], inc=True)
                        for h in HS:
                            ub_ = U_BANK[h % 4]
                            uq = h // 4
                            DVE_stt(S[:, h, :], S[:, h, :], dec3[:, h, c:c + 1], banks[ub_][:, uq * 128:(uq + 1) * 128],
                                    ALU.mult, ALU.add, [R_S(h), R_dec(h), R_bank(ub_)], [R_S(h)])
                        for h in HS:
                            ACT(Sb[:, h, :], S[:, h, :], AF.Copy, [R_S(h)], [R_Sb(h)])
                        if half == 1:
                            for h in HS:
                                ob = O_BANK[h // 4]
                                og, Rog = og_v(h)
                                rg, Rrg = rg_v(h)
                                DVE_tt(og[:, p * 128:(p + 1) * 128], banks[ob][:, (h % 4) * 128:(h % 4 + 1) * 128],
                                       rg[:, p * 128:(p + 1) * 128], ALU.mult, [R_bank(ob), Rrg], [Rog])
                tmp2 = [(T_v(4), T_v(5), 7), (ubuf_v, sA_v, 6)]
                for h in HS:
                    (og2, Rog2), (rs, Rrs), sbk = tmp2[h % 2]
                    og2 = og2[:, 0:512]
                    rs = rs[:, 0:512]
                    og, Rog = og_v(h)
                    POOL_tt(og2, og, og, ALU.mult, [Rog], [Rog2])
                    MM(banks[sbk][:, :], ones[:], og2, True, True, [FULL("ones", 512), Rog2], [R_bank(sbk)], inc=True)
                    rstd_from_ms(banks[sbk][:, :], rs, 1.0 / 128.0, [R_bank(sbk)], [Rrs])
                    DVE_stt(yT[:, 8 + h, :], og, par[:, 16 + h:17 + h], rs, ALU.mult, ALU.mult, [Rog, Rrs, Rp], [R_yT(8 + h, 9 + h)])

            proj_q(0)
            proj_f(0)
            pool_proj(0)
            proj_q(1)
            proj_f(1)
            pool_proj(1)
            proj_g(0)
            proj_g(1)
            proj_v(0)
            proj_v(1)
            drain_all()
            pool_mm()
            hgrn_chunks_all()
            if STAGE == 'mixer' and ti == NT - 1:
                DMA(dbg_d, hy[:, 8192:16384], [R_yT(0, 16)], [("dbg", 0, 1)], "dbg")

            def post_norm_residual(src_ap_fn, R_src_fn, gi, store):
                for s in range(4):
                    ms = tiny[:, 24 + s:25 + s]
                    rs = tiny[:, 28 + s:29 + s]
                    DVE_rsum(ms, tiny[:, 8 + s * 4:12 + s * 4], [R_tiny(8 + s * 4, 12 + s * 4)], [R_tiny(24 + s, 25 + s)])
                    rstd_from_ms(ms, rs, 1.0, [R_tiny(24 + s, 25 + s)], [R_tiny(28 + s, 29 + s)])
                    src = src_ap_fn(s)
                    if not store:
                        DVE_stt(xbuf[:, s, :], src, rs, xbuf[:, s, :], ALU.mult, ALU.add,
                                [R_src_fn(s), R_tiny(28 + s, 29 + s), R_x(s)], [R_x(s)])
                    else:
                        DVE_stt(src, src, rs, xbuf[:, s, :], ALU.mult, ALU.add,
                                [R_src_fn(s), R_tiny(28 + s, 29 + s), R_x(s)], [R_src_fn(s)])
                        DMA(out_d[tok0 + s * 128:tok0 + (s + 1) * 128, :], src, [R_src_fn(s)],
                            [("out", tok0 + s * 128, tok0 + (s + 1) * 128)], "st%d" % s)

            def evac_tokmajor(b, dst_ap, R_dst, s, cg, junk, R_junk, gi):
                DVE_tt(dst_ap, banks[b][:, :], gpost[:, gi, cg * 512:(cg + 1) * 512], ALU.mult,
                       [R_bank(b), ("gpost", gi * 8192, (gi + 1) * 8192)], [R_dst])
                ACT(junk[:, 0:512], banks[b][:, :], AF.Square, [R_bank(b)], [R_junk, R_tiny(8 + s * 4 + cg, 9 + s * 4 + cg)],
                    scale=float(D) ** -0.5, accum_out=tiny[:, 8 + s * 4 + cg:9 + s * 4 + cg])

            for cg in range(4):
                wv, wr = next_piece(PIECE_WOUT + cg)
                for s in range(4):
                    b = next_bank()
                    for c in range(16):
                        MM(banks[b][:, :], yT[:, c, s * 128:(s + 1) * 128], wv[:, c, :], c == 0, c == 15,
                           [wr, R_yT(c, c + 1)], [R_bank(b)], inc=(c == 15))
                    evac_tokmajor(b, mix_all[:, s, cg * 512:(cg + 1) * 512], R_mix(s), s, cg, junkA, R_junkA, 0)
            post_norm_residual(lambda s: mix_all[:, s, :], R_mix, 0, STAGE == 'mixer')
            if STAGE == 'mixer':
                for _p in range(32):
                    state['cur'] += 1
                continue

            prenorm_transpose(56, hT, R_hT)
            for p in range(16):
                wv, wr = next_piece(PIECE_MI + p)
                for sl in range(4):
                    b = next_bank()
                    for c in range(16):
                        MM(banks[b][:, :], wv[:, c, sl * 128:(sl + 1) * 128], hT[:, c, :], c == 0, c == 15,
                           [wr, R_hT(c, c + 1)], [R_bank(b)], inc=(c == 15))
                    k = (p * 4 + sl) % 2
                    rt, Rrt = av(65600 + k * 2112, 512, F32, 4), AR(65600 + k * 2112, 2048)
                    ACT(rt, banks[b][:, :], AF.Relu, [R_bank(b)], [Rrt])
                    hc = p * 4 + sl
                    POOL_tt(hid[:, hc, :], rt, rt, ALU.mult, [Rrt], [R_hid(hc, hc + 1)])
            for cg in range(4):
                for jp in range(4):
                    wv, wr = next_piece(PIECE_MO + cg * 4 + jp)
                    for s in range(4):
                        for c in range(16):
                            hc = jp * 16 + c
                            MM(banks[s][:, :], hid[:, hc, s * 128:(s + 1) * 128], wv[:, c, :],
                               (jp == 0 and c == 0), (jp == 3 and c == 15),
                               [wr, R_hid(hc, hc + 1)], [R_bank(s)], inc=(c == 15))
                for s in range(4):
                    evac_tokmajor(s, ffv[:, s, cg * 512:(cg + 1) * 512], R_ff(s), s, cg, junkB, R_junkB, 1)
            post_norm_residual(lambda s: ffv[:, s, :], R_ff, 1, True)

        keys = set()
        for e in ENGS:
            for waits, fn, incinfo in P.streams[e]:
                for (k, v) in waits:
                    keys.add(k)
                if incinfo:
                    keys.add(incinfo[0])
        for k in sorted(keys):
            getsem(k)
        final_waits = [(k, v) for k, v in P.dmacnt.items() if k.startswith("dma:st") or k.startswith("dma:dbg")]

        with nc.allow_low_precision(reason="bf16 matmul operands, fp32 accumulation"), nc.Block() as block:
            def runner(name, extra=()):
                stream = P.streams[name]

                def f(e):
                    for waits, fn, incinfo in stream:
                        for (k, v) in waits:
                            e.wait_ge(getsem(k), v)
                        ins = fn(e)
                        if incinfo:
                            ins.then_inc(getsem(incinfo[0]), incinfo[1])
                    for (k, v) in extra:
                        e.wait_ge(getsem(k), v)
                return f
            block.sync(runner("sp", final_waits))
            block.tensor(runner("pe"))
            block.scalar(runner("act"))
            block.vector(runner("dve"))
            block.gpsimd(runner("pool"))
    return nc, P


_CACHE = {}


def _consts():
    identb = np.eye(128, dtype=np.float32).astype(ml_dtypes.bfloat16)
    identf = np.eye(128, dtype=np.float32)
    s = np.arange(128)[:, None]
    t = np.arange(128)[None, :]
    maskA = ((s // 64 == t // 64) & (s <= t)).astype(np.float32)
    invc = np.zeros((128, 64), np.float32)
    for wi, w in enumerate((2, 4, 8, 16)):
        invc[:, wi * 16:(wi + 1) * 16] = 1.0 / np.minimum(np.arange(16) + 1, w).astype(np.float32)
    return {"c_identb": identb, "c_identf": identf, "c_maskA": maskA, "c_invc": invc}


def _run(inputs, n_cores, nseq, seq, stage=None):
    key = (nseq, seq, stage)
    if key not in _CACHE:
        _CACHE[key] = build_program(nseq, seq, stage)[0]
    nc = _CACHE[key]
    f32 = lambda a: np.ascontiguousarray(np.asarray(a, dtype=np.float32))
    x = f32(inputs["x"])
    shared = {
        "w_in": f32(inputs["w_in"]).reshape(D, INW),
        "pool_w": f32(inputs["pool_w"]).reshape(4, 256, 256),
        "pool_b": f32(inputs["pool_b"]).reshape(8, 128),
        "pool_scale": f32(inputs["pool_scale"]).reshape(8, 128),
        "lb_logits": f32(inputs["lb_logits"]).reshape(16, 128),
        "hgrn_norm_w": f32(inputs["hgrn_norm_w"]).reshape(8, 128),
        "w_out": f32(inputs["w_out"]).reshape(D, D),
        "norm_mix_pre": f32(inputs["norm_mix_pre"]).reshape(16, 128),
        "norm_mix_post": f32(inputs["norm_mix_post"]).reshape(1, D),
        "norm_mlp_pre": f32(inputs["norm_mlp_pre"]).reshape(16, 128),
        "norm_mlp_post": f32(inputs["norm_mlp_post"]).reshape(1, D),
        "w_mlp_in": f32(inputs["w_mlp_in"]).reshape(D, DFF),
        "w_mlp_out": f32(inputs["w_mlp_out"]).reshape(DFF, D),
    }
    shared.update(_consts())
    in_maps = []
    for c in range(n_cores):
        m = dict(shared)
        m["x"] = np.ascontiguousarray(x[c * nseq:(c + 1) * nseq].reshape(nseq * seq, D))
        in_maps.append(m)
    res = run_bass_kernel_spmd(nc, in_maps, core_ids=list(range(n_cores)))
    if stage == 'mixer':
        global DBG
        DBG = [np.asarray(r["dbg"]) for r in res.results]
    outs = [np.asarray(r["out"], dtype=np.float32).reshape(nseq, seq, D) for r in res.results]
    return np.concatenate(outs, axis=0)


def kernel(**inputs):
    x = inputs["x"]
    B, S_, _ = x.shape
    return _run(inputs, N_CORES, B // N_CORES, S_)
```

```python
import contextlib
import numpy as np
import ml_dtypes
import concourse.bass as bass
import concourse.mybir as mybir
from concourse.bass_utils import run_bass_kernel_spmd

F32 = mybir.dt.float32
BF16 = mybir.dt.bfloat16
U8 = mybir.dt.uint8
AF = mybir.ActivationFunctionType
ALU = mybir.AluOpType

D = 2048
DFF = 8192
INW = 5120
TT = 512
EPS = 1e-6
N_CORES = 8
ENGS = ("pe", "act", "dve", "pool", "sp")


class Prog:
    def __init__(self):
        self.streams = {e: [] for e in ENGS}
        self.cnt = {e: 0 for e in ENGS}
        self.waited = {e: {} for e in ENGS}
        self.regions = {}
        self.dmacnt = {}

    def _gather(self, deps, key, lo, hi, is_write):
        lst = self.regions.setdefault(key, [])
        for ent in lst:
            if ent[0] < hi and lo < ent[1]:
                w = ent[2]
                if w is not None:
                    if deps.get(w[0], 0) < w[1]:
                        deps[w[0]] = w[1]
                if is_write:
                    for k, v in ent[3].items():
                        if deps.get(k, 0) < v:
                            deps[k] = v

    def _record(self, key, lo, hi, tok, is_write):
        lst = self.regions.setdefault(key, [])
        if is_write:
            lst[:] = [e for e in lst if not (lo <= e[0] and e[1] <= hi)]
            lst.append([lo, hi, tok, {}])
        else:
            for ent in lst:
                if ent[0] == lo and ent[1] == hi:
                    if ent[3].get(tok[0], 0) < tok[1]:
                        ent[3][tok[0]] = tok[1]
                    return
            lst.append([lo, hi, None, {tok[0]: tok[1]}])

    def emit(self, eng, fn, reads=(), writes=(), inc=True, dma_slot=None):
        deps = {}
        for (k, lo, hi) in reads:
            self._gather(deps, k, lo, hi, k.startswith("ps"))
        for (k, lo, hi) in writes:
            self._gather(deps, k, lo, hi, True)
        if eng == "pe":
            deps.pop("pe", None)
        waits = []
        wd = self.waited[eng]
        for k, v in deps.items():
            if wd.get(k, 0) < v:
                wd[k] = v
                waits.append((k, v))
        if dma_slot is not None:
            sk = "dma:" + dma_slot
            self.dmacnt[sk] = self.dmacnt.get(sk, 0) + 16
            tok = (sk, self.dmacnt[sk])
            incinfo = (sk, 16)
        else:
            if inc:
                self.cnt[eng] += 1
                tok = (eng, self.cnt[eng])
                incinfo = (eng, 1)
            else:
                tok = (eng, self.cnt[eng] + 1)
                incinfo = None
        for (k, lo, hi) in reads:
            self._record(k, lo, hi, tok, False)
        for (k, lo, hi) in writes:
            self._record(k, lo, hi, tok, True)
        self.streams[eng].append((waits, fn, incinfo))
        return tok


def build_program(NSEQ, SEQ, STAGE=None):
    NTOK = NSEQ * SEQ
    TPS = SEQ // TT
    NT = NSEQ * TPS
    nc = bass.Bass("TRN2", target_bir_lowering=False)
    P = Prog()

    def din(name, shape, dt=F32):
        return nc.dram_tensor(name, list(shape), dt, kind="ExternalInput").ap()

    x_d = din("x", [NTOK, D])
    w_in_d = din("w_in", [D, INW])
    pool_w_d = din("pool_w", [4, 256, 256])
    pool_b_d = din("pool_b", [8, 128])
    pool_s_d = din("pool_scale", [8, 128])
    lbl_d = din("lb_logits", [16, 128])
    nw_d = din("hgrn_norm_w", [8, 128])
    w_out_d = din("w_out", [D, D])
    g_pre1_d = din("norm_mix_pre", [16, 128])
    g_post1_d = din("norm_mix_post", [1, D])
    g_pre2_d = din("norm_mlp_pre", [16, 128])
    g_post2_d = din("norm_mlp_post", [1, D])
    w_mi_d = din("w_mlp_in", [D, DFF])
    w_mo_d = din("w_mlp_out", [DFF, D])
    identb_d = din("c_identb", [128, 128], BF16)
    identf_d = din("c_identf", [128, 128])
    maskA_d = din("c_maskA", [128, 128])
    invc_d = din("c_invc", [128, 64])
    out_d = nc.dram_tensor("out", [NTOK, D], F32, kind="ExternalOutput").ap()
    dbg_d = nc.dram_tensor("dbg", [128, 8192], BF16, kind="ExternalOutput").ap() if STAGE == 'mixer' else None
    NPIECE = 10 + 4 + 16 + 16
    wscr = nc.dram_tensor("wscr", [NPIECE, 128, 16 * 512], BF16).ap()

    es = contextlib.ExitStack()
    with es:
        def sb(name, shape, dt):
            return es.enter_context(nc.sbuf_tensor(name, list(shape), dt))

        identb = sb("identb", [128, 128], BF16)
        identf = sb("identf", [128, 128], F32)
        maskA = sb("maskA", [128, 128], F32)
        ones = sb("ones", [128, 128], F32)
        smask = sb("smask", [128, 512], F32)
        invc = sb("invc", [128, 64], F32)
        prow = sb("prow", [72, 128], F32)
        par = sb("par", [128, 72], F32)
        par2 = sb("par2", [128, 40], F32)
        gpost = sb("gpost", [128, 2, D], F32)
        poolw = sb("poolw", [128, 8, 256], BF16)
        S = sb("S", [128, 8, 128], F32)
        Sb = sb("Sb", [128, 8, 128], BF16)
        halo = sb("halo", [128, 8, 16], F32)
        xbuf = sb("xbuf", [128, 4, D], F32)
        hy = sb("hy", [128, 2 * 16 * 512], BF16)
        ring = sb("ring", [128, 2, 16 * 512], BF16)
        tiny = sb("tiny", [128, 64], F32)
        dec_t = sb("dec_t", [128, 64], F32)
        ARENA = 78016 + 2112
        arena = sb("arena", [128, ARENA], U8)
        banks = [es.enter_context(nc.psum_tensor("bank%d" % b, [128, 512], F32)) for b in range(8)]

        sem_names = {}

        def getsem(key):
            if key not in sem_names:
                sem_names[key] = es.enter_context(nc.semaphore("s_" + key.replace(":", "_")))
            return sem_names[key]

        def av(off, n_elem, dt, esz):
            return arena[:, off:off + n_elem * esz].bitcast(dt)

        def AR(off, nbytes):
            return ("arena", off, off + nbytes)

        hT = hy[:, 0:8192].rearrange("p (c t) -> p c t", t=512)
        yT = hy[:, 8192:16384].rearrange("p (c t) -> p c t", t=512)
        ffv = hy[:].bitcast(F32).rearrange("p (s n) -> p s n", n=D)

        def R_hT(c0=0, c1=16):
            return ("hy", c0 * 1024, c1 * 1024)

        def R_yT(c0=0, c1=16):
            return ("hy", 16384 + c0 * 1024, 16384 + c1 * 1024)

        def R_ff(s):
            return ("hy", s * 8192, (s + 1) * 8192)

        def R_x(s):
            return ("xbuf", s * 8192, (s + 1) * 8192)

        def R_bank(b, lo=0, hi=2048):
            return ("ps%d" % b, 0, 2048)

        def R_ring(s):
            return ("ring", s * 16384, (s + 1) * 16384)

        def R_tiny(c0, c1):
            return ("tiny", c0 * 4, c1 * 4)

        HB = 5120
        def qh_v(h): return av(h * HB, 512, BF16, 2), AR(h * HB, 1024)
        def kt_v(h): return av(h * HB + 1024, 512, BF16, 2), AR(h * HB + 1024, 1024)
        def khtok_v(h): return av(h * HB + 2048, 512, BF16, 2).rearrange("p (s d) -> p s d", d=128), AR(h * HB + 2048, 1024)
        def v_v(h): return av(h * HB + 3072, 512, BF16, 2).rearrange("p (s d) -> p s d", d=128), AR(h * HB + 3072, 1024)
        def rg_v(h): return av(h * HB + 4096, 512, BF16, 2), AR(h * HB + 4096, 1024)
        def og_v(i): return av(40960 + i * 2048, 512, F32, 4), AR(40960 + i * 2048, 2048)
        def T_v(i): return av(49152 + i * 2048, 512, F32, 4), AR(49152 + i * 2048, 2048)
        khT_v = (av(61440, 512, BF16, 2), AR(61440, 1024))
        og2_v = T_v(0)
        rstd_v = T_v(1)
        def Am_v(i): return av(62464 + i * 256, 128, BF16, 2), AR(62464 + i * 256, 256)
        ubuf_v = (av(63488, 528, F32, 4), AR(63488, 2112))
        sA_v = (av(65600, 528, F32, 4), AR(65600, 2112))
        sB_v = (av(67712, 528, F32, 4), AR(67712, 2112))
        def dT_v(c): return av(69824 + c * 1024, 512, BF16, 2), AR(69824 + c * 1024, 1024)
        mix_all = av(0, 4 * D, F32, 4).rearrange("p (s n) -> p s n", n=D)
        def R_mix(s): return AR(s * 8192, 8192)
        hid = av(0, 64 * 512, BF16, 2).rearrange("p (c t) -> p c t", t=512)
        def R_hid(c0, c1): return AR(c0 * 1024, (c1 - c0) * 1024)
        junkA = av(57344, 2048, BF16, 2)
        R_junkA = AR(57344, 4096)
        junkB = av(69824, 512, BF16, 2)
        R_junkB = AR(69824, 1024)

        def ACT(out, in_, func, reads, writes, **kw):
            P.emit("act", lambda e: e.activation(out=out, in_=in_, func=func, **kw), reads, writes)

        def DVE_tt(out, in0, in1, op, reads, writes):
            P.emit("dve", lambda e: e.tensor_tensor(out=out, in0=in0, in1=in1, op=op), reads, writes)

        def DVE_ts(out, in0, s1, s2, op0, op1, reads, writes):
            if op1 is None:
                P.emit("dve", lambda e: e.tensor_scalar(out=out, in0=in0, scalar1=s1, scalar2=None, op0=op0), reads, writes)
            else:
                P.emit("dve", lambda e: e.tensor_scalar(out=out, in0=in0, scalar1=s1, scalar2=s2, op0=op0, op1=op1), reads, writes)

        def DVE_stt(out, in0, scalar, in1, op0, op1, reads, writes):
            P.emit("dve", lambda e: e.scalar_tensor_tensor(out=out, in0=in0, scalar=scalar, in1=in1, op0=op0, op1=op1), reads, writes)

        def DVE_copy(out, in_, reads, writes):
            P.emit("dve", lambda e: e.tensor_copy(out=out, in_=in_), reads, writes)

        def DVE_recip(out, in_, reads, writes):
            P.emit("dve", lambda e: e.reciprocal(out=out, in_=in_), reads, writes)

        def DVE_scan(out, d0, d1, reads, writes):
            P.emit("dve", lambda e: e.tensor_tensor_scan(out=out, data0=d0, data1=d1, initial=0.0, op0=ALU.mult, op1=ALU.add), reads, writes)

        def DVE_rsum(out, in_, reads, writes):
            P.emit("dve", lambda e: e.reduce_sum(out=out, in_=in_, axis=mybir.AxisListType.X), reads, writes)

        def ENG_memset(eng, ap, val, writes):
            P.emit(eng, lambda e: e.memset(ap, val), (), writes)

        def POOL_tt(out, in0, in1, op, reads, writes):
            P.emit("pool", lambda e: e.tensor_tensor(out=out, in0=in0, in1=in1, op=op), reads, writes)

        def POOL_copy(out, in_, reads, writes):
            P.emit("pool", lambda e: e.tensor_copy(out=out, in_=in_), reads, writes)

        def MM(out, lhsT, rhs, start, stop, reads, writes, inc):
            P.emit("pe", lambda e: e.matmul(out, lhsT=lhsT, rhs=rhs, start=start, stop=stop), reads, writes, inc=inc)

        def TR(out, in_, ident, reads, writes, inc):
            P.emit("pe", lambda e: e.transpose(out=out, in_=in_, identity=ident), reads, writes, inc=inc)

        def DMA(out, in_, reads, writes, slot):
            P.emit("sp", lambda e: e.dma_start(out=out, in_=in_), reads, writes, dma_slot=slot)

        def FULL(name, nbytes):
            return (name, 0, nbytes)

        DMA(identb[:], identb_d, (), [FULL("identb", 256)], "c0")
        DMA(identf[:], identf_d, (), [FULL("identf", 512)], "c1")
        DMA(maskA[:], maskA_d, (), [FULL("maskA", 512)], "c2")
        DMA(invc[:], invc_d, (), [FULL("invc", 256)], "c3")
        DMA(prow[0:16, :], lbl_d, (), [("prow", 0, 1)], "c4")
        DMA(prow[16:24, :], nw_d, (), [("prow", 1, 2)], "c5")
        DMA(prow[24:32, :], pool_b_d, (), [("prow", 2, 3)], "c6")
        DMA(prow[32:40, :], pool_s_d, (), [("prow", 3, 4)], "c7")
        DMA(prow[40:56, :], g_pre1_d, (), [("prow", 4, 5)], "c8")
        DMA(prow[56:72, :], g_pre2_d, (), [("prow", 5, 6)], "c9")
        DMA(gpost[:, 0, :], g_post1_d.partition_broadcast(128), (), [("gpost", 0, 8192)], "c10")
        DMA(gpost[:, 1, :], g_post2_d.partition_broadcast(128), (), [("gpost", 8192, 16384)], "c11")
        ENG_memset("pool", ones[:], 1.0, [FULL("ones", 512)])
        ENG_memset("pool", smask[:], 1.0, [FULL("smask", 2048)])
        ENG_memset("pool", smask[:].rearrange("p (c t) -> p c t", t=64)[:, :, 0:1], 0.0, [FULL("smask", 2048)])
        TR(banks[4][0:128, 0:72], prow[:, :], identf[0:72, 0:72], [("prow", 0, 6), FULL("identf", 512)], [R_bank(4)], True)
        DVE_copy(par[:], banks[4][:, 0:72], [R_bank(4)], [FULL("par", 288)])
        Rp = FULL("par", 288)
        Rp2 = FULL("par2", 160)
        DVE_tt(par2[:, 32:40], par[:, 8:16], par[:, 0:8], ALU.subtract, [Rp], [Rp2])
        ACT(par2[:, 32:40], par2[:, 32:40], AF.Exp, [Rp2], [Rp2])
        DVE_ts(par2[:, 32:40], par2[:, 32:40], 1.0, None, ALU.add, None, [Rp2], [Rp2])
        DVE_recip(par2[:, 0:8], par2[:, 32:40], [Rp2], [Rp2])
        DVE_ts(par2[:, 8:16], par2[:, 0:8], -1.0, 1.0, ALU.mult, ALU.add, [Rp2], [Rp2])
        DVE_ts(par2[:, 16:24], par2[:, 8:16], -1.0, None, ALU.mult, None, [Rp2], [Rp2])
        DVE_tt(par2[:, 24:32], par[:, 24:32], par[:, 32:40], ALU.mult, [Rp, Rp2], [Rp2])

        def stg_v(i):
            return av(i * 16384, 4096, F32, 4).rearrange("p (c n) -> p c n", n=512), AR(i * 16384, 16384)

        def cvt_v(i):
            return ring[:, i // 2, (i % 2) * 4096:(i % 2) * 4096 + 4096].rearrange("p (c n) -> p c n", n=512), ("ring", i * 8192, (i + 1) * 8192)

        pieces_src = []
        for p in range(10):
            pieces_src.append(w_in_d[:, p * 512:(p + 1) * 512])
        for p in range(4):
            pieces_src.append(w_out_d[:, p * 512:(p + 1) * 512])
        for p in range(16):
            pieces_src.append(w_mi_d[:, p * 512:(p + 1) * 512])
        for cg in range(4):
            for jp in range(4):
                pieces_src.append(w_mo_d[jp * 2048:(jp + 1) * 2048, cg * 512:(cg + 1) * 512])
        PIECE_WOUT = 10
        PIECE_MI = 14
        PIECE_MO = 30
        cv_engs = ["act", "dve", "pool"]
        N_PRO = PIECE_MI
        units = [(pi, half) for pi in range(N_PRO) for half in range(2)]

        def unit_load(u):
            pi, half = units[u]
            sl = u % 4
            sv_ap, sv_r = stg_v(sl)
            srcv = pieces_src[pi][half * 1024:(half + 1) * 1024, :].rearrange("(c p) n -> p c n", p=128)
            DMA(sv_ap, srcv, (), [sv_r], "stg%d" % sl)

        LOOKAHEAD = 3
        for u in range(min(LOOKAHEAD, len(units))):
            unit_load(u)
        for u, (pi, half) in enumerate(units):
            if u + LOOKAHEAD < len(units):
                unit_load(u + LOOKAHEAD)
            sl = u % 4
            sv_ap, sv_r = stg_v(sl)
            cv_ap, cv_r = cvt_v(sl)
            eng = cv_engs[u % 3]
            if eng == "act":
                ACT(cv_ap, sv_ap, AF.Copy, [sv_r], [cv_r])
            elif eng == "dve":
                DVE_copy(cv_ap, sv_ap, [sv_r], [cv_r])
            else:
                POOL_copy(cv_ap, sv_ap, [sv_r], [cv_r])
            dst = wscr[pi, :, half * 4096:(half + 1) * 4096].rearrange("p (c n) -> p c n", n=512)
            DMA(dst, cv_ap, [cv_r], [("wscr", pi * 2 + half, pi * 2 + half + 1)], "cvt%d" % sl)
        pw_stage = av(65536, 2048, F32, 4).rearrange("p (a n) -> p a n", n=256)
        R_pws = AR(65536, 8192)
        DMA(pw_stage, pool_w_d.rearrange("j (kc p) n -> p (j kc) n", p=128), (), [R_pws], "c12")
        DVE_copy(poolw[:], pw_stage, [R_pws], [FULL("poolw", 4096)])

        sched = []
        tile_order = [2, 4, 0, 3, 5, 1, 8, 9, 6, 7] + list(range(PIECE_WOUT, PIECE_WOUT + 4)) + \
            list(range(PIECE_MI, PIECE_MI + 16)) + list(range(PIECE_MO, PIECE_MO + 16))
        for t in range(NT):
            sched.extend(tile_order)
        state = {"loaded": 0, "cur": 0, "stg": 0}

        def ensure_loaded(upto):
            while state["loaded"] <= upto and state["loaded"] < len(sched):
                i = state["loaded"]
                pi = sched[i]
                sl = i % 2
                if pi >= N_PRO and i < len(tile_order):
                    rv = ring[:, sl, :].rearrange("p (c n) -> p c n", n=512)
                    for c in range(16):
                        k = state["stg"] % 4
                        state["stg"] += 1
                        st_ap, st_r = av(70848 + k * 2048, 512, F32, 4), AR(70848 + k * 2048, 2048)
                        DMA(st_ap, pieces_src[pi][c * 128:(c + 1) * 128, :], (), [st_r], "fly%d" % k)
                        DVE_copy(rv[:, c, :], st_ap, [st_r], [("ring", sl * 16384 + c * 1024, sl * 16384 + (c + 1) * 1024)])
                    DMA(wscr[pi, :, :], ring[:, sl, :], [R_ring(sl)], [("wscr", pi * 2, pi * 2 + 2)], "flyst%d" % sl)
                else:
                    DMA(ring[:, sl, :], wscr[pi, :, :], [("wscr", pi * 2, pi * 2 + 2)], [R_ring(sl)], "ring%d" % sl)
                state["loaded"] += 1

        def next_piece(expect):
            i = state["cur"]
            assert sched[i] == expect, (sched[i], expect)
            ensure_loaded(i + 1)
            state["cur"] += 1
            sl = i % 2
            return ring[:, sl, :].rearrange("p (c n) -> p c n", n=512), R_ring(sl)

        bigbank = {"i": 0}

        def next_bank():
            b = bigbank["i"] % 4
            bigbank["i"] += 1
            return b

        def rstd_from_ms(ms_ap, out_ap, scale, rd, wr):
            ACT(out_ap, ms_ap, AF.Ln, rd, wr, bias=EPS, scale=scale)
            ACT(out_ap, out_ap, AF.Exp, wr, wr, scale=-0.5)

        def prenorm_transpose(gcol0, dstT, R_dst):
            for s in range(4):
                ms = tiny[:, s:s + 1]
                rs = tiny[:, 4 + s:5 + s]
                ACT(junkA, xbuf[:, s, :], AF.Square, [R_x(s)], [R_junkA, R_tiny(s, s + 1)],
                    scale=float(D) ** -0.5, accum_out=ms)
                rstd_from_ms(ms, rs, 1.0, [R_tiny(s, s + 1)], [R_tiny(4 + s, 5 + s)])
                xn, R_xn = av(49152 + (s % 2) * 4096, 2048, BF16, 2), AR(49152 + (s % 2) * 4096, 4096)
                DVE_ts(xn, xbuf[:, s, :], rs, None, ALU.mult, None, [R_x(s), R_tiny(4 + s, 5 + s)], [R_xn])
                for half in range(2):
                    b = next_bank()
                    pb = banks[b][:].bitcast(BF16)
                    for j in range(8):
                        c = half * 8 + j
                        TR(pb[:, j * 128:(j + 1) * 128], xn[:, c * 128:(c + 1) * 128], identb[:],
                           [R_xn, FULL("identb", 256)], [R_bank(b)], inc=(j == 7))
                    gain = par[:, gcol0 + half * 8:gcol0 + half * 8 + 8].unsqueeze(2).broadcast_to([128, 8, 128])
                    DVE_tt(dstT[:, half * 8:half * 8 + 8, s * 128:(s + 1) * 128],
                           pb.rearrange("p (c t) -> p c t", t=128), gain, ALU.mult,
                           [R_bank(b), Rp], [R_dst(half * 8, half * 8 + 8)])

        dec3 = dec_t[:, :].rearrange("p (h c) -> p h c", c=8)

        def R_dec(h):
            return ("dec_t", h * 32, h * 32 + 32)

        def R_S(h):
            return ("S", h * 512, (h + 1) * 512)

        def R_Sb(h):
            return ("Sb", h * 256, (h + 1) * 256)

        R_ident = FULL("identb", 256)
        WIN = [2, 2, 4, 4, 8, 8, 16, 16]

        for ti in range(NT):
            pos = ti % TPS
            first = (pos == 0)
            tok0 = ti * TT
            for s in range(4):
                DMA(xbuf[:, s, :], x_d[tok0 + s * 128:tok0 + (s + 1) * 128, :], (), [R_x(s)], "x%d" % s)
            if first:
                ENG_memset("pool", S[:], 0.0, [FULL("S", 4096)])
                ENG_memset("pool", Sb[:], 0.0, [FULL("Sb", 2048)])
                ENG_memset("pool", halo[:], 0.0, [FULL("halo", 512)])
            def _early_store():
                for s in range(4):
                    DMA(out_d[tok0 + s * 128:tok0 + (s + 1) * 128, :], xbuf[:, s, :], [R_x(s)],
                        [("out", tok0 + s * 128, tok0 + (s + 1) * 128)], "st%d" % s)
            if STAGE == 'pro':
                _early_store()
                break
            prenorm_transpose(40, hT, R_hT)
            if STAGE == 'A':
                _early_store()
                break

            from collections import deque
            q_main = deque()
            q_pe = deque()
            slabctr = {"n": 0}

            def drain(n):
                for _ in range(n):
                    if not q_main:
                        break
                    q_main.popleft()()
                while q_pe and q_pe[0][0] <= slabctr["n"]:
                    q_pe.popleft()[1]()

            def drain_all():
                while q_main or q_pe:
                    while q_main:
                        q_main.popleft()()
                    while q_pe:
                        q_pe.popleft()[1]()

            def proj_ws(piece_idx, evac, ndrain=5):
                wv, wr = next_piece(piece_idx)
                for sl in range(4):
                    b = next_bank()
                    for c in range(16):
                        MM(banks[b][:, :], wv[:, c, sl * 128:(sl + 1) * 128], hT[:, c, :], c == 0, c == 15,
                           [wr, R_hT(c, c + 1)], [R_bank(b)], inc=(c == 15))
                    evac(sl, b)
                    slabctr["n"] += 1
                    drain(ndrain)

            def ef_v(h): return av(h * HB + 1024, 512, F32, 4), AR(h * HB + 1024, 2048)
            def eg_v(h): return av(h * HB + 3072, 512, F32, 4), AR(h * HB + 3072, 2048)

            def proj_q(G):
                hs = [4 * G + i for i in range(4)]

                def evac_q(sl, b):
                    qh, Rq = qh_v(hs[sl])
                    ACT(qh, banks[b][:, :], AF.Copy, [R_bank(b)], [Rq])
                proj_ws(2 + G, evac_q)

            def proj_f(G):
                hs = [4 * G + i for i in range(4)]

                def evac_f(sl, b):
                    h = hs[sl]
                    ef, Ref = ef_v(h)
                    ACT(ef, banks[b][:, :], AF.Exp, [R_bank(b)], [Ref], scale=-1.0)
                    T1, R1 = T_v(0); T2, R2 = T_v(1); T3, R3 = T_v(2); T4, R4 = T_v(3); T5, R5 = T_v(4)
                    qh, Rq = qh_v(h)
                    kt, Rkt = kt_v(h)
                    khT, RkhT = av(40960 + (h % 4) * 2048, 512, BF16, 2), AR(40960 + (h % 4) * 2048, 1024)
                    kh, Rkh = khtok_v(h)
                    ch = [
                        lambda: ACT(T1, ef, AF.Ln, [Ref], [R1], bias=1.0),
                        lambda: ACT(T2, T1, AF.Exp, [R1], [R2], scale=-1.0),
                        lambda: ACT(T3, T2, AF.Ln, [R2, Rp2], [R3], scale=par2[:, 8 + h:9 + h], bias=par2[:, h:h + 1]),
                        lambda: DVE_ts(T4, T2, par2[:, 16 + h:17 + h], par2[:, 8 + h:9 + h], ALU.mult, ALU.add, [R2, Rp2], [R4]),
                        lambda: DVE_scan(T1, smask[:], T3, [R3, FULL("smask", 2048)], [R1]),
                        lambda: ACT(T3, T1, AF.Exp, [R1], [R3]),
                        lambda: ACT(T5, T1, AF.Exp, [R1], [R5], scale=-1.0),
                        lambda: DVE_copy(dec3[:, h, :], T3.rearrange("p (c t) -> p c t", t=64)[:, :, 63], [R3], [R_dec(h)]),
                        lambda: DVE_tt(qh, qh, T3, ALU.mult, [Rq, R3], [Rq]),
                        lambda: DVE_tt(T2, T4, T5, ALU.mult, [R4, R5], [R2]),
                        lambda: POOL_copy(kt, T2, [R2], [Rkt]),
                        lambda: DVE_tt(khT.rearrange("p (c t) -> p c t", t=64), T2.rearrange("p (c t) -> p c t", t=64),
                                       dec3[:, h, :].unsqueeze(2).broadcast_to([128, 8, 64]), ALU.mult, [R2, R_dec(h)], [RkhT]),
                    ]

                    def pe_part():
                        pb = banks[4][:].bitcast(BF16)
                        for p in range(4):
                            TR(pb[:, p * 128:(p + 1) * 128], khT[:, p * 128:(p + 1) * 128], identb[:],
                               [RkhT, R_ident], [R_bank(4)], inc=(p == 3))
                        ACT(kh, pb[:, 0:512].rearrange("p (s d) -> p s d", d=128), AF.Copy, [R_bank(4)], [Rkh])

                    def last():
                        q_pe.append((slabctr["n"] + 3, pe_part))
                    q_main.extend(ch)
                    q_main.append(last)
                proj_ws(4 + G, evac_f)

            def proj_g(G):
                hs = [4 * G + i for i in range(4)]

                def evac_g(sl, b):
                    h = hs[sl]
                    eg, Reg = eg_v(h)
                    T6, R6 = T_v(5)
                    rg, Rrg = rg_v(h)
                    ACT(eg, banks[b][:, :], AF.Exp, [R_bank(b)], [Reg], scale=-1.0)
                    q_main.appendleft(lambda: ACT(rg, T6, AF.Exp, [R6], [Rrg], scale=-1.0))
                    q_main.appendleft(lambda: ACT(T6, eg, AF.Ln, [Reg], [R6], bias=1.0))
                proj_ws(8 + G, evac_g)

            def proj_v(G):
                wv, wr = next_piece(6 + G)
                off0 = (4 * G) * HB + 3072
                for s in range(4):
                    b = next_bank()
                    for c in range(16):
                        MM(banks[b][:, :], hT[:, c, s * 128:(s + 1) * 128], wv[:, c, :], c == 0, c == 15,
                           [wr, R_hT(c, c + 1)], [R_bank(b)], inc=(c == 15))
                    off = off0 + s * 256
                    vout = arena[:, off:off + 4 * HB].bitcast(BF16).rearrange("p (h n) -> p h n", n=HB // 2)[:, :, 0:128]
                    ACT(vout, banks[b][:, :].rearrange("p (h d) -> p h d", d=128), AF.Copy, [R_bank(b)],
                        [v_v(h)[1] for h in range(4 * G, 4 * G + 4)])
                    slabctr["n"] += 1
                    drain(5)

            ub2_v = (av(ARENA - 2112, 528, F32, 4), AR(ARENA - 2112, 2112))

            def pool_proj(piece):
                def evac_u(sl, b):
                    ch = piece * 4 + sl
                    w = WIN[ch]
                    ub, Rub = ubuf_v if ch % 2 == 0 else ub2_v
                    sA, RsA = sA_v
                    sB, RsB = sB_v
                    Rh = ("halo", ch * 64, ch * 64 + 64)
                    ACT(ub[:, 16:528], banks[b][:, :], AF.Copy, [R_bank(b)], [Rub])
                    ops = [lambda: POOL_copy(ub[:, 0:16], halo[:, ch, :], [Rh], [Rub]),
                           lambda: POOL_tt(sA[:, 1:528], ub[:, 1:528], ub[:, 0:527], ALU.add, [Rub], [RsA])]
                    fin, Rfin = sA, RsA
                    if w >= 4:
                        ops.append(lambda: POOL_tt(sB[:, 3:528], sA[:, 3:528], sA[:, 1:526], ALU.add, [RsA], [RsB]))
                        fin, Rfin = sB, RsB
                    if w >= 8:
                        ops.append(lambda: POOL_tt(sA[:, 7:528], sB[:, 7:528], sB[:, 3:524], ALU.add, [RsB], [RsA]))
                        fin, Rfin = sA, RsA
                    if w >= 16:
                        ops.append(lambda: POOL_tt(sB[:, 15:528], sA[:, 15:528], sA[:, 7:520], ALU.add, [RsA], [RsB]))
                        fin, Rfin = sB, RsB
                    ops.append(lambda: POOL_copy(halo[:, ch, :], ub[:, 512:528], [Rub], [Rh]))
                    dT, RdT = dT_v(ch)
                    ops.append(lambda: DVE_stt(dT, fin[:, 16:528], 1.0 / w, ub[:, 16:528], ALU.mult, ALU.subtract, [Rfin, Rub], [RdT]))
                    if first:
                        widx = ch // 2
                        t16 = tiny[:, 32:48]
                        ops.append(lambda: POOL_tt(t16, fin[:, 16:32], invc[:, widx * 16:(widx + 1) * 16], ALU.mult,
                                                   [Rfin, FULL("invc", 256)], [R_tiny(32, 48)]))
                        ops.append(lambda: POOL_tt(dT[:, 0:16], t16, ub[:, 16:32], ALU.subtract, [R_tiny(32, 48), Rub], [RdT]))
                    for o in reversed(ops):
                        q_main.appendleft(o)
                proj_ws(piece, evac_u, ndrain=10)

            def pool_mm():
                for j in range(4):
                    for oc in range(2):
                        b = next_bank()
                        for kc in range(2):
                            dT, RdT = dT_v(2 * j + kc)
                            MM(banks[b][:, :], poolw[:, j * 2 + kc, oc * 128:(oc + 1) * 128], dT, kc == 0, kc == 1,
                               [FULL("poolw", 4096), RdT], [R_bank(b)], inc=(kc == 1))
                        ch = 2 * j + oc
                        ACT(yT[:, ch, :], banks[b][:, :], AF.Identity, [R_bank(b), Rp, Rp2], [R_yT(ch, ch + 1)],
                            scale=par[:, 32 + ch:33 + ch], bias=par2[:, 24 + ch:25 + ch])

            def Am8_v(h): return av(61440 + h * 256, 128, BF16, 2), AR(61440 + h * 256, 256)
            A_BANK = [4, 5]
            U_BANK = [6, 0, 2, 3]
            O_BANK = [7, 1]

            def hgrn_chunks_all():
                HS = list(range(8))
                for p in range(4):
                    for h in HS:
                        kt, Rkt = kt_v(h)
                        qh, Rq = qh_v(h)
                        ab = A_BANK[h // 4]
                        MM(banks[ab][:, (h % 4) * 128:(h % 4 + 1) * 128], kt[:, p * 128:(p + 1) * 128], qh[:, p * 128:(p + 1) * 128],
                           True, True, [Rkt, Rq], [R_bank(ab)], inc=(h % 4 == 3))
                    for h in HS:
                        ab = A_BANK[h // 4]
                        Am, RAm = Am8_v(h)
                        DVE_tt(Am, banks[ab][:, (h % 4) * 128:(h % 4 + 1) * 128], maskA[:], ALU.mult,
                               [R_bank(ab), FULL("maskA", 512)], [RAm])
                    for half in range(2):
                        c = 2 * p + half
                        for h in HS:
                            qh, Rq = qh_v(h)
                            vv, Rv = v_v(h)
                            kh, Rkh = khtok_v(h)
                            Am, RAm = Am8_v(h)
                            ob = O_BANK[h // 4]
                            o_lo = (h % 4) * 128 + half * 64
                            o_ap = banks[ob][:, o_lo:o_lo + 64]
                            MM(o_ap, Sb[:, h, :], qh[:, c * 64:(c + 1) * 64], True, False, [R_Sb(h), Rq], [R_bank(ob)], inc=False)
                            MM(o_ap, vv[:, p, :], Am[:, half * 64:(half + 1) * 64], False, True, [Rv, RAm], [R_bank(ob)], inc=True)
                            ub_ = U_BANK[h % 4]
                            uq = h // 4
                            MM(banks[ub_][:, uq * 128:(uq + 1) * 128], kh[64 * half:64 * half + 64, p, :],
                               vv[64 * half:64 * half + 64, p, :], True, True, [Rkh, Rv], [R_bank(ub_)], inc=True)
                        for h in HS:
                            ub_ = U_BANK[h % 4]
                            uq = h // 4
                            DVE_stt(S[:, h, :], S[:, h, :], dec3[:, h, c:c + 1], banks[ub_][:, uq * 128:(uq + 1) * 128],
                                    ALU.mult, ALU.add, [R_S(h), R_dec(h), R_bank(ub_)], [R_S(h)])
                        for h in HS:
                            ACT(Sb[:, h, :], S[:, h, :], AF.Copy, [R_S(h)], [R_Sb(h)])
                        if half == 1:
                            for h in HS:
                                ob = O_BANK[h // 4]
                                og, Rog = og_v(h)
                                rg, Rrg = rg_v(h)
                                DVE_tt(og[:, p * 128:(p + 1) * 128], banks[ob][:, (h % 4) * 128:(h % 4 + 1) * 128],
                                       rg[:, p * 128:(p + 1) * 128], ALU.mult, [R_bank(ob), Rrg], [Rog])
                tmp2 = [(T_v(4), T_v(5), 7), (ubuf_v, sA_v, 6)]
                for h in HS:
                    (og2, Rog2), (rs, Rrs), sbk = tmp2[h % 2]
                    og2 = og2[:, 0:512]
                    rs = rs[:, 0:512]
                    og, Rog = og_v(h)
                    POOL_tt(og2, og, og, ALU.mult, [Rog], [Rog2])
                    MM(banks[sbk][:, :], ones[:], og2, True, True, [FULL("ones", 512), Rog2], [R_bank(sbk)], inc=True)
                    rstd_from_ms(banks[sbk][:, :], rs, 1.0 / 128.0, [R_bank(sbk)], [Rrs])
                    DVE_stt(yT[:, 8 + h, :], og, par[:, 16 + h:17 + h], rs, ALU.mult, ALU.mult, [Rog, Rrs, Rp], [R_yT(8 + h, 9 + h)])

            proj_q(0)
            proj_f(0)
            pool_proj(0)
            proj_q(1)
            proj_f(1)
            pool_proj(1)
            proj_g(0)
            proj_g(1)
            proj_v(0)
            proj_v(1)
            drain_all()
            pool_mm()
            hgrn_chunks_all()
            if STAGE == 'mixer' and ti == NT - 1:
                DMA(dbg_d, hy[:, 8192:16384], [R_yT(0, 16)], [("dbg", 0, 1)], "dbg")

            def post_norm_residual(src_ap_fn, R_src_fn, gi, store):
                for s in range(4):
                    ms = tiny[:, 24 + s:25 + s]
                    rs = tiny[:, 28 + s:29 + s]
                    DVE_rsum(ms, tiny[:, 8 + s * 4:12 + s * 4], [R_tiny(8 + s * 4, 12 + s * 4)], [R_tiny(24 + s, 25 + s)])
                    rstd_from_ms(ms, rs, 1.0, [R_tiny(24 + s, 25 + s)], [R_tiny(28 + s, 29 + s)])
                    src = src_ap_fn(s)
                    if not store:
                        DVE_stt(xbuf[:, s, :], src, rs, xbuf[:, s, :], ALU.mult, ALU.add,
                                [R_src_fn(s), R_tiny(28 + s, 29 + s), R_x(s)], [R_x(s)])
                    else:
                        DVE_stt(src, src, rs, xbuf[:, s, :], ALU.mult, ALU.add,
                                [R_src_fn(s), R_tiny(28 + s, 29 + s), R_x(s)], [R_src_fn(s)])
                        DMA(out_d[tok0 + s * 128:tok0 + (s + 1) * 128, :], src, [R_src_fn(s)],
                            [("out", tok0 + s * 128, tok0 + (s + 1) * 128)], "st%d" % s)

            def evac_tokmajor(b, dst_ap, R_dst, s, cg, junk, R_junk, gi):
                DVE_tt(dst_ap, banks[b][:, :], gpost[:, gi, cg * 512:(cg + 1) * 512], ALU.mult,
                       [R_bank(b), ("gpost", gi * 8192, (gi + 1) * 8192)], [R_dst])
                ACT(junk[:, 0:512], banks[b][:, :], AF.Square, [R_bank(b)], [R_junk, R_tiny(8 + s * 4 + cg, 9 + s * 4 + cg)],
                    scale=float(D) ** -0.5, accum_out=tiny[:, 8 + s * 4 + cg:9 + s * 4 + cg])

            for cg in range(4):
                wv, wr = next_piece(PIECE_WOUT + cg)
                for s in range(4):
                    b = next_bank()
                    for c in range(16):
                        MM(banks[b][:, :], yT[:, c, s * 128:(s + 1) * 128], wv[:, c, :], c == 0, c == 15,
                           [wr, R_yT(c, c + 1)], [R_bank(b)], inc=(c == 15))
                    evac_tokmajor(b, mix_all[:, s, cg * 512:(cg + 1) * 512], R_mix(s), s, cg, junkA, R_junkA, 0)
            post_norm_residual(lambda s: mix_all[:, s, :], R_mix, 0, STAGE == 'mixer')
            if STAGE == 'mixer':
                for _p in range(32):
                    state['cur'] += 1
                continue

            prenorm_transpose(56, hT, R_hT)
            for p in range(16):
                wv, wr = next_piece(PIECE_MI + p)
                for sl in range(4):
                    b = next_bank()
                    for c in range(16):
                        MM(banks[b][:, :], wv[:, c, sl * 128:(sl + 1) * 128], hT[:, c, :], c == 0, c == 15,
                           [wr, R_hT(c, c + 1)], [R_bank(b)], inc=(c == 15))
                    k = (p * 4 + sl) % 2
                    rt, Rrt = av(65600 + k * 2112, 512, F32, 4), AR(65600 + k * 2112, 2048)
                    ACT(rt, banks[b][:, :], AF.Relu, [R_bank(b)], [Rrt])
                    hc = p * 4 + sl
                    POOL_tt(hid[:, hc, :], rt, rt, ALU.mult, [Rrt], [R_hid(hc, hc + 1)])
            for cg in range(4):
                for jp in range(4):
                    wv, wr = next_piece(PIECE_MO + cg * 4 + jp)
                    for s in range(4):
                        for c in range(16):
                            hc = jp * 16 + c
                            MM(banks[s][:, :], hid[:, hc, s * 128:(s + 1) * 128], wv[:, c, :],
                               (jp == 0 and c == 0), (jp == 3 and c == 15),
                               [wr, R_hid(hc, hc + 1)], [R_bank(s)], inc=(c == 15))
                for s in range(4):
                    evac_tokmajor(s, ffv[:, s, cg * 512:(cg + 1) * 512], R_ff(s), s, cg, junkB, R_junkB, 1)
            post_norm_residual(lambda s: ffv[:, s, :], R_ff, 1, True)

        keys = set()
        for e in ENGS:
            for waits, fn, incinfo in P.streams[e]:
                for (k, v) in waits:
                    keys.add(k)
                if incinfo:
                    keys.add(incinfo[0])
        for k in sorted(keys):
            getsem(k)
        final_waits = [(k, v) for k, v in P.dmacnt.items() if k.startswith("dma:st") or k.startswith("dma:dbg") or k.startswith("dma:flyst")]

        with nc.allow_low_precision(reason="bf16 matmul operands, fp32 accumulation"), nc.Block() as block:
            def runner(name, extra=()):
                stream = P.streams[name]

                def f(e):
                    for waits, fn, incinfo in stream:
                        for (k, v) in waits:
                            e.wait_ge(getsem(k), v)
                        ins = fn(e)
                        if incinfo:
                            ins.then_inc(getsem(incinfo[0]), incinfo[1])
                    for (k, v) in extra:
                        e.wait_ge(getsem(k), v)
                return f
            block.sync(runner("sp", final_waits))
            block.tensor(runner("pe"))
            block.scalar(runner("act"))
            block.vector(runner("dve"))
            block.gpsimd(runner("pool"))
    return nc, P


_CACHE = {}


def _consts():
    identb = np.eye(128, dtype=np.float32).astype(ml_dtypes.bfloat16)
    identf = np.eye(128, dtype=np.float32)
    s = np.arange(128)[:, None]
    t = np.arange(128)[None, :]
    maskA = ((s // 64 == t // 64) & (s <= t)).astype(np.float32)
    invc = np.zeros((128, 64), np.float32)
    for wi, w in enumerate((2, 4, 8, 16)):
        invc[:, wi * 16:(wi + 1) * 16] = 1.0 / np.minimum(np.arange(16) + 1, w).astype(np.float32)
    return {"c_identb": identb, "c_identf": identf, "c_maskA": maskA, "c_invc": invc}


def _run(inputs, n_cores, nseq, seq, stage=None):
    key = (nseq, seq, stage)
    if key not in _CACHE:
        _CACHE[key] = build_program(nseq, seq, stage)[0]
    nc = _CACHE[key]
    f32 = lambda a: np.ascontiguousarray(np.asarray(a, dtype=np.float32))
    x = f32(inputs["x"])
    shared = {
        "w_in": f32(inputs["w_in"]).reshape(D, INW),
        "pool_w": f32(inputs["pool_w"]).reshape(4, 256, 256),
        "pool_b": f32(inputs["pool_b"]).reshape(8, 128),
        "pool_scale": f32(inputs["pool_scale"]).reshape(8, 128),
        "lb_logits": f32(inputs["lb_logits"]).reshape(16, 128),
        "hgrn_norm_w": f32(inputs["hgrn_norm_w"]).reshape(8, 128),
        "w_out": f32(inputs["w_out"]).reshape(D, D),
        "norm_mix_pre": f32(inputs["norm_mix_pre"]).reshape(16, 128),
        "norm_mix_post": f32(inputs["norm_mix_post"]).reshape(1, D),
        "norm_mlp_pre": f32(inputs["norm_mlp_pre"]).reshape(16, 128),
        "norm_mlp_post": f32(inputs["norm_mlp_post"]).reshape(1, D),
        "w_mlp_in": f32(inputs["w_mlp_in"]).reshape(D, DFF),
        "w_mlp_out": f32(inputs["w_mlp_out"]).reshape(DFF, D),
    }
    shared.update(_consts())
    in_maps = []
    for c in range(n_cores):
        m = dict(shared)
        m["x"] = np.ascontiguousarray(x[c * nseq:(c + 1) * nseq].reshape(nseq * seq, D))
        in_maps.append(m)
    res = run_bass_kernel_spmd(nc, in_maps, core_ids=list(range(n_cores)))
    if stage == 'mixer':
        global DBG
        DBG = [np.asarray(r["dbg"]) for r in res.results]
    outs = [np.asarray(r["out"], dtype=np.float32).reshape(nseq, seq, D) for r in res.results]
    return np.concatenate(outs, axis=0)


def kernel(**inputs):
    x = inputs["x"]
    B, S_, _ = x.shape
    return _run(inputs, N_CORES, B // N_CORES, S_)
```

```python
import contextlib
import numpy as np
import ml_dtypes
import concourse.bass as bass
import concourse.mybir as mybir
from concourse.bass_utils import run_bass_kernel_spmd

F32 = mybir.dt.float32
BF16 = mybir.dt.bfloat16
U8 = mybir.dt.uint8
AF = mybir.ActivationFunctionType
ALU = mybir.AluOpType

D = 2048
DFF = 8192
INW = 5120
TT = 512
EPS = 1e-6
N_CORES = 8
ENGS = ("pe", "act", "dve", "pool", "sp")


class Prog:
    def __init__(self):
        self.streams = {e: [] for e in ENGS}
        self.cnt = {e: 0 for e in ENGS}
        self.waited = {e: {} for e in ENGS}
        self.regions = {}
        self.dmacnt = {}

    def _gather(self, deps, key, lo, hi, is_write):
        lst = self.regions.setdefault(key, [])
        for ent in lst:
            if ent[0] < hi and lo < ent[1]:
                w = ent[2]
                if w is not None:
                    if deps.get(w[0], 0) < w[1]:
                        deps[w[0]] = w[1]
                if is_write:
                    for k, v in ent[3].items():
                        if deps.get(k, 0) < v:
                            deps[k] = v

    def _record(self, key, lo, hi, tok, is_write):
        lst = self.regions.setdefault(key, [])
        if is_write:
            lst[:] = [e for e in lst if not (lo <= e[0] and e[1] <= hi)]
            lst.append([lo, hi, tok, {}])
        else:
            for ent in lst:
                if ent[0] == lo and ent[1] == hi:
                    if ent[3].get(tok[0], 0) < tok[1]:
                        ent[3][tok[0]] = tok[1]
                    return
            lst.append([lo, hi, None, {tok[0]: tok[1]}])

    def emit(self, eng, fn, reads=(), writes=(), inc=True, dma_slot=None):
        deps = {}
        for (k, lo, hi) in reads:
            self._gather(deps, k, lo, hi, k.startswith("ps"))
        for (k, lo, hi) in writes:
            self._gather(deps, k, lo, hi, True)
        if eng == "pe":
            deps.pop("pe", None)
        waits = []
        wd = self.waited[eng]
        for k, v in deps.items():
            if wd.get(k, 0) < v:
                wd[k] = v
                waits.append((k, v))
        if dma_slot is not None:
            sk = "dma:" + dma_slot
            self.dmacnt[sk] = self.dmacnt.get(sk, 0) + 16
            tok = (sk, self.dmacnt[sk])
            incinfo = (sk, 16)
        else:
            if inc:
                self.cnt[eng] += 1
                tok = (eng, self.cnt[eng])
                incinfo = (eng, 1)
            else:
                tok = (eng, self.cnt[eng] + 1)
                incinfo = None
        for (k, lo, hi) in reads:
            self._record(k, lo, hi, tok, False)
        for (k, lo, hi) in writes:
            self._record(k, lo, hi, tok, True)
        self.streams[eng].append((waits, fn, incinfo))
        return tok


def build_program(NSEQ, SEQ, STAGE=None):
    NTOK = NSEQ * SEQ
    TPS = SEQ // TT
    NT = NSEQ * TPS
    nc = bass.Bass("TRN2", target_bir_lowering=False)
    P = Prog()

    def din(name, shape, dt=F32):
        return nc.dram_tensor(name, list(shape), dt, kind="ExternalInput").ap()

    x_d = din("x", [NTOK, D])
    w_in_d = din("w_in", [D, INW])
    pool_w_d = din("pool_w", [4, 256, 256])
    pool_b_d = din("pool_b", [8, 128])
    pool_s_d = din("pool_scale", [8, 128])
    lbl_d = din("lb_logits", [16, 128])
    nw_d = din("hgrn_norm_w", [8, 128])
    w_out_d = din("w_out", [D, D])
    g_pre1_d = din("norm_mix_pre", [16, 128])
    g_post1_d = din("norm_mix_post", [1, D])
    g_pre2_d = din("norm_mlp_pre", [16, 128])
    g_post2_d = din("norm_mlp_post", [1, D])
    w_mi_d = din("w_mlp_in", [D, DFF])
    w_mo_d = din("w_mlp_out", [DFF, D])
    identb_d = din("c_identb", [128, 128], BF16)
    identf_d = din("c_identf", [128, 128])
    maskA_d = din("c_maskA", [128, 128])
    invc_d = din("c_invc", [128, 64])
    out_d = nc.dram_tensor("out", [NTOK, D], F32, kind="ExternalOutput").ap()
    dbg_d = nc.dram_tensor("dbg", [128, 8192], BF16, kind="ExternalOutput").ap() if STAGE == 'mixer' else None
    NPIECE = 10 + 4 + 16 + 16
    wscr = nc.dram_tensor("wscr", [NPIECE, 128, 16 * 512], BF16).ap()

    es = contextlib.ExitStack()
    with es:
        def sb(name, shape, dt):
            return es.enter_context(nc.sbuf_tensor(name, list(shape), dt))

        identb = sb("identb", [128, 128], BF16)
        identf = sb("identf", [128, 128], F32)
        maskA = sb("maskA", [128, 128], F32)
        ones = sb("ones", [128, 128], F32)
        smask = sb("smask", [128, 512], F32)
        invc = sb("invc", [128, 64], F32)
        prow = sb("prow", [72, 128], F32)
        par = sb("par", [128, 72], F32)
        par2 = sb("par2", [128, 40], F32)
        gpost = sb("gpost", [128, 2, D], F32)
        poolw = sb("poolw", [128, 8, 256], BF16)
        S = sb("S", [128, 8, 128], F32)
        Sb = sb("Sb", [128, 8, 128], BF16)
        halo = sb("halo", [128, 8, 16], F32)
        xbuf = sb("xbuf", [128, 4, D], F32)
        hy = sb("hy", [128, 2 * 16 * 512], BF16)
        ring = sb("ring", [128, 2, 16 * 512], BF16)
        tiny = sb("tiny", [128, 64], F32)
        dec_t = sb("dec_t", [128, 64], F32)
        ARENA = 78016 + 2112
        arena = sb("arena", [128, ARENA], U8)
        banks = [es.enter_context(nc.psum_tensor("bank%d" % b, [128, 512], F32)) for b in range(8)]

        sem_names = {}

        def getsem(key):
            if key not in sem_names:
                sem_names[key] = es.enter_context(nc.semaphore("s_" + key.replace(":", "_")))
            return sem_names[key]

        def av(off, n_elem, dt, esz):
            return arena[:, off:off + n_elem * esz].bitcast(dt)

        def AR(off, nbytes):
            return ("arena", off, off + nbytes)

        hT = hy[:, 0:8192].rearrange("p (c t) -> p c t", t=512)
        yT = hy[:, 8192:16384].rearrange("p (c t) -> p c t", t=512)
        ffv = hy[:].bitcast(F32).rearrange("p (s n) -> p s n", n=D)

        def R_hT(c0=0, c1=16):
            return ("hy", c0 * 1024, c1 * 1024)

        def R_yT(c0=0, c1=16):
            return ("hy", 16384 + c0 * 1024, 16384 + c1 * 1024)

        def R_ff(s):
            return ("hy", s * 8192, (s + 1) * 8192)

        def R_x(s):
            return ("xbuf", s * 8192, (s + 1) * 8192)

        def R_bank(b, lo=0, hi=2048):
            return ("ps%d" % b, 0, 2048)

        def R_ring(s):
            return ("ring", s * 16384, (s + 1) * 16384)

        def R_tiny(c0, c1):
            return ("tiny", c0 * 4, c1 * 4)

        HB = 5120
        def qh_v(h): return av(h * HB, 512, BF16, 2), AR(h * HB, 1024)
        def kt_v(h): return av(h * HB + 1024, 512, BF16, 2), AR(h * HB + 1024, 1024)
        def khtok_v(h): return av(h * HB + 2048, 512, BF16, 2).rearrange("p (s d) -> p s d", d=128), AR(h * HB + 2048, 1024)
        def v_v(h): return av(h * HB + 3072, 512, BF16, 2).rearrange("p (s d) -> p s d", d=128), AR(h * HB + 3072, 1024)
        def rg_v(h): return av(h * HB + 4096, 512, BF16, 2), AR(h * HB + 4096, 1024)
        def og_v(i): return av(40960 + i * 2048, 512, F32, 4), AR(40960 + i * 2048, 2048)
        def T_v(i): return av(49152 + i * 2048, 512, F32, 4), AR(49152 + i * 2048, 2048)
        khT_v = (av(61440, 512, BF16, 2), AR(61440, 1024))
        og2_v = T_v(0)
        rstd_v = T_v(1)
        def Am_v(i): return av(62464 + i * 256, 128, BF16, 2), AR(62464 + i * 256, 256)
        ubuf_v = (av(63488, 528, F32, 4), AR(63488, 2112))
        sA_v = (av(65600, 528, F32, 4), AR(65600, 2112))
        sB_v = (av(67712, 528, F32, 4), AR(67712, 2112))
        def dT_v(c): return av(69824 + c * 1024, 512, BF16, 2), AR(69824 + c * 1024, 1024)
        mix_all = av(0, 4 * D, F32, 4).rearrange("p (s n) -> p s n", n=D)
        def R_mix(s): return AR(s * 8192, 8192)
        hid = av(0, 64 * 512, BF16, 2).rearrange("p (c t) -> p c t", t=512)
        def R_hid(c0, c1): return AR(c0 * 1024, (c1 - c0) * 1024)
        junkA = av(57344, 2048, BF16, 2)
        R_junkA = AR(57344, 4096)
        junkB = av(69824, 512, BF16, 2)
        R_junkB = AR(69824, 1024)

        def ACT(out, in_, func, reads, writes, **kw):
            P.emit("act", lambda e: e.activation(out=out, in_=in_, func=func, **kw), reads, writes)

        def DVE_tt(out, in0, in1, op, reads, writes):
            P.emit("dve", lambda e: e.tensor_tensor(out=out, in0=in0, in1=in1, op=op), reads, writes)

        def DVE_ts(out, in0, s1, s2, op0, op1, reads, writes):
            if op1 is None:
                P.emit("dve", lambda e: e.tensor_scalar(out=out, in0=in0, scalar1=s1, scalar2=None, op0=op0), reads, writes)
            else:
                P.emit("dve", lambda e: e.tensor_scalar(out=out, in0=in0, scalar1=s1, scalar2=s2, op0=op0, op1=op1), reads, writes)

        def DVE_stt(out, in0, scalar, in1, op0, op1, reads, writes):
            P.emit("dve", lambda e: e.scalar_tensor_tensor(out=out, in0=in0, scalar=scalar, in1=in1, op0=op0, op1=op1), reads, writes)

        def DVE_copy(out, in_, reads, writes):
            P.emit("dve", lambda e: e.tensor_copy(out=out, in_=in_), reads, writes)

        def DVE_recip(out, in_, reads, writes):
            P.emit("dve", lambda e: e.reciprocal(out=out, in_=in_), reads, writes)

        def DVE_scan(out, d0, d1, reads, writes):
            P.emit("dve", lambda e: e.tensor_tensor_scan(out=out, data0=d0, data1=d1, initial=0.0, op0=ALU.mult, op1=ALU.add), reads, writes)

        def DVE_rsum(out, in_, reads, writes):
            P.emit("dve", lambda e: e.reduce_sum(out=out, in_=in_, axis=mybir.AxisListType.X), reads, writes)

        def ENG_memset(eng, ap, val, writes):
            P.emit(eng, lambda e: e.memset(ap, val), (), writes)

        def POOL_tt(out, in0, in1, op, reads, writes):
            P.emit("pool", lambda e: e.tensor_tensor(out=out, in0=in0, in1=in1, op=op), reads, writes)

        def POOL_copy(out, in_, reads, writes):
            P.emit("pool", lambda e: e.tensor_copy(out=out, in_=in_), reads, writes)

        def MM(out, lhsT, rhs, start, stop, reads, writes, inc):
            P.emit("pe", lambda e: e.matmul(out, lhsT=lhsT, rhs=rhs, start=start, stop=stop), reads, writes, inc=inc)

        def TR(out, in_, ident, reads, writes, inc):
            P.emit("pe", lambda e: e.transpose(out=out, in_=in_, identity=ident), reads, writes, inc=inc)

        def DMA(out, in_, reads, writes, slot):
            P.emit("sp", lambda e: e.dma_start(out=out, in_=in_), reads, writes, dma_slot=slot)

        def FULL(name, nbytes):
            return (name, 0, nbytes)

        DMA(identb[:], identb_d, (), [FULL("identb", 256)], "c0")
        DMA(identf[:], identf_d, (), [FULL("identf", 512)], "c1")
        DMA(maskA[:], maskA_d, (), [FULL("maskA", 512)], "c2")
        DMA(invc[:], invc_d, (), [FULL("invc", 256)], "c3")
        DMA(prow[0:16, :], lbl_d, (), [("prow", 0, 1)], "c4")
        DMA(prow[16:24, :], nw_d, (), [("prow", 1, 2)], "c5")
        DMA(prow[24:32, :], pool_b_d, (), [("prow", 2, 3)], "c6")
        DMA(prow[32:40, :], pool_s_d, (), [("prow", 3, 4)], "c7")
        DMA(prow[40:56, :], g_pre1_d, (), [("prow", 4, 5)], "c8")
        DMA(prow[56:72, :], g_pre2_d, (), [("prow", 5, 6)], "c9")
        DMA(gpost[:, 0, :], g_post1_d.partition_broadcast(128), (), [("gpost", 0, 8192)], "c10")
        DMA(gpost[:, 1, :], g_post2_d.partition_broadcast(128), (), [("gpost", 8192, 16384)], "c11")
        ENG_memset("pool", ones[:], 1.0, [FULL("ones", 512)])
        ENG_memset("pool", smask[:], 1.0, [FULL("smask", 2048)])
        ENG_memset("pool", smask[:].rearrange("p (c t) -> p c t", t=64)[:, :, 0:1], 0.0, [FULL("smask", 2048)])
        TR(banks[4][0:128, 0:72], prow[:, :], identf[0:72, 0:72], [("prow", 0, 6), FULL("identf", 512)], [R_bank(4)], True)
        DVE_copy(par[:], banks[4][:, 0:72], [R_bank(4)], [FULL("par", 288)])
        Rp = FULL("par", 288)
        Rp2 = FULL("par2", 160)
        DVE_tt(par2[:, 32:40], par[:, 8:16], par[:, 0:8], ALU.subtract, [Rp], [Rp2])
        ACT(par2[:, 32:40], par2[:, 32:40], AF.Exp, [Rp2], [Rp2])
        DVE_ts(par2[:, 32:40], par2[:, 32:40], 1.0, None, ALU.add, None, [Rp2], [Rp2])
        DVE_recip(par2[:, 0:8], par2[:, 32:40], [Rp2], [Rp2])
        DVE_ts(par2[:, 8:16], par2[:, 0:8], -1.0, 1.0, ALU.mult, ALU.add, [Rp2], [Rp2])
        DVE_ts(par2[:, 16:24], par2[:, 8:16], -1.0, None, ALU.mult, None, [Rp2], [Rp2])
        DVE_tt(par2[:, 24:32], par[:, 24:32], par[:, 32:40], ALU.mult, [Rp, Rp2], [Rp2])

        def stg_v(i):
            return av(i * 16384, 4096, F32, 4).rearrange("p (c n) -> p c n", n=512), AR(i * 16384, 16384)

        def cvt_v(i):
            return ring[:, i // 2, (i % 2) * 4096:(i % 2) * 4096 + 4096].rearrange("p (c n) -> p c n", n=512), ("ring", i * 8192, (i + 1) * 8192)

        pieces_src = []
        for p in range(10):
            pieces_src.append(w_in_d[:, p * 512:(p + 1) * 512])
        for p in range(4):
            pieces_src.append(w_out_d[:, p * 512:(p + 1) * 512])
        for p in range(16):
            pieces_src.append(w_mi_d[:, p * 512:(p + 1) * 512])
        for cg in range(4):
            for jp in range(4):
                pieces_src.append(w_mo_d[jp * 2048:(jp + 1) * 2048, cg * 512:(cg + 1) * 512])
        PIECE_WOUT = 10
        PIECE_MI = 14
        PIECE_MO = 30
        cv_engs = ["act", "dve", "pool"]
        N_PRO = PIECE_MI
        units = [(pi, half) for pi in range(N_PRO) for half in range(2)]

        def unit_load(u):
            pi, half = units[u]
            sl = u % 4
            sv_ap, sv_r = stg_v(sl)
            srcv = pieces_src[pi][half * 1024:(half + 1) * 1024, :].rearrange("(c p) n -> p c n", p=128)
            DMA(sv_ap, srcv, (), [sv_r], "stg%d" % sl)

        LOOKAHEAD = 3
        for u in range(min(LOOKAHEAD, len(units))):
            unit_load(u)
        for u, (pi, half) in enumerate(units):
            if u + LOOKAHEAD < len(units):
                unit_load(u + LOOKAHEAD)
            sl = u % 4
            sv_ap, sv_r = stg_v(sl)
            cv_ap, cv_r = cvt_v(sl)
            eng = cv_engs[u % 3]
            if eng == "act":
                ACT(cv_ap, sv_ap, AF.Copy, [sv_r], [cv_r])
            elif eng == "dve":
                DVE_copy(cv_ap, sv_ap, [sv_r], [cv_r])
            else:
                POOL_copy(cv_ap, sv_ap, [sv_r], [cv_r])
            dst = wscr[pi, :, half * 4096:(half + 1) * 4096].rearrange("p (c n) -> p c n", n=512)
            DMA(dst, cv_ap, [cv_r], [("wscr", pi * 2 + half, pi * 2 + half + 1)], "cvt%d" % sl)
        pw_stage = av(65536, 2048, F32, 4).rearrange("p (a n) -> p a n", n=256)
        R_pws = AR(65536, 8192)
        DMA(pw_stage, pool_w_d.rearrange("j (kc p) n -> p (j kc) n", p=128), (), [R_pws], "c12")
        DVE_copy(poolw[:], pw_stage, [R_pws], [FULL("poolw", 4096)])

        sched = []
        tile_order = [2, 4, 0, 3, 5, 1, 8, 9, 6, 7] + list(range(PIECE_WOUT, PIECE_WOUT + 4)) + \
            list(range(PIECE_MI, PIECE_MI + 16)) + list(range(PIECE_MO, PIECE_MO + 16))
        for t in range(NT):
            sched.extend(tile_order)
        state = {"loaded": 0, "cur": 0, "stg": 0}

        def ensure_loaded(upto):
            while state["loaded"] <= upto and state["loaded"] < len(sched):
                i = state["loaded"]
                pi = sched[i]
                sl = i % 2
                if pi >= N_PRO and i < len(tile_order):
                    rv = ring[:, sl, :].rearrange("p (c n) -> p c n", n=512)
                    rel = i - tile_order.index(PIECE_MI)
                    nslots = 12 if 1 <= rel <= 20 else 4
                    for c in range(16):
                        k = state["stg"] % nslots
                        state["stg"] += 1
                        if k < 4:
                            st_ap, st_r = av(70848 + k * 2048, 512, F32, 4), AR(70848 + k * 2048, 2048)
                        else:
                            o = 16384 + (k - 4) * 2048
                            st_ap, st_r = hy[:, o // 2:(o + 2048) // 2].bitcast(F32), ("hy", o, o + 2048)
                        DMA(st_ap, pieces_src[pi][c * 128:(c + 1) * 128, :], (), [st_r], "fly%d" % k)
                        DVE_copy(rv[:, c, :], st_ap, [st_r], [("ring", sl * 16384 + c * 1024, sl * 16384 + (c + 1) * 1024)])
                    DMA(wscr[pi, :, :], ring[:, sl, :], [R_ring(sl)], [("wscr", pi * 2, pi * 2 + 2)], "flyst%d" % sl)
                else:
                    DMA(ring[:, sl, :], wscr[pi, :, :], [("wscr", pi * 2, pi * 2 + 2)], [R_ring(sl)], "ring%d" % sl)
                state["loaded"] += 1

        def next_piece(expect):
            i = state["cur"]
            assert sched[i] == expect, (sched[i], expect)
            ensure_loaded(i + 1)
            state["cur"] += 1
            sl = i % 2
            return ring[:, sl, :].rearrange("p (c n) -> p c n", n=512), R_ring(sl)

        bigbank = {"i": 0}

        def next_bank():
            b = bigbank["i"] % 4
            bigbank["i"] += 1
            return b

        def rstd_from_ms(ms_ap, out_ap, scale, rd, wr):
            ACT(out_ap, ms_ap, AF.Ln, rd, wr, bias=EPS, scale=scale)
            ACT(out_ap, out_ap, AF.Exp, wr, wr, scale=-0.5)

        def prenorm_transpose(gcol0, dstT, R_dst):
            for s in range(4):
                ms = tiny[:, s:s + 1]
                rs = tiny[:, 4 + s:5 + s]
                ACT(junkA, xbuf[:, s, :], AF.Square, [R_x(s)], [R_junkA, R_tiny(s, s + 1)],
                    scale=float(D) ** -0.5, accum_out=ms)
                rstd_from_ms(ms, rs, 1.0, [R_tiny(s, s + 1)], [R_tiny(4 + s, 5 + s)])
                xn, R_xn = av(49152 + (s % 2) * 4096, 2048, BF16, 2), AR(49152 + (s % 2) * 4096, 4096)
                DVE_ts(xn, xbuf[:, s, :], rs, None, ALU.mult, None, [R_x(s), R_tiny(4 + s, 5 + s)], [R_xn])
                for half in range(2):
                    b = next_bank()
                    pb = banks[b][:].bitcast(BF16)
                    for j in range(8):
                        c = half * 8 + j
                        TR(pb[:, j * 128:(j + 1) * 128], xn[:, c * 128:(c + 1) * 128], identb[:],
                           [R_xn, FULL("identb", 256)], [R_bank(b)], inc=(j == 7))
                    gain = par[:, gcol0 + half * 8:gcol0 + half * 8 + 8].unsqueeze(2).broadcast_to([128, 8, 128])
                    DVE_tt(dstT[:, half * 8:half * 8 + 8, s * 128:(s + 1) * 128],
                           pb.rearrange("p (c t) -> p c t", t=128), gain, ALU.mult,
                           [R_bank(b), Rp], [R_dst(half * 8, half * 8 + 8)])

        dec3 = dec_t[:, :].rearrange("p (h c) -> p h c", c=8)

        def R_dec(h):
            return ("dec_t", h * 32, h * 32 + 32)

        def R_S(h):
            return ("S", h * 512, (h + 1) * 512)

        def R_Sb(h):
            return ("Sb", h * 256, (h + 1) * 256)

        R_ident = FULL("identb", 256)
        WIN = [2, 2, 4, 4, 8, 8, 16, 16]

        for ti in range(NT):
            pos = ti % TPS
            first = (pos == 0)
            tok0 = ti * TT
            for s in range(4):
                DMA(xbuf[:, s, :], x_d[tok0 + s * 128:tok0 + (s + 1) * 128, :], (), [R_x(s)], "x%d" % s)
            if first:
                ENG_memset("pool", S[:], 0.0, [FULL("S", 4096)])
                ENG_memset("pool", Sb[:], 0.0, [FULL("Sb", 2048)])
                ENG_memset("pool", halo[:], 0.0, [FULL("halo", 512)])
            def _early_store():
                for s in range(4):
                    DMA(out_d[tok0 + s * 128:tok0 + (s + 1) * 128, :], xbuf[:, s, :], [R_x(s)],
                        [("out", tok0 + s * 128, tok0 + (s + 1) * 128)], "st%d" % s)
            if STAGE == 'pro':
                _early_store()
                break
            prenorm_transpose(40, hT, R_hT)
            if STAGE == 'A':
                _early_store()
                break

            from collections import deque
            q_main = deque()
            q_pe = deque()
            slabctr = {"n": 0}

            def drain(n):
                for _ in range(n):
                    if not q_main:
                        break
                    q_main.popleft()()
                while q_pe and q_pe[0][0] <= slabctr["n"]:
                    q_pe.popleft()[1]()

            def drain_all():
                while q_main or q_pe:
                    while q_main:
                        q_main.popleft()()
                    while q_pe:
                        q_pe.popleft()[1]()

            def proj_ws(piece_idx, evac, ndrain=5):
                wv, wr = next_piece(piece_idx)
                for sl in range(4):
                    b = next_bank()
                    for c in range(16):
                        MM(banks[b][:, :], wv[:, c, sl * 128:(sl + 1) * 128], hT[:, c, :], c == 0, c == 15,
                           [wr, R_hT(c, c + 1)], [R_bank(b)], inc=(c == 15))
                    evac(sl, b)
                    slabctr["n"] += 1
                    drain(ndrain)

            def ef_v(h): return av(h * HB + 1024, 512, F32, 4), AR(h * HB + 1024, 2048)
            def eg_v(h): return av(h * HB + 3072, 512, F32, 4), AR(h * HB + 3072, 2048)

            def proj_q(G):
                hs = [4 * G + i for i in range(4)]

                def evac_q(sl, b):
                    qh, Rq = qh_v(hs[sl])
                    ACT(qh, banks[b][:, :], AF.Copy, [R_bank(b)], [Rq])
                proj_ws(2 + G, evac_q)

            def proj_f(G):
                hs = [4 * G + i for i in range(4)]

                def evac_f(sl, b):
                    h = hs[sl]
                    ef, Ref = ef_v(h)
                    ACT(ef, banks[b][:, :], AF.Exp, [R_bank(b)], [Ref], scale=-1.0)
                    T1, R1 = T_v(0); T2, R2 = T_v(1); T3, R3 = T_v(2); T4, R4 = T_v(3); T5, R5 = T_v(4)
                    qh, Rq = qh_v(h)
                    kt, Rkt = kt_v(h)
                    khT, RkhT = av(40960 + (h % 4) * 2048, 512, BF16, 2), AR(40960 + (h % 4) * 2048, 1024)
                    kh, Rkh = khtok_v(h)
                    ch = [
                        lambda: ACT(T1, ef, AF.Ln, [Ref], [R1], bias=1.0),
                        lambda: ACT(T2, T1, AF.Exp, [R1], [R2], scale=-1.0),
                        lambda: ACT(T3, T2, AF.Ln, [R2, Rp2], [R3], scale=par2[:, 8 + h:9 + h], bias=par2[:, h:h + 1]),
                        lambda: DVE_ts(T4, T2, par2[:, 16 + h:17 + h], par2[:, 8 + h:9 + h], ALU.mult, ALU.add, [R2, Rp2], [R4]),
                        lambda: DVE_scan(T1, smask[:], T3, [R3, FULL("smask", 2048)], [R1]),
                        lambda: ACT(T3, T1, AF.Exp, [R1], [R3]),
                        lambda: ACT(T5, T1, AF.Exp, [R1], [R5], scale=-1.0),
                        lambda: DVE_copy(dec3[:, h, :], T3.rearrange("p (c t) -> p c t", t=64)[:, :, 63], [R3], [R_dec(h)]),
                        lambda: DVE_tt(qh, qh, T3, ALU.mult, [Rq, R3], [Rq]),
                        lambda: DVE_tt(T2, T4, T5, ALU.mult, [R4, R5], [R2]),
                        lambda: POOL_copy(kt, T2, [R2], [Rkt]),
                        lambda: DVE_tt(khT.rearrange("p (c t) -> p c t", t=64), T2.rearrange("p (c t) -> p c t", t=64),
                                       dec3[:, h, :].unsqueeze(2).broadcast_to([128, 8, 64]), ALU.mult, [R2, R_dec(h)], [RkhT]),
                    ]

                    def pe_part():
                        pb = banks[4][:].bitcast(BF16)
                        for p in range(4):
                            TR(pb[:, p * 128:(p + 1) * 128], khT[:, p * 128:(p + 1) * 128], identb[:],
                               [RkhT, R_ident], [R_bank(4)], inc=(p == 3))
                        ACT(kh, pb[:, 0:512].rearrange("p (s d) -> p s d", d=128), AF.Copy, [R_bank(4)], [Rkh])

                    def last():
                        q_pe.append((slabctr["n"] + 3, pe_part))
                    q_main.extend(ch)
                    q_main.append(last)
                proj_ws(4 + G, evac_f)

            def proj_g(G):
                hs = [4 * G + i for i in range(4)]

                def evac_g(sl, b):
                    h = hs[sl]
                    eg, Reg = eg_v(h)
                    T6, R6 = T_v(5)
                    rg, Rrg = rg_v(h)
                    ACT(eg, banks[b][:, :], AF.Exp, [R_bank(b)], [Reg], scale=-1.0)
                    q_main.appendleft(lambda: ACT(rg, T6, AF.Exp, [R6], [Rrg], scale=-1.0))
                    q_main.appendleft(lambda: ACT(T6, eg, AF.Ln, [Reg], [R6], bias=1.0))
                proj_ws(8 + G, evac_g)

            def proj_v(G):
                wv, wr = next_piece(6 + G)
                off0 = (4 * G) * HB + 3072
                for s in range(4):
                    b = next_bank()
                    for c in range(16):
                        MM(banks[b][:, :], hT[:, c, s * 128:(s + 1) * 128], wv[:, c, :], c == 0, c == 15,
                           [wr, R_hT(c, c + 1)], [R_bank(b)], inc=(c == 15))
                    off = off0 + s * 256
                    vout = arena[:, off:off + 4 * HB].bitcast(BF16).rearrange("p (h n) -> p h n", n=HB // 2)[:, :, 0:128]
                    ACT(vout, banks[b][:, :].rearrange("p (h d) -> p h d", d=128), AF.Copy, [R_bank(b)],
                        [v_v(h)[1] for h in range(4 * G, 4 * G + 4)])
                    slabctr["n"] += 1
                    drain(5)

            ub2_v = (av(ARENA - 2112, 528, F32, 4), AR(ARENA - 2112, 2112))

            def pool_proj(piece):
                def evac_u(sl, b):
                    ch = piece * 4 + sl
                    w = WIN[ch]
                    ub, Rub = ubuf_v if ch % 2 == 0 else ub2_v
                    sA, RsA = sA_v
                    sB, RsB = sB_v
                    Rh = ("halo", ch * 64, ch * 64 + 64)
                    ACT(ub[:, 16:528], banks[b][:, :], AF.Copy, [R_bank(b)], [Rub])
                    ops = [lambda: POOL_copy(ub[:, 0:16], halo[:, ch, :], [Rh], [Rub]),
                           lambda: POOL_tt(sA[:, 1:528], ub[:, 1:528], ub[:, 0:527], ALU.add, [Rub], [RsA])]
                    fin, Rfin = sA, RsA
                    if w >= 4:
                        ops.append(lambda: POOL_tt(sB[:, 3:528], sA[:, 3:528], sA[:, 1:526], ALU.add, [RsA], [RsB]))
                        fin, Rfin = sB, RsB
                    if w >= 8:
                        ops.append(lambda: POOL_tt(sA[:, 7:528], sB[:, 7:528], sB[:, 3:524], ALU.add, [RsB], [RsA]))
                        fin, Rfin = sA, RsA
                    if w >= 16:
                        ops.append(lambda: POOL_tt(sB[:, 15:528], sA[:, 15:528], sA[:, 7:520], ALU.add, [RsA], [RsB]))
                        fin, Rfin = sB, RsB
                    ops.append(lambda: POOL_copy(halo[:, ch, :], ub[:, 512:528], [Rub], [Rh]))
                    dT, RdT = dT_v(ch)
                    ops.append(lambda: DVE_stt(dT, fin[:, 16:528], 1.0 / w, ub[:, 16:528], ALU.mult, ALU.subtract, [Rfin, Rub], [RdT]))
                    if first:
                        widx = ch // 2
                        t16 = tiny[:, 32:48]
                        ops.append(lambda: POOL_tt(t16, fin[:, 16:32], invc[:, widx * 16:(widx + 1) * 16], ALU.mult,
                                                   [Rfin, FULL("invc", 256)], [R_tiny(32, 48)]))
                        ops.append(lambda: POOL_tt(dT[:, 0:16], t16, ub[:, 16:32], ALU.subtract, [R_tiny(32, 48), Rub], [RdT]))
                    for o in reversed(ops):
                        q_main.appendleft(o)
                proj_ws(piece, evac_u, ndrain=10)

            def pool_mm():
                for j in range(4):
                    for oc in range(2):
                        b = next_bank()
                        for kc in range(2):
                            dT, RdT = dT_v(2 * j + kc)
                            MM(banks[b][:, :], poolw[:, j * 2 + kc, oc * 128:(oc + 1) * 128], dT, kc == 0, kc == 1,
                               [FULL("poolw", 4096), RdT], [R_bank(b)], inc=(kc == 1))
                        ch = 2 * j + oc
                        ACT(yT[:, ch, :], banks[b][:, :], AF.Identity, [R_bank(b), Rp, Rp2], [R_yT(ch, ch + 1)],
                            scale=par[:, 32 + ch:33 + ch], bias=par2[:, 24 + ch:25 + ch])

            def Am8_v(h): return av(61440 + h * 256, 128, BF16, 2), AR(61440 + h * 256, 256)
            A_BANK = [4, 5]
            U_BANK = [6, 0, 2, 3]
            O_BANK = [7, 1]

            def hgrn_chunks_all():
                HS = list(range(8))
                for p in range(4):
                    for h in HS:
                        kt, Rkt = kt_v(h)
                        qh, Rq = qh_v(h)
                        ab = A_BANK[h // 4]
                        MM(banks[ab][:, (h % 4) * 128:(h % 4 + 1) * 128], kt[:, p * 128:(p + 1) * 128], qh[:, p * 128:(p + 1) * 128],
                           True, True, [Rkt, Rq], [R_bank(ab)], inc=(h % 4 == 3))
                    for h in HS:
                        ab = A_BANK[h // 4]
                        Am, RAm = Am8_v(h)
                        DVE_tt(Am, banks[ab][:, (h % 4) * 128:(h % 4 + 1) * 128], maskA[:], ALU.mult,
                               [R_bank(ab), FULL("maskA", 512)], [RAm])
                    for half in range(2):
                        c = 2 * p + half
                        for h in HS:
                            qh, Rq = qh_v(h)
                            vv, Rv = v_v(h)
                            kh, Rkh = khtok_v(h)
                            Am, RAm = Am8_v(h)
                            ob = O_BANK[h // 4]
                            o_lo = (h % 4) * 128 + half * 64
                            o_ap = banks[ob][:, o_lo:o_lo + 64]
                            MM(o_ap, Sb[:, h, :], qh[:, c * 64:(c + 1) * 64], True, False, [R_Sb(h), Rq], [R_bank(ob)], inc=False)
                            MM(o_ap, vv[:, p, :], Am[:, half * 64:(half + 1) * 64], False, True, [Rv, RAm], [R_bank(ob)], inc=True)
                            ub_ = U_BANK[h % 4]
                            uq = h // 4
                            MM(banks[ub_][:, uq * 128:(uq + 1) * 128], kh[64 * half:64 * half + 64, p, :],
                               vv[64 * half:64 * half + 64, p, :], True, True, [Rkh, Rv], [R_bank(ub_)], inc=True)
                        for h in HS:
                            ub_ = U_BANK[h % 4]
                            uq = h // 4
                            DVE_stt(S[:, h, :], S[:, h, :], dec3[:, h, c:c + 1], banks[ub_][:, uq * 128:(uq + 1) * 128],
                                    ALU.mult, ALU.add, [R_S(h), R_dec(h), R_bank(ub_)], [R_S(h)])
                        for h in HS:
                            ACT(Sb[:, h, :], S[:, h, :], AF.Copy, [R_S(h)], [R_Sb(h)])
                        if half == 1:
                            for h in HS:
                                ob = O_BANK[h // 4]
                                og, Rog = og_v(h)
                                rg, Rrg = rg_v(h)
                                DVE_tt(og[:, p * 128:(p + 1) * 128], banks[ob][:, (h % 4) * 128:(h % 4 + 1) * 128],
                                       rg[:, p * 128:(p + 1) * 128], ALU.mult, [R_bank(ob), Rrg], [Rog])
                tmp2 = [(T_v(4), T_v(5), 7), (ubuf_v, sA_v, 6)]
                for h in HS:
                    (og2, Rog2), (rs, Rrs), sbk = tmp2[h % 2]
                    og2 = og2[:, 0:512]
                    rs = rs[:, 0:512]
                    og, Rog = og_v(h)
                    POOL_tt(og2, og, og, ALU.mult, [Rog], [Rog2])
                    MM(banks[sbk][:, :], ones[:], og2, True, True, [FULL("ones", 512), Rog2], [R_bank(sbk)], inc=True)
                    rstd_from_ms(banks[sbk][:, :], rs, 1.0 / 128.0, [R_bank(sbk)], [Rrs])
                    DVE_stt(yT[:, 8 + h, :], og, par[:, 16 + h:17 + h], rs, ALU.mult, ALU.mult, [Rog, Rrs, Rp], [R_yT(8 + h, 9 + h)])

            proj_q(0)
            proj_f(0)
            pool_proj(0)
            proj_q(1)
            proj_f(1)
            pool_proj(1)
            proj_g(0)
            proj_g(1)
            proj_v(0)
            proj_v(1)
            drain_all()
            pool_mm()
            hgrn_chunks_all()
            if STAGE == 'mixer' and ti == NT - 1:
                DMA(dbg_d, hy[:, 8192:16384], [R_yT(0, 16)], [("dbg", 0, 1)], "dbg")

            def post_norm_residual(src_ap_fn, R_src_fn, gi, store):
                for s in range(4):
                    ms = tiny[:, 24 + s:25 + s]
                    rs = tiny[:, 28 + s:29 + s]
                    DVE_rsum(ms, tiny[:, 8 + s * 4:12 + s * 4], [R_tiny(8 + s * 4, 12 + s * 4)], [R_tiny(24 + s, 25 + s)])
                    rstd_from_ms(ms, rs, 1.0, [R_tiny(24 + s, 25 + s)], [R_tiny(28 + s, 29 + s)])
                    src = src_ap_fn(s)
                    if not store:
                        DVE_stt(xbuf[:, s, :], src, rs, xbuf[:, s, :], ALU.mult, ALU.add,
                                [R_src_fn(s), R_tiny(28 + s, 29 + s), R_x(s)], [R_x(s)])
                    else:
                        DVE_stt(src, src, rs, xbuf[:, s, :], ALU.mult, ALU.add,
                                [R_src_fn(s), R_tiny(28 + s, 29 + s), R_x(s)], [R_src_fn(s)])
                        DMA(out_d[tok0 + s * 128:tok0 + (s + 1) * 128, :], src, [R_src_fn(s)],
                            [("out", tok0 + s * 128, tok0 + (s + 1) * 128)], "st%d" % s)

            def evac_tokmajor(b, dst_ap, R_dst, s, cg, junk, R_junk, gi):
                DVE_tt(dst_ap, banks[b][:, :], gpost[:, gi, cg * 512:(cg + 1) * 512], ALU.mult,
                       [R_bank(b), ("gpost", gi * 8192, (gi + 1) * 8192)], [R_dst])
                ACT(junk[:, 0:512], banks[b][:, :], AF.Square, [R_bank(b)], [R_junk, R_tiny(8 + s * 4 + cg, 9 + s * 4 + cg)],
                    scale=float(D) ** -0.5, accum_out=tiny[:, 8 + s * 4 + cg:9 + s * 4 + cg])

            for cg in range(4):
                wv, wr = next_piece(PIECE_WOUT + cg)
                for s in range(4):
                    b = next_bank()
                    for c in range(16):
                        MM(banks[b][:, :], yT[:, c, s * 128:(s + 1) * 128], wv[:, c, :], c == 0, c == 15,
                           [wr, R_yT(c, c + 1)], [R_bank(b)], inc=(c == 15))
                    evac_tokmajor(b, mix_all[:, s, cg * 512:(cg + 1) * 512], R_mix(s), s, cg, junkA, R_junkA, 0)
            post_norm_residual(lambda s: mix_all[:, s, :], R_mix, 0, STAGE == 'mixer')
            if STAGE == 'mixer':
                for _p in range(32):
                    state['cur'] += 1
                continue

            prenorm_transpose(56, hT, R_hT)
            for p in range(16):
                wv, wr = next_piece(PIECE_MI + p)
                for sl in range(4):
                    b = next_bank()
                    for c in range(16):
                        MM(banks[b][:, :], wv[:, c, sl * 128:(sl + 1) * 128], hT[:, c, :], c == 0, c == 15,
                           [wr, R_hT(c, c + 1)], [R_bank(b)], inc=(c == 15))
                    k = (p * 4 + sl) % 2
                    rt, Rrt = av(65600 + k * 2112, 512, F32, 4), AR(65600 + k * 2112, 2048)
                    ACT(rt, banks[b][:, :], AF.Relu, [R_bank(b)], [Rrt])
                    hc = p * 4 + sl
                    POOL_tt(hid[:, hc, :], rt, rt, ALU.mult, [Rrt], [R_hid(hc, hc + 1)])
            for cg in range(4):
                for jp in range(4):
                    wv, wr = next_piece(PIECE_MO + cg * 4 + jp)
                    for s in range(4):
                        for c in range(16):
                            hc = jp * 16 + c
                            MM(banks[s][:, :], hid[:, hc, s * 128:(s + 1) * 128], wv[:, c, :],
                               (jp == 0 and c == 0), (jp == 3 and c == 15),
                               [wr, R_hid(hc, hc + 1)], [R_bank(s)], inc=(c == 15))
                for s in range(4):
                    evac_tokmajor(s, ffv[:, s, cg * 512:(cg + 1) * 512], R_ff(s), s, cg, junkB, R_junkB, 1)
            post_norm_residual(lambda s: ffv[:, s, :], R_ff, 1, True)

        keys = set()
        for e in ENGS:
            for waits, fn, incinfo in P.streams[e]:
                for (k, v) in waits:
                    keys.add(k)
                if incinfo:
                    keys.add(incinfo[0])
        for k in sorted(keys):
            getsem(k)
        final_waits = [(k, v) for k, v in P.dmacnt.items() if k.startswith("dma:st") or k.startswith("dma:dbg") or k.startswith("dma:flyst")]

        with nc.allow_low_precision(reason="bf16 matmul operands, fp32 accumulation"), nc.Block() as block:
            def runner(name, extra=()):
                stream = P.streams[name]

                def f(e):
                    for waits, fn, incinfo in stream:
                        for (k, v) in waits:
                            e.wait_ge(getsem(k), v)
                        ins = fn(e)
                        if incinfo:
                            ins.then_inc(getsem(incinfo[0]), incinfo[1])
                    for (k, v) in extra:
                        e.wait_ge(getsem(k), v)
                return f
            block.sync(runner("sp", final_waits))
            block.tensor(runner("pe"))
            block.scalar(runner("act"))
            block.vector(runner("dve"))
            block.gpsimd(runner("pool"))
    return nc, P


_CACHE = {}


def _consts():
    identb = np.eye(128, dtype=np.float32).astype(ml_dtypes.bfloat16)
    identf = np.eye(128, dtype=np.float32)
    s = np.arange(128)[:, None]
    t = np.arange(128)[None, :]
    maskA = ((s // 64 == t // 64) & (s <= t)).astype(np.float32)
    invc = np.zeros((128, 64), np.float32)
    for wi, w in enumerate((2, 4, 8, 16)):
        invc[:, wi * 16:(wi + 1) * 16] = 1.0 / np.minimum(np.arange(16) + 1, w).astype(np.float32)
    return {"c_identb": identb, "c_identf": identf, "c_maskA": maskA, "c_invc": invc}


def _run(inputs, n_cores, nseq, seq, stage=None):
    key = (nseq, seq, stage)
    if key not in _CACHE:
        _CACHE[key] = build_program(nseq, seq, stage)[0]
    nc = _CACHE[key]
    f32 = lambda a: np.ascontiguousarray(np.asarray(a, dtype=np.float32))
    x = f32(inputs["x"])
    shared = {
        "w_in": f32(inputs["w_in"]).reshape(D, INW),
        "pool_w": f32(inputs["pool_w"]).reshape(4, 256, 256),
        "pool_b": f32(inputs["pool_b"]).reshape(8, 128),
        "pool_scale": f32(inputs["pool_scale"]).reshape(8, 128),
        "lb_logits": f32(inputs["lb_logits"]).reshape(16, 128),
        "hgrn_norm_w": f32(inputs["hgrn_norm_w"]).reshape(8, 128),
        "w_out": f32(inputs["w_out"]).reshape(D, D),
        "norm_mix_pre": f32(inputs["norm_mix_pre"]).reshape(16, 128),
        "norm_mix_post": f32(inputs["norm_mix_post"]).reshape(1, D),
        "norm_mlp_pre": f32(inputs["norm_mlp_pre"]).reshape(16, 128),
        "norm_mlp_post": f32(inputs["norm_mlp_post"]).reshape(1, D),
        "w_mlp_in": f32(inputs["w_mlp_in"]).reshape(D, DFF),
        "w_mlp_out": f32(inputs["w_mlp_out"]).reshape(DFF, D),
    }
    shared.update(_consts())
    in_maps = []
    for c in range(n_cores):
        m = dict(shared)
        m["x"] = np.ascontiguousarray(x[c * nseq:(c + 1) * nseq].reshape(nseq * seq, D))
        in_maps.append(m)
    res = run_bass_kernel_spmd(nc, in_maps, core_ids=list(range(n_cores)))
    if stage == 'mixer':
        global DBG
        DBG = [np.asarray(r["dbg"]) for r in res.results]
    outs = [np.asarray(r["out"], dtype=np.float32).reshape(nseq, seq, D) for r in res.results]
    return np.concatenate(outs, axis=0)


def kernel(**inputs):
    x = inputs["x"]
    B, S_, _ = x.shape
    return _run(inputs, N_CORES, B // N_CORES, S_)
```

```python
import contextlib
import numpy as np
import ml_dtypes
import concourse.bass as bass
import concourse.mybir as mybir
from concourse.bass_utils import run_bass_kernel_spmd

F32 = mybir.dt.float32
BF16 = mybir.dt.bfloat16
U8 = mybir.dt.uint8
AF = mybir.ActivationFunctionType
ALU = mybir.AluOpType

D = 2048
DFF = 8192
INW = 5120
TT = 512
EPS = 1e-6
N_CORES = 8
ENGS = ("pe", "act", "dve", "pool", "sp")


class Prog:
    def __init__(self):
        self.streams = {e: [] for e in ENGS}
        self.cnt = {e: 0 for e in ENGS}
        self.waited = {e: {} for e in ENGS}
        self.regions = {}
        self.dmacnt = {}

    def _gather(self, deps, key, lo, hi, is_write):
        lst = self.regions.setdefault(key, [])
        for ent in lst:
            if ent[0] < hi and lo < ent[1]:
                w = ent[2]
                if w is not None:
                    if deps.get(w[0], 0) < w[1]:
                        deps[w[0]] = w[1]
                if is_write:
                    for k, v in ent[3].items():
                        if deps.get(k, 0) < v:
                            deps[k] = v

    def _record(self, key, lo, hi, tok, is_write):
        lst = self.regions.setdefault(key, [])
        if is_write:
            lst[:] = [e for e in lst if not (lo <= e[0] and e[1] <= hi)]
            lst.append([lo, hi, tok, {}])
        else:
            for ent in lst:
                if ent[0] == lo and ent[1] == hi:
                    if ent[3].get(tok[0], 0) < tok[1]:
                        ent[3][tok[0]] = tok[1]
                    return
            lst.append([lo, hi, None, {tok[0]: tok[1]}])

    def emit(self, eng, fn, reads=(), writes=(), inc=True, dma_slot=None):
        deps = {}
        for (k, lo, hi) in reads:
            self._gather(deps, k, lo, hi, k.startswith("ps"))
        for (k, lo, hi) in writes:
            self._gather(deps, k, lo, hi, True)
        if eng == "pe":
            deps.pop("pe", None)
        waits = []
        wd = self.waited[eng]
        for k, v in deps.items():
            if wd.get(k, 0) < v:
                wd[k] = v
                waits.append((k, v))
        if dma_slot is not None:
            sk = "dma:" + dma_slot
            self.dmacnt[sk] = self.dmacnt.get(sk, 0) + 16
            tok = (sk, self.dmacnt[sk])
            incinfo = (sk, 16)
        else:
            if inc:
                self.cnt[eng] += 1
                tok = (eng, self.cnt[eng])
                incinfo = (eng, 1)
            else:
                tok = (eng, self.cnt[eng] + 1)
                incinfo = None
        for (k, lo, hi) in reads:
            self._record(k, lo, hi, tok, False)
        for (k, lo, hi) in writes:
            self._record(k, lo, hi, tok, True)
        self.streams[eng].append((waits, fn, incinfo))
        return tok


def build_program(NSEQ, SEQ, STAGE=None):
    NTOK = NSEQ * SEQ
    TPS = SEQ // TT
    NT = NSEQ * TPS
    nc = bass.Bass("TRN2", target_bir_lowering=False)
    P = Prog()

    def din(name, shape, dt=F32):
        return nc.dram_tensor(name, list(shape), dt, kind="ExternalInput").ap()

    x_d = din("x", [NTOK, D])
    w_in_d = din("w_in", [D, INW])
    pool_w_d = din("pool_w", [4, 256, 256])
    pool_b_d = din("pool_b", [8, 128])
    pool_s_d = din("pool_scale", [8, 128])
    lbl_d = din("lb_logits", [16, 128])
    nw_d = din("hgrn_norm_w", [8, 128])
    w_out_d = din("w_out", [D, D])
    g_pre1_d = din("norm_mix_pre", [16, 128])
    g_post1_d = din("norm_mix_post", [1, D])
    g_pre2_d = din("norm_mlp_pre", [16, 128])
    g_post2_d = din("norm_mlp_post", [1, D])
    w_mi_d = din("w_mlp_in", [D, DFF])
    w_mo_d = din("w_mlp_out", [DFF, D])
    identb_d = din("c_identb", [128, 128], BF16)
    identf_d = din("c_identf", [128, 128])
    maskA_d = din("c_maskA", [128, 128])
    invc_d = din("c_invc", [128, 64])
    out_d = nc.dram_tensor("out", [NTOK, D], F32, kind="ExternalOutput").ap()
    dbg_d = nc.dram_tensor("dbg", [128, 8192], BF16, kind="ExternalOutput").ap() if STAGE == 'mixer' else None
    NPIECE = 10 + 4 + 16 + 16
    wscr = nc.dram_tensor("wscr", [NPIECE, 128, 16 * 512], BF16).ap()

    es = contextlib.ExitStack()
    with es:
        def sb(name, shape, dt):
            return es.enter_context(nc.sbuf_tensor(name, list(shape), dt))

        identb = sb("identb", [128, 128], BF16)
        identf = sb("identf", [128, 128], F32)
        maskA = sb("maskA", [128, 128], F32)
        ones = sb("ones", [128, 128], F32)
        smask = sb("smask", [128, 512], F32)
        invc = sb("invc", [128, 64], F32)
        prow = sb("prow", [72, 128], F32)
        par = sb("par", [128, 72], F32)
        par2 = sb("par2", [128, 40], F32)
        gpost = sb("gpost", [128, 2, D], F32)
        poolw = sb("poolw", [128, 8, 256], BF16)
        S = sb("S", [128, 8, 128], F32)
        Sb = sb("Sb", [128, 8, 128], BF16)
        halo = sb("halo", [128, 8, 16], F32)
        xbuf = sb("xbuf", [128, 4, D], F32)
        hy = sb("hy", [128, 2 * 16 * 512], BF16)
        ring = sb("ring", [128, 2, 16 * 512], BF16)
        tiny = sb("tiny", [128, 64], F32)
        dec_t = sb("dec_t", [128, 64], F32)
        ARENA = 78016 + 2112
        arena = sb("arena", [128, ARENA], U8)
        banks = [es.enter_context(nc.psum_tensor("bank%d" % b, [128, 512], F32)) for b in range(8)]

        sem_names = {}

        def getsem(key):
            if key not in sem_names:
                sem_names[key] = es.enter_context(nc.semaphore("s_" + key.replace(":", "_")))
            return sem_names[key]

        def av(off, n_elem, dt, esz):
            return arena[:, off:off + n_elem * esz].bitcast(dt)

        def AR(off, nbytes):
            return ("arena", off, off + nbytes)

        hT = hy[:, 0:8192].rearrange("p (c t) -> p c t", t=512)
        yT = hy[:, 8192:16384].rearrange("p (c t) -> p c t", t=512)
        ffv = hy[:].bitcast(F32).rearrange("p (s n) -> p s n", n=D)

        def R_hT(c0=0, c1=16):
            return ("hy", c0 * 1024, c1 * 1024)

        def R_yT(c0=0, c1=16):
            return ("hy", 16384 + c0 * 1024, 16384 + c1 * 1024)

        def R_ff(s):
            return ("hy", s * 8192, (s + 1) * 8192)

        def R_x(s):
            return ("xbuf", s * 8192, (s + 1) * 8192)

        def R_bank(b, lo=0, hi=2048):
            return ("ps%d" % b, 0, 2048)

        def R_ring(s):
            return ("ring", s * 16384, (s + 1) * 16384)

        def R_tiny(c0, c1):
            return ("tiny", c0 * 4, c1 * 4)

        HB = 5120
        def qh_v(h): return av(h * HB, 512, BF16, 2), AR(h * HB, 1024)
        def kt_v(h): return av(h * HB + 1024, 512, BF16, 2), AR(h * HB + 1024, 1024)
        def khtok_v(h): return av(h * HB + 2048, 512, BF16, 2).rearrange("p (s d) -> p s d", d=128), AR(h * HB + 2048, 1024)
        def v_v(h): return av(h * HB + 3072, 512, BF16, 2).rearrange("p (s d) -> p s d", d=128), AR(h * HB + 3072, 1024)
        def rg_v(h): return av(h * HB + 4096, 512, BF16, 2), AR(h * HB + 4096, 1024)
        def og_v(i): return av(40960 + i * 2048, 512, F32, 4), AR(40960 + i * 2048, 2048)
        def T_v(i): return av(49152 + i * 2048, 512, F32, 4), AR(49152 + i * 2048, 2048)
        khT_v = (av(61440, 512, BF16, 2), AR(61440, 1024))
        og2_v = T_v(0)
        rstd_v = T_v(1)
        def Am_v(i): return av(62464 + i * 256, 128, BF16, 2), AR(62464 + i * 256, 256)
        ubuf_v = (av(63488, 528, F32, 4), AR(63488, 2112))
        sA_v = (av(65600, 528, F32, 4), AR(65600, 2112))
        sB_v = (av(67712, 528, F32, 4), AR(67712, 2112))
        def dT_v(c): return av(69824 + c * 1024, 512, BF16, 2), AR(69824 + c * 1024, 1024)
        mix_all = av(0, 4 * D, F32, 4).rearrange("p (s n) -> p s n", n=D)
        def R_mix(s): return AR(s * 8192, 8192)
        hid = av(0, 64 * 512, BF16, 2).rearrange("p (c t) -> p c t", t=512)
        def R_hid(c0, c1): return AR(c0 * 1024, (c1 - c0) * 1024)
        junkA = av(57344, 2048, BF16, 2)
        R_junkA = AR(57344, 4096)
        junkB = av(69824, 512, BF16, 2)
        R_junkB = AR(69824, 1024)

        def ACT(out, in_, func, reads, writes, **kw):
            P.emit("act", lambda e: e.activation(out=out, in_=in_, func=func, **kw), reads, writes)

        def DVE_tt(out, in0, in1, op, reads, writes):
            P.emit("dve", lambda e: e.tensor_tensor(out=out, in0=in0, in1=in1, op=op), reads, writes)

        def DVE_ts(out, in0, s1, s2, op0, op1, reads, writes):
            if op1 is None:
                P.emit("dve", lambda e: e.tensor_scalar(out=out, in0=in0, scalar1=s1, scalar2=None, op0=op0), reads, writes)
            else:
                P.emit("dve", lambda e: e.tensor_scalar(out=out, in0=in0, scalar1=s1, scalar2=s2, op0=op0, op1=op1), reads, writes)

        def DVE_stt(out, in0, scalar, in1, op0, op1, reads, writes):
            P.emit("dve", lambda e: e.scalar_tensor_tensor(out=out, in0=in0, scalar=scalar, in1=in1, op0=op0, op1=op1), reads, writes)

        def DVE_copy(out, in_, reads, writes):
            P.emit("dve", lambda e: e.tensor_copy(out=out, in_=in_), reads, writes)

        def DVE_recip(out, in_, reads, writes):
            P.emit("dve", lambda e: e.reciprocal(out=out, in_=in_), reads, writes)

        def DVE_scan(out, d0, d1, reads, writes):
            P.emit("dve", lambda e: e.tensor_tensor_scan(out=out, data0=d0, data1=d1, initial=0.0, op0=ALU.mult, op1=ALU.add), reads, writes)

        def DVE_rsum(out, in_, reads, writes):
            P.emit("dve", lambda e: e.reduce_sum(out=out, in_=in_, axis=mybir.AxisListType.X), reads, writes)

        def ENG_memset(eng, ap, val, writes):
            P.emit(eng, lambda e: e.memset(ap, val), (), writes)

        def POOL_tt(out, in0, in1, op, reads, writes):
            P.emit("pool", lambda e: e.tensor_tensor(out=out, in0=in0, in1=in1, op=op), reads, writes)

        def POOL_copy(out, in_, reads, writes):
            P.emit("pool", lambda e: e.tensor_copy(out=out, in_=in_), reads, writes)

        def MM(out, lhsT, rhs, start, stop, reads, writes, inc):
            P.emit("pe", lambda e: e.matmul(out, lhsT=lhsT, rhs=rhs, start=start, stop=stop), reads, writes, inc=inc)

        def TR(out, in_, ident, reads, writes, inc):
            P.emit("pe", lambda e: e.transpose(out=out, in_=in_, identity=ident), reads, writes, inc=inc)

        def DMA(out, in_, reads, writes, slot):
            P.emit("sp", lambda e: e.dma_start(out=out, in_=in_), reads, writes, dma_slot=slot)

        def FULL(name, nbytes):
            return (name, 0, nbytes)

        DMA(identb[:], identb_d, (), [FULL("identb", 256)], "c0")
        DMA(identf[:], identf_d, (), [FULL("identf", 512)], "c1")
        DMA(maskA[:], maskA_d, (), [FULL("maskA", 512)], "c2")
        DMA(invc[:], invc_d, (), [FULL("invc", 256)], "c3")
        DMA(prow[0:16, :], lbl_d, (), [("prow", 0, 1)], "c4")
        DMA(prow[16:24, :], nw_d, (), [("prow", 1, 2)], "c5")
        DMA(prow[24:32, :], pool_b_d, (), [("prow", 2, 3)], "c6")
        DMA(prow[32:40, :], pool_s_d, (), [("prow", 3, 4)], "c7")
        DMA(prow[40:56, :], g_pre1_d, (), [("prow", 4, 5)], "c8")
        DMA(prow[56:72, :], g_pre2_d, (), [("prow", 5, 6)], "c9")
        DMA(gpost[:, 0, :], g_post1_d.partition_broadcast(128), (), [("gpost", 0, 8192)], "c10")
        DMA(gpost[:, 1, :], g_post2_d.partition_broadcast(128), (), [("gpost", 8192, 16384)], "c11")
        ENG_memset("pool", ones[:], 1.0, [FULL("ones", 512)])
        ENG_memset("pool", smask[:], 1.0, [FULL("smask", 2048)])
        ENG_memset("pool", smask[:].rearrange("p (c t) -> p c t", t=64)[:, :, 0:1], 0.0, [FULL("smask", 2048)])
        TR(banks[4][0:128, 0:72], prow[:, :], identf[0:72, 0:72], [("prow", 0, 6), FULL("identf", 512)], [R_bank(4)], True)
        DVE_copy(par[:], banks[4][:, 0:72], [R_bank(4)], [FULL("par", 288)])
        Rp = FULL("par", 288)
        Rp2 = FULL("par2", 160)
        DVE_tt(par2[:, 32:40], par[:, 8:16], par[:, 0:8], ALU.subtract, [Rp], [Rp2])
        ACT(par2[:, 32:40], par2[:, 32:40], AF.Exp, [Rp2], [Rp2])
        DVE_ts(par2[:, 32:40], par2[:, 32:40], 1.0, None, ALU.add, None, [Rp2], [Rp2])
        DVE_recip(par2[:, 0:8], par2[:, 32:40], [Rp2], [Rp2])
        DVE_ts(par2[:, 8:16], par2[:, 0:8], -1.0, 1.0, ALU.mult, ALU.add, [Rp2], [Rp2])
        DVE_ts(par2[:, 16:24], par2[:, 8:16], -1.0, None, ALU.mult, None, [Rp2], [Rp2])
        DVE_tt(par2[:, 24:32], par[:, 24:32], par[:, 32:40], ALU.mult, [Rp, Rp2], [Rp2])

        def stg_v(i):
            return av(i * 16384, 4096, F32, 4).rearrange("p (c n) -> p c n", n=512), AR(i * 16384, 16384)

        def cvt_v(i):
            return ring[:, i // 2, (i % 2) * 4096:(i % 2) * 4096 + 4096].rearrange("p (c n) -> p c n", n=512), ("ring", i * 8192, (i + 1) * 8192)

        pieces_src = []
        for p in range(10):
            pieces_src.append(w_in_d[:, p * 512:(p + 1) * 512])
        for p in range(4):
            pieces_src.append(w_out_d[:, p * 512:(p + 1) * 512])
        for p in range(16):
            pieces_src.append(w_mi_d[:, p * 512:(p + 1) * 512])
        for cg in range(4):
            for jp in range(4):
                pieces_src.append(w_mo_d[jp * 2048:(jp + 1) * 2048, cg * 512:(cg + 1) * 512])
        PIECE_WOUT = 10
        PIECE_MI = 14
        PIECE_MO = 30
        cv_engs = ["act", "dve", "pool"]
        N_PRO = PIECE_MI
        units = [(pi, half) for pi in range(N_PRO) for half in range(2)]

        def unit_load(u):
            pi, half = units[u]
            sl = u % 4
            sv_ap, sv_r = stg_v(sl)
            srcv = pieces_src[pi][half * 1024:(half + 1) * 1024, :].rearrange("(c p) n -> p c n", p=128)
            DMA(sv_ap, srcv, (), [sv_r], "stg%d" % sl)

        LOOKAHEAD = 3
        for u in range(min(LOOKAHEAD, len(units))):
            unit_load(u)
        for u, (pi, half) in enumerate(units):
            if u + LOOKAHEAD < len(units):
                unit_load(u + LOOKAHEAD)
            sl = u % 4
            sv_ap, sv_r = stg_v(sl)
            cv_ap, cv_r = cvt_v(sl)
            eng = cv_engs[u % 3]
            if eng == "act":
                ACT(cv_ap, sv_ap, AF.Copy, [sv_r], [cv_r])
            elif eng == "dve":
                DVE_copy(cv_ap, sv_ap, [sv_r], [cv_r])
            else:
                POOL_copy(cv_ap, sv_ap, [sv_r], [cv_r])
            dst = wscr[pi, :, half * 4096:(half + 1) * 4096].rearrange("p (c n) -> p c n", n=512)
            DMA(dst, cv_ap, [cv_r], [("wscr", pi * 2 + half, pi * 2 + half + 1)], "cvt%d" % sl)
        pw_stage = av(65536, 2048, F32, 4).rearrange("p (a n) -> p a n", n=256)
        R_pws = AR(65536, 8192)
        DMA(pw_stage, pool_w_d.rearrange("j (kc p) n -> p (j kc) n", p=128), (), [R_pws], "c12")
        DVE_copy(poolw[:], pw_stage, [R_pws], [FULL("poolw", 4096)])

        sched = []
        tile_order = [2, 4, 0, 3, 5, 1, 8, 9, 6, 7] + list(range(PIECE_WOUT, PIECE_WOUT + 4)) + \
            list(range(PIECE_MI, PIECE_MI + 16)) + list(range(PIECE_MO, PIECE_MO + 16))
        for t in range(NT):
            sched.extend(tile_order)
        state = {"loaded": 0, "cur": 0, "stg": 0}

        def ensure_loaded(upto):
            while state["loaded"] <= upto and state["loaded"] < len(sched):
                i = state["loaded"]
                pi = sched[i]
                sl = i % 2
                if pi >= N_PRO and i < len(tile_order):
                    rv = ring[:, sl, :].rearrange("p (c n) -> p c n", n=512)
                    rel = i - tile_order.index(PIECE_MI)
                    nslots = 12 if 1 <= rel <= 20 else 4
                    for c in range(16):
                        k = state["stg"] % nslots
                        state["stg"] += 1
                        if k < 4:
                            st_ap, st_r = av(70848 + k * 2048, 512, F32, 4), AR(70848 + k * 2048, 2048)
                        else:
                            o = 16384 + (k - 4) * 2048
                            st_ap, st_r = hy[:, o // 2:(o + 2048) // 2].bitcast(F32), ("hy", o, o + 2048)
                        DMA(st_ap, pieces_src[pi][c * 128:(c + 1) * 128, :], (), [st_r], "fly%d" % k)
                        DVE_copy(rv[:, c, :], st_ap, [st_r], [("ring", sl * 16384 + c * 1024, sl * 16384 + (c + 1) * 1024)])
                    DMA(wscr[pi, :, :], ring[:, sl, :], [R_ring(sl)], [("wscr", pi * 2, pi * 2 + 2)], "flyst%d" % sl)
                else:
                    DMA(ring[:, sl, :], wscr[pi, :, :], [("wscr", pi * 2, pi * 2 + 2)], [R_ring(sl)], "ring%d" % sl)
                state["loaded"] += 1

        def next_piece(expect):
            i = state["cur"]
            assert sched[i] == expect, (sched[i], expect)
            ensure_loaded(i + 1)
            state["cur"] += 1
            sl = i % 2
            return ring[:, sl, :].rearrange("p (c n) -> p c n", n=512), R_ring(sl)

        bigbank = {"i": 0}

        def next_bank():
            b = bigbank["i"] % 4
            bigbank["i"] += 1
            return b

        def rstd_from_ms(ms_ap, out_ap, scale, rd, wr):
            ACT(out_ap, ms_ap, AF.Ln, rd, wr, bias=EPS, scale=scale)
            ACT(out_ap, out_ap, AF.Exp, wr, wr, scale=-0.5)

        def prenorm_transpose(gcol0, dstT, R_dst):
            for s in range(4):
                ms = tiny[:, s:s + 1]
                rs = tiny[:, 4 + s:5 + s]
                ACT(junkA, xbuf[:, s, :], AF.Square, [R_x(s)], [R_junkA, R_tiny(s, s + 1)],
                    scale=float(D) ** -0.5, accum_out=ms)
                rstd_from_ms(ms, rs, 1.0, [R_tiny(s, s + 1)], [R_tiny(4 + s, 5 + s)])
                xn, R_xn = av(49152 + (s % 2) * 4096, 2048, BF16, 2), AR(49152 + (s % 2) * 4096, 4096)
                DVE_ts(xn, xbuf[:, s, :], rs, None, ALU.mult, None, [R_x(s), R_tiny(4 + s, 5 + s)], [R_xn])
                for half in range(2):
                    b = next_bank()
                    pb = banks[b][:].bitcast(BF16)
                    for j in range(8):
                        c = half * 8 + j
                        TR(pb[:, j * 128:(j + 1) * 128], xn[:, c * 128:(c + 1) * 128], identb[:],
                           [R_xn, FULL("identb", 256)], [R_bank(b)], inc=(j == 7))
                    gain = par[:, gcol0 + half * 8:gcol0 + half * 8 + 8].unsqueeze(2).broadcast_to([128, 8, 128])
                    DVE_tt(dstT[:, half * 8:half * 8 + 8, s * 128:(s + 1) * 128],
                           pb.rearrange("p (c t) -> p c t", t=128), gain, ALU.mult,
                           [R_bank(b), Rp], [R_dst(half * 8, half * 8 + 8)])

        dec3 = dec_t[:, :].rearrange("p (h c) -> p h c", c=8)

        def R_dec(h):
            return ("dec_t", h * 32, h * 32 + 32)

        def R_S(h):
            return ("S", h * 512, (h + 1) * 512)

        def R_Sb(h):
            return ("Sb", h * 256, (h + 1) * 256)

        R_ident = FULL("identb", 256)
        WIN = [2, 2, 4, 4, 8, 8, 16, 16]

        for ti in range(NT):
            pos = ti % TPS
            first = (pos == 0)
            tok0 = ti * TT
            if ti == 0 or STAGE is not None:
                for s in range(4):
                    DMA(xbuf[:, s, :], x_d[tok0 + s * 128:tok0 + (s + 1) * 128, :], (), [R_x(s)], "x%d" % s)
            if first:
                ENG_memset("pool", S[:], 0.0, [FULL("S", 4096)])
                ENG_memset("pool", Sb[:], 0.0, [FULL("Sb", 2048)])
                ENG_memset("pool", halo[:], 0.0, [FULL("halo", 512)])
            def _early_store():
                for s in range(4):
                    DMA(out_d[tok0 + s * 128:tok0 + (s + 1) * 128, :], xbuf[:, s, :], [R_x(s)],
                        [("out", tok0 + s * 128, tok0 + (s + 1) * 128)], "st%d" % s)
            if STAGE == 'pro':
                _early_store()
                break
            prenorm_transpose(40, hT, R_hT)
            if STAGE == 'A':
                _early_store()
                break

            from collections import deque
            q_main = deque()
            q_pe = deque()
            slabctr = {"n": 0}

            def drain(n):
                for _ in range(n):
                    if not q_main:
                        break
                    q_main.popleft()()
                while q_pe and q_pe[0][0] <= slabctr["n"]:
                    q_pe.popleft()[1]()

            def drain_all():
                while q_main or q_pe:
                    while q_main:
                        q_main.popleft()()
                    while q_pe:
                        q_pe.popleft()[1]()

            def proj_ws(piece_idx, evac, ndrain=5):
                wv, wr = next_piece(piece_idx)
                for sl in range(4):
                    b = next_bank()
                    for c in range(16):
                        MM(banks[b][:, :], wv[:, c, sl * 128:(sl + 1) * 128], hT[:, c, :], c == 0, c == 15,
                           [wr, R_hT(c, c + 1)], [R_bank(b)], inc=(c == 15))
                    evac(sl, b)
                    slabctr["n"] += 1
                    drain(ndrain)

            def ef_v(h): return av(h * HB + 1024, 512, F32, 4), AR(h * HB + 1024, 2048)
            def eg_v(h): return av(h * HB + 3072, 512, F32, 4), AR(h * HB + 3072, 2048)

            def proj_q(G):
                hs = [4 * G + i for i in range(4)]

                def evac_q(sl, b):
                    qh, Rq = qh_v(hs[sl])
                    ACT(qh, banks[b][:, :], AF.Copy, [R_bank(b)], [Rq])
                proj_ws(2 + G, evac_q)

            def proj_f(G):
                hs = [4 * G + i for i in range(4)]

                def evac_f(sl, b):
                    h = hs[sl]
                    ef, Ref = ef_v(h)
                    ACT(ef, banks[b][:, :], AF.Exp, [R_bank(b)], [Ref], scale=-1.0)
                    T1, R1 = T_v(0); T2, R2 = T_v(1); T3, R3 = T_v(2); T4, R4 = T_v(3); T5, R5 = T_v(4)
                    qh, Rq = qh_v(h)
                    kt, Rkt = kt_v(h)
                    khT, RkhT = av(40960 + (h % 4) * 2048, 512, BF16, 2), AR(40960 + (h % 4) * 2048, 1024)
                    kh, Rkh = khtok_v(h)
                    ch = [
                        lambda: ACT(T1, ef, AF.Ln, [Ref], [R1], bias=1.0),
                        lambda: ACT(T2, T1, AF.Exp, [R1], [R2], scale=-1.0),
                        lambda: ACT(T3, T2, AF.Ln, [R2, Rp2], [R3], scale=par2[:, 8 + h:9 + h], bias=par2[:, h:h + 1]),
                        lambda: DVE_ts(T4, T2, par2[:, 16 + h:17 + h], par2[:, 8 + h:9 + h], ALU.mult, ALU.add, [R2, Rp2], [R4]),
                        lambda: DVE_scan(T1, smask[:], T3, [R3, FULL("smask", 2048)], [R1]),
                        lambda: ACT(T3, T1, AF.Exp, [R1], [R3]),
                        lambda: ACT(T5, T1, AF.Exp, [R1], [R5], scale=-1.0),
                        lambda: DVE_copy(dec3[:, h, :], T3.rearrange("p (c t) -> p c t", t=64)[:, :, 63], [R3], [R_dec(h)]),
                        lambda: DVE_tt(qh, qh, T3, ALU.mult, [Rq, R3], [Rq]),
                        lambda: DVE_tt(T2, T4, T5, ALU.mult, [R4, R5], [R2]),
                        lambda: POOL_copy(kt, T2, [R2], [Rkt]),
                        lambda: DVE_tt(khT.rearrange("p (c t) -> p c t", t=64), T2.rearrange("p (c t) -> p c t", t=64),
                                       dec3[:, h, :].unsqueeze(2).broadcast_to([128, 8, 64]), ALU.mult, [R2, R_dec(h)], [RkhT]),
                    ]

                    def pe_part():
                        pb = banks[4][:].bitcast(BF16)
                        for p in range(4):
                            TR(pb[:, p * 128:(p + 1) * 128], khT[:, p * 128:(p + 1) * 128], identb[:],
                               [RkhT, R_ident], [R_bank(4)], inc=(p == 3))
                        ACT(kh, pb[:, 0:512].rearrange("p (s d) -> p s d", d=128), AF.Copy, [R_bank(4)], [Rkh])

                    def last():
                        q_pe.append((slabctr["n"] + 3, pe_part))
                    q_main.extend(ch)
                    q_main.append(last)
                proj_ws(4 + G, evac_f)

            def proj_g(G):
                hs = [4 * G + i for i in range(4)]

                def evac_g(sl, b):
                    h = hs[sl]
                    eg, Reg = eg_v(h)
                    T6, R6 = T_v(5)
                    rg, Rrg = rg_v(h)
                    ACT(eg, banks[b][:, :], AF.Exp, [R_bank(b)], [Reg], scale=-1.0)
                    q_main.appendleft(lambda: ACT(rg, T6, AF.Exp, [R6], [Rrg], scale=-1.0))
                    q_main.appendleft(lambda: ACT(T6, eg, AF.Ln, [Reg], [R6], bias=1.0))
                proj_ws(8 + G, evac_g)

            def proj_v(G):
                wv, wr = next_piece(6 + G)
                off0 = (4 * G) * HB + 3072
                for s in range(4):
                    b = next_bank()
                    for c in range(16):
                        MM(banks[b][:, :], hT[:, c, s * 128:(s + 1) * 128], wv[:, c, :], c == 0, c == 15,
                           [wr, R_hT(c, c + 1)], [R_bank(b)], inc=(c == 15))
                    off = off0 + s * 256
                    vout = arena[:, off:off + 4 * HB].bitcast(BF16).rearrange("p (h n) -> p h n", n=HB // 2)[:, :, 0:128]
                    ACT(vout, banks[b][:, :].rearrange("p (h d) -> p h d", d=128), AF.Copy, [R_bank(b)],
                        [v_v(h)[1] for h in range(4 * G, 4 * G + 4)])
                    slabctr["n"] += 1
                    drain(5)

            ub2_v = (av(ARENA - 2112, 528, F32, 4), AR(ARENA - 2112, 2112))

            def pool_proj(piece):
                def evac_u(sl, b):
                    ch = piece * 4 + sl
                    w = WIN[ch]
                    ub, Rub = ubuf_v if ch % 2 == 0 else ub2_v
                    sA, RsA = sA_v
                    sB, RsB = sB_v
                    Rh = ("halo", ch * 64, ch * 64 + 64)
                    ACT(ub[:, 16:528], banks[b][:, :], AF.Copy, [R_bank(b)], [Rub])
                    ops = [lambda: POOL_copy(ub[:, 0:16], halo[:, ch, :], [Rh], [Rub]),
                           lambda: POOL_tt(sA[:, 1:528], ub[:, 1:528], ub[:, 0:527], ALU.add, [Rub], [RsA])]
                    fin, Rfin = sA, RsA
                    if w >= 4:
                        ops.append(lambda: POOL_tt(sB[:, 3:528], sA[:, 3:528], sA[:, 1:526], ALU.add, [RsA], [RsB]))
                        fin, Rfin = sB, RsB
                    if w >= 8:
                        ops.append(lambda: POOL_tt(sA[:, 7:528], sB[:, 7:528], sB[:, 3:524], ALU.add, [RsB], [RsA]))
                        fin, Rfin = sA, RsA
                    if w >= 16:
                        ops.append(lambda: POOL_tt(sB[:, 15:528], sA[:, 15:528], sA[:, 7:520], ALU.add, [RsA], [RsB]))
                        fin, Rfin = sB, RsB
                    ops.append(lambda: POOL_copy(halo[:, ch, :], ub[:, 512:528], [Rub], [Rh]))
                    dT, RdT = dT_v(ch)
                    ops.append(lambda: DVE_stt(dT, fin[:, 16:528], 1.0 / w, ub[:, 16:528], ALU.mult, ALU.subtract, [Rfin, Rub], [RdT]))
                    if first:
                        widx = ch // 2
                        t16 = tiny[:, 32:48]
                        ops.append(lambda: POOL_tt(t16, fin[:, 16:32], invc[:, widx * 16:(widx + 1) * 16], ALU.mult,
                                                   [Rfin, FULL("invc", 256)], [R_tiny(32, 48)]))
                        ops.append(lambda: POOL_tt(dT[:, 0:16], t16, ub[:, 16:32], ALU.subtract, [R_tiny(32, 48), Rub], [RdT]))
                    for o in reversed(ops):
                        q_main.appendleft(o)
                proj_ws(piece, evac_u, ndrain=10)

            def pool_mm():
                for j in range(4):
                    for oc in range(2):
                        b = next_bank()
                        for kc in range(2):
                            dT, RdT = dT_v(2 * j + kc)
                            MM(banks[b][:, :], poolw[:, j * 2 + kc, oc * 128:(oc + 1) * 128], dT, kc == 0, kc == 1,
                               [FULL("poolw", 4096), RdT], [R_bank(b)], inc=(kc == 1))
                        ch = 2 * j + oc
                        ACT(yT[:, ch, :], banks[b][:, :], AF.Identity, [R_bank(b), Rp, Rp2], [R_yT(ch, ch + 1)],
                            scale=par[:, 32 + ch:33 + ch], bias=par2[:, 24 + ch:25 + ch])

            def Am8_v(h): return av(61440 + h * 256, 128, BF16, 2), AR(61440 + h * 256, 256)
            A_BANK = [4, 5]
            U_BANK = [6, 0, 2, 3]
            O_BANK = [7, 1]

            def hgrn_chunks_all():
                HS = list(range(8))
                for p in range(4):
                    for h in HS:
                        kt, Rkt = kt_v(h)
                        qh, Rq = qh_v(h)
                        ab = A_BANK[h // 4]
                        MM(banks[ab][:, (h % 4) * 128:(h % 4 + 1) * 128], kt[:, p * 128:(p + 1) * 128], qh[:, p * 128:(p + 1) * 128],
                           True, True, [Rkt, Rq], [R_bank(ab)], inc=(h % 4 == 3))
                    for h in HS:
                        ab = A_BANK[h // 4]
                        Am, RAm = Am8_v(h)
                        DVE_tt(Am, banks[ab][:, (h % 4) * 128:(h % 4 + 1) * 128], maskA[:], ALU.mult,
                               [R_bank(ab), FULL("maskA", 512)], [RAm])
                    for half in range(2):
                        c = 2 * p + half
                        for h in HS:
                            qh, Rq = qh_v(h)
                            vv, Rv = v_v(h)
                            kh, Rkh = khtok_v(h)
                            Am, RAm = Am8_v(h)
                            ob = O_BANK[h // 4]
                            o_lo = (h % 4) * 128 + half * 64
                            o_ap = banks[ob][:, o_lo:o_lo + 64]
                            MM(o_ap, Sb[:, h, :], qh[:, c * 64:(c + 1) * 64], True, False, [R_Sb(h), Rq], [R_bank(ob)], inc=False)
                            MM(o_ap, vv[:, p, :], Am[:, half * 64:(half + 1) * 64], False, True, [Rv, RAm], [R_bank(ob)], inc=True)
                            ub_ = U_BANK[h % 4]
                            uq = h // 4
                            MM(banks[ub_][:, uq * 128:(uq + 1) * 128], kh[64 * half:64 * half + 64, p, :],
                               vv[64 * half:64 * half + 64, p, :], True, True, [Rkh, Rv], [R_bank(ub_)], inc=True)
                        for h in HS:
                            ub_ = U_BANK[h % 4]
                            uq = h // 4
                            DVE_stt(S[:, h, :], S[:, h, :], dec3[:, h, c:c + 1], banks[ub_][:, uq * 128:(uq + 1) * 128],
                                    ALU.mult, ALU.add, [R_S(h), R_dec(h), R_bank(ub_)], [R_S(h)])
                        for h in HS:
                            ACT(Sb[:, h, :], S[:, h, :], AF.Copy, [R_S(h)], [R_Sb(h)])
                        if half == 1:
                            for h in HS:
                                ob = O_BANK[h // 4]
                                og, Rog = og_v(h)
                                rg, Rrg = rg_v(h)
                                DVE_tt(og[:, p * 128:(p + 1) * 128], banks[ob][:, (h % 4) * 128:(h % 4 + 1) * 128],
                                       rg[:, p * 128:(p + 1) * 128], ALU.mult, [R_bank(ob), Rrg], [Rog])
                tmp2 = [(T_v(4), T_v(5), 7), (ubuf_v, sA_v, 6)]
                for h in HS:
                    (og2, Rog2), (rs, Rrs), sbk = tmp2[h % 2]
                    og2 = og2[:, 0:512]
                    rs = rs[:, 0:512]
                    og, Rog = og_v(h)
                    POOL_tt(og2, og, og, ALU.mult, [Rog], [Rog2])
                    MM(banks[sbk][:, :], ones[:], og2, True, True, [FULL("ones", 512), Rog2], [R_bank(sbk)], inc=True)
                    rstd_from_ms(banks[sbk][:, :], rs, 1.0 / 128.0, [R_bank(sbk)], [Rrs])
                    DVE_stt(yT[:, 8 + h, :], og, par[:, 16 + h:17 + h], rs, ALU.mult, ALU.mult, [Rog, Rrs, Rp], [R_yT(8 + h, 9 + h)])

            proj_q(0)
            proj_f(0)
            pool_proj(0)
            proj_q(1)
            proj_f(1)
            pool_proj(1)
            proj_g(0)
            proj_g(1)
            proj_v(0)
            proj_v(1)
            drain_all()
            pool_mm()
            hgrn_chunks_all()
            if STAGE == 'mixer' and ti == NT - 1:
                DMA(dbg_d, hy[:, 8192:16384], [R_yT(0, 16)], [("dbg", 0, 1)], "dbg")

            def post_norm_residual(src_ap_fn, R_src_fn, gi, store):
                for s in range(4):
                    ms = tiny[:, 24 + s:25 + s]
                    rs = tiny[:, 28 + s:29 + s]
                    DVE_rsum(ms, tiny[:, 8 + s * 4:12 + s * 4], [R_tiny(8 + s * 4, 12 + s * 4)], [R_tiny(24 + s, 25 + s)])
                    rstd_from_ms(ms, rs, 1.0, [R_tiny(24 + s, 25 + s)], [R_tiny(28 + s, 29 + s)])
                    src = src_ap_fn(s)
                    if not store:
                        DVE_stt(xbuf[:, s, :], src, rs, xbuf[:, s, :], ALU.mult, ALU.add,
                                [R_src_fn(s), R_tiny(28 + s, 29 + s), R_x(s)], [R_x(s)])
                    else:
                        DVE_stt(src, src, rs, xbuf[:, s, :], ALU.mult, ALU.add,
                                [R_src_fn(s), R_tiny(28 + s, 29 + s), R_x(s)], [R_src_fn(s)])
                        DMA(out_d[tok0 + s * 128:tok0 + (s + 1) * 128, :], src, [R_src_fn(s)],
                            [("out", tok0 + s * 128, tok0 + (s + 1) * 128)], "st%d" % s)
                        if ti + 1 < NT and STAGE is None:
                            nt0 = tok0 + TT
                            DMA(xbuf[:, s, :], x_d[nt0 + s * 128:nt0 + (s + 1) * 128, :], (), [R_x(s)], "x%d" % s)

            def evac_tokmajor(b, dst_ap, R_dst, s, cg, junk, R_junk, gi):
                DVE_tt(dst_ap, banks[b][:, :], gpost[:, gi, cg * 512:(cg + 1) * 512], ALU.mult,
                       [R_bank(b), ("gpost", gi * 8192, (gi + 1) * 8192)], [R_dst])
                ACT(junk[:, 0:512], banks[b][:, :], AF.Square, [R_bank(b)], [R_junk, R_tiny(8 + s * 4 + cg, 9 + s * 4 + cg)],
                    scale=float(D) ** -0.5, accum_out=tiny[:, 8 + s * 4 + cg:9 + s * 4 + cg])

            for cg in range(4):
                wv, wr = next_piece(PIECE_WOUT + cg)
                for s in range(4):
                    b = next_bank()
                    for c in range(16):
                        MM(banks[b][:, :], yT[:, c, s * 128:(s + 1) * 128], wv[:, c, :], c == 0, c == 15,
                           [wr, R_yT(c, c + 1)], [R_bank(b)], inc=(c == 15))
                    evac_tokmajor(b, mix_all[:, s, cg * 512:(cg + 1) * 512], R_mix(s), s, cg, junkA, R_junkA, 0)
            post_norm_residual(lambda s: mix_all[:, s, :], R_mix, 0, STAGE == 'mixer')
            if STAGE == 'mixer':
                for _p in range(32):
                    state['cur'] += 1
                continue

            prenorm_transpose(56, hT, R_hT)
            for p in range(16):
                wv, wr = next_piece(PIECE_MI + p)
                for sl in range(4):
                    b = next_bank()
                    for c in range(16):
                        MM(banks[b][:, :], wv[:, c, sl * 128:(sl + 1) * 128], hT[:, c, :], c == 0, c == 15,
                           [wr, R_hT(c, c + 1)], [R_bank(b)], inc=(c == 15))
                    k = (p * 4 + sl) % 2
                    rt, Rrt = av(65600 + k * 2112, 512, F32, 4), AR(65600 + k * 2112, 2048)
                    ACT(rt, banks[b][:, :], AF.Relu, [R_bank(b)], [Rrt])
                    hc = p * 4 + sl
                    POOL_tt(hid[:, hc, :], rt, rt, ALU.mult, [Rrt], [R_hid(hc, hc + 1)])
            for cg in range(4):
                for jp in range(4):
                    wv, wr = next_piece(PIECE_MO + cg * 4 + jp)
                    for s in range(4):
                        for c in range(16):
                            hc = jp * 16 + c
                            MM(banks[s][:, :], hid[:, hc, s * 128:(s + 1) * 128], wv[:, c, :],
                               (jp == 0 and c == 0), (jp == 3 and c == 15),
                               [wr, R_hid(hc, hc + 1)], [R_bank(s)], inc=(c == 15))
                for s in range(4):
                    evac_tokmajor(s, ffv[:, s, cg * 512:(cg + 1) * 512], R_ff(s), s, cg, junkB, R_junkB, 1)
            post_norm_residual(lambda s: ffv[:, s, :], R_ff, 1, True)

        keys = set()
        for e in ENGS:
            for waits, fn, incinfo in P.streams[e]:
                for (k, v) in waits:
                    keys.add(k)
                if incinfo:
                    keys.add(incinfo[0])
        for k in sorted(keys):
            getsem(k)
        final_waits = [(k, v) for k, v in P.dmacnt.items() if k.startswith("dma:st") or k.startswith("dma:dbg") or k.startswith("dma:flyst")]

        with nc.allow_low_precision(reason="bf16 matmul operands, fp32 accumulation"), nc.Block() as block:
            def runner(name, extra=()):
                stream = P.streams[name]

                def f(e):
                    for waits, fn, incinfo in stream:
                        for (k, v) in waits:
                            e.wait_ge(getsem(k), v)
                        ins = fn(e)
                        if incinfo:
                            ins.then_inc(getsem(incinfo[0]), incinfo[1])
                    for (k, v) in extra:
                        e.wait_ge(getsem(k), v)
                return f
            block.sync(runner("sp", final_waits))
            block.tensor(runner("pe"))
            block.scalar(runner("act"))
            block.vector(runner("dve"))
            block.gpsimd(runner("pool"))
    return nc, P


_CACHE = {}


def _consts():
    identb = np.eye(128, dtype=np.float32).astype(ml_dtypes.bfloat16)
    identf = np.eye(128, dtype=np.float32)
    s = np.arange(128)[:, None]
    t = np.arange(128)[None, :]
    maskA = ((s // 64 == t // 64) & (s <= t)).astype(np.float32)
    invc = np.zeros((128, 64), np.float32)
    for wi, w in enumerate((2, 4, 8, 16)):
        invc[:, wi * 16:(wi + 1) * 16] = 1.0 / np.minimum(np.arange(16) + 1, w).astype(np.float32)
    return {"c_identb": identb, "c_identf": identf, "c_maskA": maskA, "c_invc": invc}


def _run(inputs, n_cores, nseq, seq, stage=None):
    key = (nseq, seq, stage)
    if key not in _CACHE:
        _CACHE[key] = build_program(nseq, seq, stage)[0]
    nc = _CACHE[key]
    f32 = lambda a: np.ascontiguousarray(np.asarray(a, dtype=np.float32))
    x = f32(inputs["x"])
    shared = {
        "w_in": f32(inputs["w_in"]).reshape(D, INW),
        "pool_w": f32(inputs["pool_w"]).reshape(4, 256, 256),
        "pool_b": f32(inputs["pool_b"]).reshape(8, 128),
        "pool_scale": f32(inputs["pool_scale"]).reshape(8, 128),
        "lb_logits": f32(inputs["lb_logits"]).reshape(16, 128),
        "hgrn_norm_w": f32(inputs["hgrn_norm_w"]).reshape(8, 128),
        "w_out": f32(inputs["w_out"]).reshape(D, D),
        "norm_mix_pre": f32(inputs["norm_mix_pre"]).reshape(16, 128),
        "norm_mix_post": f32(inputs["norm_mix_post"]).reshape(1, D),
        "norm_mlp_pre": f32(inputs["norm_mlp_pre"]).reshape(16, 128),
        "norm_mlp_post": f32(inputs["norm_mlp_post"]).reshape(1, D),
        "w_mlp_in": f32(inputs["w_mlp_in"]).reshape(D, DFF),
        "w_mlp_out": f32(inputs["w_mlp_out"]).reshape(DFF, D),
    }
    shared.update(_consts())
    in_maps = []
    for c in range(n_cores):
        m = dict(shared)
        m["x"] = np.ascontiguousarray(x[c * nseq:(c + 1) * nseq].reshape(nseq * seq, D))
        in_maps.append(m)
    res = run_bass_kernel_spmd(nc, in_maps, core_ids=list(range(n_cores)))
    if stage == 'mixer':
        global DBG
        DBG = [np.asarray(r["dbg"]) for r in res.results]
    outs = [np.asarray(r["out"], dtype=np.float32).reshape(nseq, seq, D) for r in res.results]
    return np.concatenate(outs, axis=0)


def kernel(**inputs):
    x = inputs["x"]
    B, S_, _ = x.shape
    return _run(inputs, N_CORES, B // N_CORES, S_)
```
